# Optimizing a Trainium2 kernel written in Bass

```python
import jax, jax.numpy as jnp
from jax import lax
import numpy as np

D_MODEL = 2048
BATCH = 8
SEQ = 2048
DEPTH = 1

HEAD_DIM = 128
ATTN_HEADS = 12
DILATION_GROUPS = ((128, 1), (512, 4), (2048, 16))
N_ATTN_GROUPS = len(DILATION_GROUPS)
ATTN_WIDTH = ATTN_HEADS * HEAD_DIM
ROPE_THETA = 10000.0
SSD_WIDTH = 2 * D_MODEL
SSD_HEADDIM = 64
SSD_HEADS = SSD_WIDTH // SSD_HEADDIM
SSD_GROUPS = 8
SSD_HEADS_PER_GROUP = SSD_HEADS // SSD_GROUPS
SSD_STATE = 128
SSD_CHUNK = 128
CONV_WIDTH = 5
CONV_CH = SSD_WIDTH + 2 * SSD_GROUPS * SSD_STATE
EPS = 1e-6
QKV_COLS = 3 * N_ATTN_GROUPS * ATTN_WIDTH
SPLIT_SIZES = (QKV_COLS, ATTN_WIDTH, SSD_WIDTH, CONV_CH, 2 * SSD_HEADS, D_MODEL, D_MODEL)
IN_COLS = sum(SPLIT_SIZES)

kernel_name = "hybrid_dilated_attn_bissd_block"


def rms_norm(t, g):
    tf = t.astype(jnp.float32)
    tf = tf * lax.rsqrt(jnp.mean(tf * tf, axis=-1, keepdims=True) + EPS)
    return (tf * g.astype(jnp.float32)).astype(t.dtype)


def rotary(t, pos):
    dh = t.shape[-1]
    inv = ROPE_THETA ** (-jnp.arange(0, dh, 2, dtype=jnp.float32) / dh)
    ang = pos.astype(jnp.float32)[..., None] * inv
    cos = jnp.cos(ang)[:, :, None, :]
    sin = jnp.sin(ang)[:, :, None, :]
    tf = t.astype(jnp.float32)
    t1, t2 = tf[..., : dh // 2], tf[..., dh // 2:]
    return jnp.concatenate([t1 * cos - t2 * sin, t2 * cos + t1 * sin], axis=-1).astype(t.dtype)


def dilated_window_attention(q, k, v, window, dilation):
    b, s, h, dh = q.shape
    n_side = window // (2 * dilation)
    blk = n_side
    L = s // dilation
    Lp = -(-L // blk) * blk
    nb = Lp // blk

    def to_cls(t):
        return t.reshape(b, L, dilation, h, dh)

    qc = jnp.pad(to_cls(q), ((0, 0), (0, Lp - L), (0, 0), (0, 0), (0, 0)))
    qc = qc.reshape(b, nb, blk, dilation, h, dh)

    def key_blocks(t):
        tp = jnp.pad(to_cls(t), ((0, 0), (blk, Lp - L + blk), (0, 0), (0, 0), (0, 0)))
        tp = tp.reshape(b, nb + 2, blk, dilation, h, dh)
        return jnp.concatenate([tp[:, :-2], tp[:, 1:-1], tp[:, 2:]], axis=2)

    kb, vb = key_blocks(k), key_blocks(v)
    scores = jnp.einsum('bnqrhe,bnkrhe->bnrhqk', qc, kb).astype(jnp.float32) * (dh ** -0.5)
    qi = jnp.arange(nb)[:, None] * blk + jnp.arange(blk)[None, :]
    ki = jnp.arange(nb)[:, None] * blk + jnp.arange(3 * blk)[None, :] - blk
    kk = ki[:, None, :]
    valid = (jnp.abs(qi[:, :, None] - kk) <= n_side) & (kk >= 0) & (kk < L)
    scores = jnp.where(valid[None, :, None, None], scores, -jnp.inf)
    m = jnp.max(scores, axis=-1, keepdims=True)
    p = jnp.exp(scores - m)
    den = jnp.sum(p, axis=-1)
    o = jnp.einsum('bnrhqk,bnkrhe->bnqrhe', p, vb.astype(jnp.float32))
    o = o / jnp.transpose(den, (0, 1, 4, 2, 3))[..., None]
    lse = jnp.transpose(m[..., 0] + jnp.log(den), (0, 1, 4, 2, 3))
    o = o.reshape(b, Lp, dilation, h, dh)[:, :L].reshape(b, s, h, dh)
    lse = lse.reshape(b, Lp, dilation, h)[:, :L].reshape(b, s, h)
    return o, lse


def ssd_chunked(xh, dt, A, Bm, Cm):
    b, s, G, E, P = xh.shape
    N = Bm.shape[-1]
    Q = SSD_CHUNK
    nc = s // Q
    xc = (xh * dt[..., None]).reshape(b, nc, Q, G, E, P)
    Bc = Bm.reshape(b, nc, Q, G, N)
    Cc = Cm.reshape(b, nc, Q, G, N)
    a = (dt * A).reshape(b, nc, Q, G, E).transpose(0, 1, 3, 4, 2)
    acum = jnp.cumsum(a, axis=-1)
    lower = jnp.tril(jnp.ones((Q, Q), dtype=bool))
    Lmat = jnp.exp(jnp.where(lower, acum[..., :, None] - acum[..., None, :], -jnp.inf))
    cb = jnp.einsum('bclgn,bcsgn->bcgls', Cc, Bc)
    y_diag = jnp.einsum('bcgls,bcgels,bcsgep->bclgep', cb, Lmat, xc)
    decay = jnp.exp(acum[..., -1:] - acum)
    states = jnp.einsum('bclgn,bcgel,bclgep->bcgepn', Bc, decay, xc)
    chunk_decay = jnp.exp(acum[..., -1])

    def step(carry, inp):
        st, dec = inp
        return carry * dec[..., None, None] + st, carry

    init = jnp.zeros((b, G, E, P, N), jnp.float32)
    _, prev = lax.scan(step, init, (jnp.moveaxis(states, 1, 0), jnp.moveaxis(chunk_decay, 1, 0)))
    prev = jnp.moveaxis(prev, 0, 1)
    y_off = jnp.einsum('bclgn,bcgepn,bcgel->bclgep', Cc, prev, jnp.exp(acum))
    return (y_diag + y_off).reshape(b, s, G, E, P)


def centred_depthwise_conv(t, w, bias):
    ch = t.shape[-1]
    pad = (CONV_WIDTH - 1) // 2
    out = lax.conv_general_dilated(t, w.astype(t.dtype)[:, None, :], window_strides=(1,),
                                   padding=[(pad, pad)], dimension_numbers=('NWC', 'WIO', 'NWC'),
                                   feature_group_count=ch)
    return out + bias.astype(t.dtype)


def setup_inputs(seed: int = 0) -> dict:
    key = jax.random.key(seed)
    ks = jax.random.split(key, 20)
    f32 = jnp.float32
    D = D_MODEL
    x = jax.random.normal(ks[0], (BATCH, SEQ, D), f32)
    c = jax.random.normal(ks[1], (BATCH, D), f32)
    positions = (jnp.arange(SEQ, dtype=jnp.int32)[None, :]
                 + jax.random.randint(ks[2], (BATCH, 1), 0, 4096, dtype=jnp.int32))
    norm_g = 1.0 + 0.02 * jax.random.normal(ks[3], (DEPTH, D), f32)
    w_ada = 0.5 * D ** -0.5 * jax.random.normal(ks[4], (DEPTH, D, 3 * D), f32)
    b_ada = 0.02 * jax.random.normal(ks[5], (DEPTH, 3 * D), f32)
    w_in = D ** -0.5 * jax.random.normal(ks[6], (DEPTH, D, IN_COLS), f32)
    conv_w = CONV_WIDTH ** -0.5 * jax.random.normal(ks[7], (DEPTH, CONV_WIDTH, CONV_CH), f32)
    conv_b = 0.02 * jax.random.normal(ks[8], (DEPTH, CONV_CH), f32)
    dt0 = jnp.exp(jax.random.uniform(ks[9], (DEPTH, 2, SSD_HEADS), f32, np.log(1e-3), np.log(1e-1)))
    dt_bias = dt0 + jnp.log(-jnp.expm1(-dt0))
    a_log = jnp.log(jax.random.uniform(ks[10], (DEPTH, 2, SSD_HEADS), f32, 1.0, 16.0))
    d_skip = 1.0 + 0.1 * jax.random.normal(ks[11], (DEPTH, SSD_HEADS), f32)
    ssd_norm_g = 1.0 + 0.02 * jax.random.normal(ks[12], (DEPTH, SSD_WIDTH), f32)
    w_br_attn = ATTN_WIDTH ** -0.5 * jax.random.normal(ks[13], (DEPTH, ATTN_WIDTH, D), f32)
    w_br_ssd = SSD_WIDTH ** -0.5 * jax.random.normal(ks[14], (DEPTH, SSD_WIDTH, D), f32)
    w_out = D ** -0.5 * jax.random.normal(ks[15], (DEPTH, D, D), f32)
    final_g = 1.0 + 0.02 * jax.random.normal(ks[16], (D,), f32)
    return {"x": x, "c": c, "positions": positions, "norm_g": norm_g, "w_ada": w_ada, "b_ada": b_ada,
            "w_in": w_in, "conv_w": conv_w, "conv_b": conv_b, "dt_bias": dt_bias, "a_log": a_log,
            "d_skip": d_skip, "ssd_norm_g": ssd_norm_g, "w_br_attn": w_br_attn, "w_br_ssd": w_br_ssd,
            "w_out": w_out, "final_g": final_g}


def reference(x, c, positions, norm_g, w_ada, b_ada, w_in, conv_w, conv_b, dt_bias, a_log, d_skip,
              ssd_norm_g, w_br_attn, w_br_ssd, w_out, final_g):
    b, s, D = x.shape
    dtype = x.dtype
    G, E, P, N = SSD_GROUPS, SSD_HEADS_PER_GROUP, SSD_HEADDIM, SSD_STATE
    split_at = np.cumsum(SPLIT_SIZES)[:-1].tolist()
    flip = lambda t: jnp.flip(t, axis=1)
    for i in range(DEPTH):
        ada = c @ w_ada[i] + b_ada[i]
        shift, scale, gate = jnp.split(ada, 3, axis=-1)
        h = rms_norm(x, norm_g[i]) * (1.0 + scale[:, None, :]) + shift[:, None, :]
        proj = h @ w_in[i]
        qkv, z_a, z_s, xbc, dt_raw, g_a, g_s = jnp.split(proj, split_at, axis=-1)

        qkv = qkv.reshape(b, s, 3, N_ATTN_GROUPS * ATTN_HEADS, HEAD_DIM)
        q = rotary(qkv[:, :, 0], positions).reshape(b, s, N_ATTN_GROUPS, ATTN_HEADS, HEAD_DIM)
        k = rotary(qkv[:, :, 1], positions).reshape(b, s, N_ATTN_GROUPS, ATTN_HEADS, HEAD_DIM)
        v = qkv[:, :, 2].reshape(b, s, N_ATTN_GROUPS, ATTN_HEADS, HEAD_DIM)
        outs, lses = [], []
        for gi, (win, dil) in enumerate(DILATION_GROUPS):
            o_g, l_g = dilated_window_attention(q[:, :, gi], k[:, :, gi], v[:, :, gi], win, dil)
            outs.append(o_g)
            lses.append(l_g)
        wts = jax.nn.softmax(jnp.stack(lses), axis=0)
        o = jnp.sum(wts[..., None] * jnp.stack(outs), axis=0).reshape(b, s, ATTN_WIDTH).astype(dtype)
        y_a = (o * jax.nn.silu(z_a)) @ w_br_attn[i]

        xbc = jax.nn.silu(centred_depthwise_conv(xbc, conv_w[i], conv_b[i]))
        xs, Bm, Cm = jnp.split(xbc.astype(jnp.float32), [SSD_WIDTH, SSD_WIDTH + G * N], axis=-1)
        xs = xs.reshape(b, s, G, E, P)
        Bm = Bm.reshape(b, s, G, N)
        Cm = Cm.reshape(b, s, G, N)
        dt = jax.nn.softplus(dt_raw.astype(jnp.float32).reshape(b, s, 2, SSD_HEADS) + dt_bias[i])
        dt = dt.reshape(b, s, 2, G, E)
        A = -jnp.exp(a_log[i].astype(jnp.float32)).reshape(2, G, E)
        y_f = ssd_chunked(xs, dt[:, :, 0], A[0], Bm, Cm)
        y_b = flip(ssd_chunked(flip(xs), flip(dt[:, :, 1]), A[1], flip(Bm), flip(Cm)))
        y = y_f + y_b + d_skip[i].astype(jnp.float32).reshape(G, E)[..., None] * xs
        y = y.reshape(b, s, SSD_WIDTH) * jax.nn.silu(z_s.astype(jnp.float32))
        y_s = rms_norm(y, ssd_norm_g[i]).astype(dtype) @ w_br_ssd[i]

        merged = jax.nn.sigmoid(g_a) * y_a + jax.nn.sigmoid(g_s) * y_s
        x = x + gate[:, None, :] * (merged @ w_out[i])
    return rms_norm(x, final_g)
```

```python
import math
from contextlib import ExitStack
import numpy as np
import ml_dtypes
import concourse.bass as bass
import concourse.mybir as mybir
from concourse.bass_utils import run_bass_kernel_spmd

F32 = mybir.dt.float32
BF16 = mybir.dt.bfloat16
I32 = mybir.dt.int32
AF = mybir.ActivationFunctionType
ALU = mybir.AluOpType

S_LEN = 2048
D = 2048
NT = 16
KC = 16
IN_COLS = 29824
EPS = 1e-6
DILS = (1, 4, 16)


class _Op:
    __slots__ = ("eng", "fn", "idx", "deps", "is_dma", "milestone", "sem", "semval", "prev_slot_val")


class Sched:
    COMPUTE = ("pe", "act", "dve", "pool")
    NSLOT = 24

    def __init__(self, nc):
        self.nc = nc
        self.ops = []
        self.last_writer = {}
        self.readers = {}
        self.barrier_deps = set()

    def add(self, eng, fn, reads=(), writes=(), dma=False):
        op = _Op()
        op.eng = eng
        op.fn = fn
        op.is_dma = dma
        op.idx = len(self.ops)
        op.milestone = False
        deps = set(self.barrier_deps)
        for r in reads:
            w = self.last_writer.get(r)
            if w is not None:
                deps.add(w)
        for w_ in writes:
            w = self.last_writer.get(w_)
            if w is not None:
                deps.add(w)
            rd = self.readers.get(w_)
            if rd:
                deps.update(rd.values())
        key = ("dma", op.idx) if dma else eng
        for r in reads:
            self.readers.setdefault(r, {})[key] = op.idx
        for w_ in writes:
            self.last_writer[w_] = op.idx
            self.readers[w_] = {}
        deps.discard(op.idx)
        op.deps = deps
        self.ops.append(op)
        return op

    def barrier(self):
        last = {}
        for op in self.ops:
            if op.is_dma:
                last[("dma", op.idx)] = op.idx
            else:
                last[op.eng] = op.idx
        dmas = [k for k in last if isinstance(k, tuple)]
        keep = set(v for k, v in last.items() if not isinstance(k, tuple))
        keep.update(last[k] for k in dmas[-(3 * self.NSLOT):])
        self.barrier_deps = keep

    def emit(self):
        nc = self.nc
        ops = self.ops
        for op in ops:
            for d in op.deps:
                x = ops[d]
                if x.is_dma:
                    continue
                if x.eng == op.eng and not op.is_dma and op.eng == "pe":
                    continue
                x.milestone = True
        sems = {e: nc.alloc_semaphore("sem_" + e) for e in self.COMPUTE}
        dsems = {e: [nc.alloc_semaphore("dsem_%s_%d" % (e, i)) for i in range(self.NSLOT)]
                 for e in ("sp", "pool", "act")}
        cnt = {e: 0 for e in self.COMPUTE}
        dcnt = {e: [0] * self.NSLOT for e in dsems}
        drr = {e: 0 for e in dsems}
        for op in ops:
            if op.is_dma:
                k = drr[op.eng] % self.NSLOT
                drr[op.eng] += 1
                op.sem = dsems[op.eng][k]
                op.prev_slot_val = dcnt[op.eng][k]
                dcnt[op.eng][k] += 16
                op.semval = dcnt[op.eng][k]
            elif op.milestone:
                cnt[op.eng] += 1
                op.sem = sems[op.eng]
                op.semval = cnt[op.eng]
        final_dma = {e: [(dsems[e][k], dcnt[e][k]) for k in range(self.NSLOT) if dcnt[e][k] > 0]
                     for e in dsems}

        def run_engine(ename, eng):
            waited = {}
            for op in ops:
                if op.eng != ename:
                    continue
                need = []
                for d in op.deps:
                    x = ops[d]
                    if (not x.is_dma) and x.eng == op.eng and not op.is_dma and op.eng == "pe":
                        continue
                    need.append((x.sem, x.semval))
                if op.is_dma and op.prev_slot_val > 0:
                    need.append((op.sem, op.prev_slot_val))
                for s, v in need:
                    key = id(s)
                    if waited.get(key, 0) < v:
                        eng.wait_ge(s, v)
                        waited[key] = v
                ins = op.fn(eng)
                if op.is_dma:
                    ins.then_inc(op.sem, 16)
                elif op.milestone:
                    ins.then_inc(op.sem, 1)
            for s, v in final_dma.get(ename, ()):
                if waited.get(id(s), 0) < v:
                    eng.wait_ge(s, v)
                    waited[id(s)] = v

        with nc.Block() as block:
            @block.tensor
            def _(e):
                run_engine("pe", e)

            @block.scalar
            def _(e):
                run_engine("act", e)

            @block.vector
            def _(e):
                run_engine("dve", e)

            @block.gpsimd
            def _(e):
                run_engine("pool", e)

            @block.sync
            def _(e):
                run_engine("sp", e)


def _col_tiles():
    order = []
    for hh in range(12):
        for t in range(3):
            for gi in range(3):
                order.append(t * 36 + gi * 12 + hh)
        order.append(108 + hh)
    order.append(200)
    for g in range(8):
        order.append(184 + g)
        order.append(192 + g)
        for j in range(4):
            order.append(152 + 4 * g + j)
        for j in range(4):
            order.append(120 + 4 * g + j)
    for j in range(16):
        order.append(201 + j)
    for j in range(16):
        order.append(217 + j)
    assert len(order) == 233 and sorted(order) == list(range(233))
    return order


COL_ORDER = _col_tiles()
OFF_ATT = 0
OFF_DT = 120
OFF_SSD = 121
OFF_GA = 201
OFF_GS = 217


def build(stop_after="all", dbg=()):
    nc = bass.Bass("TRN2", target_bir_lowering=False)
    S = Sched(nc)
    es = ExitStack()

    def dram_in(name, shape, dt=F32):
        return nc.dram_tensor(name, list(shape), dt, kind="ExternalInput").ap()

    def dram_out(name, shape, dt=F32):
        return nc.dram_tensor(name, list(shape), dt, kind="ExternalOutput").ap()

    def dram_tmp(name, shape, dt):
        return nc.dram_tensor(name, list(shape), dt, kind="Internal").ap()

    def sb(name, shape, dt, stack=None):
        return (stack or es).enter_context(nc.sbuf_tensor(name, list(shape), dt))

    x_d = dram_in("x", [S_LEN, D])
    cT_d = dram_in("cT", [128, KC])
    pos_d = dram_in("pos", [1, S_LEN], I32)
    wada_d = dram_in("w_ada", [D, 3 * D])
    bada_d = dram_in("b_adaT", [128, 48])
    normg_d = dram_in("norm_gT", [128, KC])
    win_d = dram_in("w_in", [D, IN_COLS])
    convw_d = dram_in("conv_wT", [128, 48, 5])
    convb_d = dram_in("conv_bT", [128, 48])
    dtb_d = dram_in("dt_biasT", [128, 1])
    alog_d = dram_in("a_logT", [128, 1])
    dskip_d = dram_in("d_skip", [1, 64])
    ssdg_d = dram_in("ssd_norm_gT", [128, 32])
    wbra_d = dram_in("w_br_attn", [1536, D])
    wbrs_d = dram_in("w_br_ssd", [4096, D])
    wout_d = dram_in("w_out", [D, D])
    fing_d = dram_in("final_g", [1, D])
    invf_d = dram_in("inv_freq", [128, 1])
    out_d = dram_out("out", [S_LEN, D])

    ua_scr = dram_tmp("ua_scr", [12, 128, S_LEN], BF16)

    dbg_out = {}

    ps = [nc.alloc_psum_tensor("ps%d" % i, [128, 512], F32) for i in range(8)]
    PS = ["ps%d" % i for i in range(8)]

    NWB = 3
    wbuf = [sb("wb%d" % i, [128, KC, 512], BF16) for i in range(NWB)]
    ident_f = sb("ident_f", [128, 128], F32)
    ident_b = sb("ident_b", [128, 128], BF16)
    ones_b = sb("ones_b", [128, 128], BF16)
    ones_f = sb("ones_f", [128, 128], F32)
    negm = sb("negm", [128, 256], BF16)
    adaT = sb("adaT", [128, 48], F32)
    gmod = sb("gmod", [128, KC], F32)
    small = {}
    for nm, shp in (("cT", [128, KC]), ("b_adaT", [128, 48]), ("norm_gT", [128, KC]), ("inv_freq", [128, 1]),
                    ("conv_wT", [128, 48, 5]), ("conv_bT", [128, 48]), ("dt_biasT", [128, 1]), ("a_logT", [128, 1]),
                    ("ssd_norm_gT", [128, 32])):
        small[nm] = sb("s_" + nm, shp, F32)
    dt_tok = sb("dt_tok", [128, NT, 128], F32)
    a_tok = sb("a_tok", [128, NT, 128], F32)
    dskip_row = sb("dskip_row", [128, 64], F32)
    ssq = sb("ssq_ssd", [128, NT], F32)
    hst = ExitStack()
    hT = sb("hT", [128, KC, S_LEN], BF16, hst)

    dma_rr = [0]

    def dma(eng, out, in_, reads, writes):
        S.add(eng, lambda e: e.dma_start(out=out, in_=in_), reads=reads, writes=writes, dma=True)

    def mk_consts():
        S.add("pool", lambda e: e.memset(ident_f[:], 1.0), writes=["ident_f"])
        S.add("pool", lambda e: e.affine_select(out=ident_f[:], in_=ident_f[:], pattern=[[-1, 128]],
                                               compare_op=ALU.is_equal, fill=0.0, base=0, channel_multiplier=1),
              reads=["ident_f"], writes=["ident_f"])
        S.add("dve", lambda e: e.tensor_copy(out=ident_b[:], in_=ident_f[:]), reads=["ident_f"], writes=["ident_b"])
        S.add("dve", lambda e: e.memset(ones_b[:], 1.0), writes=["ones_b"])
        S.add("dve", lambda e: e.memset(ones_f[:], 1.0), writes=["ones_f"])
        S.add("pool", lambda e: e.memset(negm[:], 0.0), writes=["negm"])
        S.add("pool", lambda e: e.affine_select(out=negm[:], in_=negm[:], pattern=[[1, 256]],
                                               compare_op=ALU.is_ge, fill=-30000.0, base=0, channel_multiplier=-1),
              reads=["negm"], writes=["negm"])
        S.add("pool", lambda e: e.affine_select(out=negm[:], in_=negm[:], pattern=[[-1, 256]],
                                               compare_op=ALU.is_ge, fill=-30000.0, base=128, channel_multiplier=1),
              reads=["negm"], writes=["negm"])

    mk_consts()

    dma("sp", small["cT"][:], cT_d, [], ["cT"])
    dma("sp", small["b_adaT"][:], bada_d, [], ["b_adaT"])
    dma("sp", small["norm_gT"][:], normg_d, [], ["norm_gT"])
    dma("sp", small["inv_freq"][:], invf_d, [], ["inv_freq"])

    with ExitStack() as st0:
        wa = [sb("wa%d" % i, [128, KC, 256], F32, st0) for i in range(2)]
        for blk in range(24):
            t = wa[blk % 2]
            rn = "wa%d" % (blk % 2)
            src = wada_d[:, blk * 256:(blk + 1) * 256].rearrange("(kc p) c -> p kc c", p=128)
            dma("sp", t[:], src, [], [rn])

            def mm(e, t=t, blk=blk):
                ins = None
                for jj in range(2):
                    j = blk * 2 + jj
                    for kc in range(KC):
                        ins = e.matmul(ps[7][:, j:j + 1], lhsT=t[:, kc, jj * 128:(jj + 1) * 128],
                                       rhs=small["cT"][:, kc:kc + 1], start=(kc == 0), stop=(kc == KC - 1),
                                       skip_group_check=True)
                return ins
            S.add("pe", mm, reads=[rn, "cT"], writes=[PS[7]])
        S.add("dve", lambda e: e.tensor_tensor(out=adaT[:], in0=ps[7][:, 0:48], in1=small["b_adaT"][:], op=ALU.add),
              reads=[PS[7], "b_adaT"], writes=["adaT"])
        S.add("dve", lambda e: e.scalar_tensor_tensor(out=gmod[:], in0=adaT[:, 16:32], scalar=1.0, in1=small["norm_gT"][:],
                                                     op0=ALU.add, op1=ALU.mult),
              reads=["adaT", "norm_gT"], writes=["gmod"])
        S.barrier()

    if "ada" in dbg:
        dbg_out["ada"] = dram_out("dbg_ada", [128, 48])
        dma("sp", dbg_out["ada"], adaT[:], ["adaT"], ["dbg_ada"])

    with ExitStack() as st1:
        xt = [sb("xt%d" % i, [128, D], F32, st1) for i in range(2)]
        sq = sb("sq_junk", [128, D], F32, st1)
        ssq0 = sb("ssq", [128, NT], F32, st1)
        rstd = sb("rstd", [128, NT], F32, st1)
        for i in range(NT):
            t = xt[i % 2]
            rn = "xt%d" % (i % 2)
            dma("sp", t[:], x_d[i * 128:(i + 1) * 128, :], [], [rn])
            S.add("act", lambda e, t=t, i=i: e.activation(out=sq[:], in_=t[:], func=AF.Square, accum_out=ssq0[:, i:i + 1]),
                  reads=[rn], writes=["sq", "ssq%d" % i])
            S.add("dve", lambda e, i=i: e.tensor_scalar(out=rstd[:, i:i + 1], in0=ssq0[:, i:i + 1], scalar1=1.0 / D, scalar2=EPS,
                                                       op0=ALU.mult, op1=ALU.add),
                  reads=["ssq%d" % i], writes=["rstd%d" % i])
            S.add("act", lambda e, i=i: e.sqrt(out=rstd[:, i:i + 1], in_=rstd[:, i:i + 1]),
                  reads=["rstd%d" % i], writes=["rstd%d" % i])
            S.add("dve", lambda e, i=i: e.reciprocal(out=rstd[:, i:i + 1], in_=rstd[:, i:i + 1]),
                  reads=["rstd%d" % i], writes=["rstd%d" % i])
            S.add("dve", lambda e, t=t, i=i: e.tensor_scalar(out=t[:], in0=t[:], scalar1=rstd[:, i:i + 1], scalar2=None, op0=ALU.mult),
                  reads=[rn, "rstd%d" % i], writes=[rn])
            for q4 in range(4):
                bank = 4 + (q4 % 2)

                def tr(e, t=t, q4=q4, bank=bank):
                    ins = None
                    for k4 in range(4):
                        kc = q4 * 4 + k4
                        ins = e.transpose(out=ps[bank][:, k4 * 128:(k4 + 1) * 128], in_=t[:, kc * 128:(kc + 1) * 128],
                                          identity=ident_f[:])
                    return ins
                S.add("pe", tr, reads=[rn, "ident_f"], writes=[PS[bank]])
                for k4 in range(4):
                    kc = q4 * 4 + k4
                    S.add("act", lambda e, kc=kc, k4=k4, bank=bank, i=i: e.activation(
                        out=hT[:, kc, i * 128:(i + 1) * 128], in_=ps[bank][:, k4 * 128:(k4 + 1) * 128], func=AF.Identity,
                        scale=gmod[:, kc:kc + 1], bias=adaT[:, kc:kc + 1]),
                        reads=[PS[bank], "gmod", "adaT"], writes=["hT"])
        S.barrier()

    if "h" in dbg:
        dbg_out["hT"] = dram_out("dbg_hT", [128, KC, S_LEN], BF16)
        dma("sp", dbg_out["hT"], hT[:], ["hT"], ["dbg_hT"])

    if stop_after == "h":
        S.emit()
        return nc, dbg_out

    att = ExitStack()
    cosT = sb("cosT", [128, S_LEN], F32, att)
    sinS = sb("sinS", [128, S_LEN], F32, att)
    with ExitStack() as st2:
        posi = sb("posi", [128, S_LEN], I32, st2)
        ang = sb("ang", [128, S_LEN], F32, st2)
        nn = sb("nn", [128, S_LEN], F32, st2)
        dma("sp", posi[:], pos_d.partition_broadcast(128), [], ["posi"])
        S.add("dve", lambda e: e.tensor_copy(out=ang[:], in_=posi[:]), reads=["posi"], writes=["ang"])
        S.add("dve", lambda e: e.tensor_scalar(out=ang[:], in0=ang[:], scalar1=small["inv_freq"][:, 0:1], scalar2=None, op0=ALU.mult),
              reads=["ang", "inv_freq"], writes=["ang"])
        MAGIC = 12582912.0
        S.add("dve", lambda e: e.tensor_scalar(out=nn[:], in0=ang[:], scalar1=1.0 / (2 * math.pi), scalar2=MAGIC, op0=ALU.mult, op1=ALU.add),
              reads=["ang"], writes=["nn"])
        S.add("dve", lambda e: e.tensor_scalar(out=nn[:], in0=nn[:], scalar1=-MAGIC, scalar2=None, op0=ALU.add),
              reads=["nn"], writes=["nn"])
        C1 = 6.28125
        C2 = 2 * math.pi - C1
        S.add("dve", lambda e: e.scalar_tensor_tensor(out=ang[:], in0=nn[:], scalar=-C1, in1=ang[:], op0=ALU.mult, op1=ALU.add),
              reads=["nn", "ang"], writes=["ang"])
        S.add("dve", lambda e: e.scalar_tensor_tensor(out=ang[:], in0=nn[:], scalar=-C2, in1=ang[:], op0=ALU.mult, op1=ALU.add),
              reads=["nn", "ang"], writes=["ang"])
        PI_LO = 3.1415925
        S.add("dve", lambda e: e.tensor_scalar(out=ang[:], in0=ang[:], scalar1=PI_LO, scalar2=-PI_LO, op0=ALU.min, op1=ALU.max),
              reads=["ang"], writes=["ang"])
        S.add("act", lambda e: e.activation(out=sinS[:], in_=ang[:], func=AF.Sin), reads=["ang"], writes=["sinS"])
        S.add("dve", lambda e: e.tensor_scalar(out=nn[:], in0=ang[:], scalar1=-1.0, scalar2=None, op0=ALU.mult), reads=["ang"], writes=["nn"])
        S.add("dve", lambda e: e.tensor_tensor(out=nn[:], in0=nn[:], in1=ang[:], op=ALU.max), reads=["ang", "nn"], writes=["nn"])
        S.add("dve", lambda e: e.tensor_scalar(out=nn[:], in0=nn[:], scalar1=-1.0, scalar2=math.pi / 2, op0=ALU.mult, op1=ALU.add),
              reads=["nn"], writes=["nn"])
        S.add("act", lambda e: e.activation(out=cosT[:], in_=nn[:], func=AF.Sin), reads=["nn"], writes=["cosT"])
        S.add("dve", lambda e: e.tensor_scalar(out=sinS[64:128, :], in0=sinS[64:128, :], scalar1=-1.0, scalar2=None, op0=ALU.mult),
              reads=["sinS"], writes=["sinS"])
        S.barrier()

    wb_rr = [0]

    def load_w(src2d, nk, ncols):
        k = wb_rr[0] % NWB
        wb_rr[0] += 1
        t = wbuf[k]
        rn = "wb%d" % k
        dma("pool", t[:, 0:nk, 0:ncols], src2d.rearrange("(kc p) c -> p kc c", p=128), [], [rn])
        return t, rn

    def wcols(tile0, ntiles):
        return win_d[:, tile0 * 128:(tile0 + ntiles) * 128]

    pj_rr = [0]

    def proj_fm(wt, wrn, c0, tb):
        bank = pj_rr[0] % 2
        pj_rr[0] += 1

        def mm(e):
            ins = None
            for kc in range(KC):
                ins = e.matmul(ps[bank][:], lhsT=wt[:, kc, c0:c0 + 128], rhs=hT[:, kc, tb * 512:(tb + 1) * 512],
                               start=(kc == 0), stop=(kc == KC - 1))
            return ins
        S.add("pe", mm, reads=[wrn, "hT"], writes=[PS[bank]])
        return bank

    QT = [sb("QT%d" % g, [128, S_LEN], BF16, att) for g in range(3)]
    KT = [sb("KT%d" % g, [128, S_LEN], BF16, att) for g in range(3)]
    Vt = [sb("Vt%d" % g, [128, 16, 128], BF16, att) for g in range(3)]
    sza = sb("sza", [128, S_LEN], BF16, att)
    rt1 = [sb("rt1_%d" % i, [128, 512], F32, att) for i in range(2)]
    rt2 = [sb("rt2_%d" % i, [128, 512], F32, att) for i in range(2)]
    PT = [sb("PT%d" % i, [128, 256], BF16, att) for i in range(3)]
    rd = sb("rd", [128, 512], F32, att)
    ot = sb("ot", [128, 512], F32, att)
    ub = [sb("ub%d" % i, [128, 512], BF16, att) for i in range(2)]
    rot_rr = [0]
    st_rr = [0]
    ub_rr = [0]
    SCALE = 128.0 ** -0.5

    def rotary_evac(bank, dest, dest_rn, tb):
        k = rot_rr[0] % 2
        rot_rr[0] += 1
        t1, t2 = rt1[k], rt2[k]
        sl = slice(tb * 512, (tb + 1) * 512)
        S.add("dve", lambda e: e.tensor_tensor(out=t1[:], in0=ps[bank][:], in1=cosT[:, sl], op=ALU.mult),
              reads=[PS[bank], "cosT"], writes=["rt1_%d" % k])
        S.add("dve", lambda e: e.tensor_tensor(out=t2[0:64, :], in0=ps[bank][64:128, :], in1=sinS[64:128, sl], op=ALU.mult),
              reads=[PS[bank], "sinS"], writes=["rt2a_%d" % k])
        S.add("dve", lambda e: e.tensor_tensor(out=t2[64:128, :], in0=ps[bank][0:64, :], in1=sinS[0:64, sl], op=ALU.mult),
              reads=[PS[bank], "sinS"], writes=["rt2b_%d" % k])
        S.add("dve", lambda e: e.tensor_tensor(out=dest[:, sl], in0=t1[:], in1=t2[:], op=ALU.add),
              reads=["rt1_%d" % k, "rt2a_%d" % k, "rt2b_%d" % k], writes=[dest_rn])

    n_hh = 12 if stop_after != "att1" else 1
    for hh in range(n_hh):
        base = OFF_ATT + hh * 10
        w1, w1n = load_w(wcols(base, 4), KC, 512)
        w2, w2n = load_w(wcols(base + 4, 4), KC, 512)
        w3, w3n = load_w(wcols(base + 8, 2), KC, 256)
        qk = [(w1, w1n, 0, QT[0], "QT0"), (w1, w1n, 128, QT[1], "QT1"), (w1, w1n, 256, QT[2], "QT2"),
              (w1, w1n, 384, KT[0], "KT0"), (w2, w2n, 0, KT[1], "KT1"), (w2, w2n, 128, KT[2], "KT2")]
        def do_qk():
            for (wt, wrn, c0, dest, drn) in qk:
                for tb in range(4):
                    bank = proj_fm(wt, wrn, c0, tb)
                    rotary_evac(bank, dest, drn, tb)

        def do_v():
            for gi, (wt, wrn, c0) in enumerate(((w2, w2n, 256), (w2, w2n, 384), (w3, w3n, 0))):
                d = DILS[gi]
                nlb = 16 // d
                for q4 in range(4):
                    bank = 6 + (q4 % 2)

                    def vmm(e, q4=q4, bank=bank, d=d, nlb=nlb, wt=wt, c0=c0):
                        ins = None
                        for k4 in range(4):
                            tid = q4 * 4 + k4
                            r, lb = tid // nlb, tid % nlb
                            s0 = r + d * 128 * lb
                            for kc in range(KC):
                                ins = e.matmul(ps[bank][:, k4 * 128:(k4 + 1) * 128], lhsT=hT[:, kc, s0:s0 + d * 127 + 1:d],
                                               rhs=wt[:, kc, c0:c0 + 128], start=(kc == 0), stop=(kc == KC - 1),
                                               skip_group_check=True)
                        return ins
                    S.add("pe", vmm, reads=[wrn, "hT"], writes=[PS[bank]])
                    S.add("act", lambda e, gi=gi, q4=q4, bank=bank: e.copy(
                        out=Vt[gi][:, q4 * 4:(q4 + 1) * 4, :], in_=ps[bank][:].rearrange("p (a b) -> p a b", a=4)),
                        reads=[PS[bank]], writes=["Vt%d" % gi])

        def do_za():
            for tb in range(4):
                bank = proj_fm(w3, w3n, 128, tb)
                S.add("act", lambda e, bank=bank, tb=tb: e.activation(out=sza[:, tb * 512:(tb + 1) * 512], in_=ps[bank][:], func=AF.Silu),
                      reads=[PS[bank]], writes=["sza"])

        if hh == 0:
            do_v()
            do_za()
            do_qk()
        else:
            do_qk()
            do_v()
            do_za()
        for QB in range(4):
            first = True
            pend = None
            for gi, d in enumerate(DILS):
                nlb = 16 // d
                nq = 512 // d
                lq0 = QB * nq
                for r in range(d):
                    lb_lo = max(0, (lq0 - 64) // 128)
                    lb_hi = min(nlb - 1, (lq0 + nq - 1 + 64) // 128)
                    for lb in range(lb_lo, lb_hi + 1):
                        k0 = lb * 128
                        qs = max(lq0, k0 - 64)
                        qe = min(lq0 + nq, k0 + 128 + 64)
                        nqq = qe - qs
                        if nqq <= 0:
                            continue
                        off = qs - k0 + 64
                        qtok0 = r + d * qs
                        ktok0 = r + d * k0
                        oc0 = r + d * (qs - lq0)
                        stb = 2 + (st_rr[0] % 2)
                        pk = st_rr[0] % 3
                        st_rr[0] += 1
                        pt = PT[pk]

                        def smm(e, stb=stb, nqq=nqq, off=off, gi=gi, d=d, ktok0=ktok0, qtok0=qtok0):
                            e.matmul(ps[stb][:, 0:nqq], lhsT=ident_b[:], rhs=negm[:, off:off + nqq], start=True, stop=False)
                            return e.matmul(ps[stb][:, 0:nqq], lhsT=KT[gi][:, ktok0:ktok0 + d * 127 + 1:d],
                                            rhs=QT[gi][:, qtok0:qtok0 + d * (nqq - 1) + 1:d], start=False, stop=True)
                        S.add("pe", smm, reads=["KT%d" % gi, "QT%d" % gi, "negm", "ident_b"], writes=[PS[stb]])
                        S.add("act", lambda e, stb=stb, nqq=nqq, pt=pt: e.activation(out=pt[:, 0:nqq], in_=ps[stb][:, 0:nqq],
                                                                                      func=AF.Exp, scale=SCALE),
                              reads=[PS[stb]], writes=["PT%d" % pk])

                        def pv(e, first=first, oc0=oc0, d=d, nqq=nqq, gi=gi, tid=r * nlb + lb, pt=pt):
                            osl = slice(oc0, oc0 + d * (nqq - 1) + 1, d)
                            e.matmul(ps[4][:, osl], lhsT=Vt[gi][:, tid, :], rhs=pt[:, 0:nqq], start=first, stop=False,
                                     skip_group_check=True)
                            return e.matmul(ps[5][:, osl], lhsT=ones_b[:], rhs=pt[:, 0:nqq], start=first, stop=False,
                                            skip_group_check=True)
                        if pend:
                            S.add("pe", pend[0], reads=pend[1], writes=[PS[4], PS[5]])
                        pend = (pv, ["Vt%d" % gi, "PT%d" % pk, "ones_b", PS[4], PS[5]])
                        first = False
            if pend:
                S.add("pe", pend[0], reads=pend[1], writes=[PS[4], PS[5]])
            uk = ub_rr[0] % 2
            ub_rr[0] += 1
            sl = slice(QB * 512, (QB + 1) * 512)
            S.add("dve", lambda e: e.reciprocal(out=rd[:], in_=ps[5][:]), reads=[PS[5]], writes=["rd"])
            S.add("dve", lambda e: e.tensor_tensor(out=ot[:], in0=ps[4][:], in1=rd[:], op=ALU.mult), reads=[PS[4], "rd"], writes=["ot"])
            S.add("dve", lambda e, uk=uk, sl=sl: e.tensor_tensor(out=ub[uk][:], in0=ot[:], in1=sza[:, sl], op=ALU.mult),
                  reads=["ot", "sza"], writes=["ub%d" % uk])
            dma("sp", ua_scr[hh, :, sl], ub[uk][:], ["ub%d" % uk], ["ua_scr"])

    if "att" in dbg:
        dbg_out["ua"] = dram_out("dbg_ua", [12, 128, S_LEN], BF16)
        dma("sp", dbg_out["ua"], ua_scr, ["ua_scr"], ["dbg_ua"])
        dbg_out["QT0"] = dram_out("dbg_QT0", [128, S_LEN], BF16)
        dma("sp", dbg_out["QT0"], QT[0][:], ["QT0"], ["dbg_QT0"])
    print("sbuf remaining after attention alloc:", nc.sbuf_bytes_remaining)
    if stop_after in ("att", "att1"):
        S.emit()
        return nc, dbg_out
    S.barrier()
    att.close()

    xbc_scr = dram_tmp("xbc_scr", [48, 128, S_LEN], BF16)
    sz_scr = dram_tmp("sz_scr", [S_LEN, 4096], BF16)
    sg_scr = dram_tmp("sg_scr", [32, 128, S_LEN], BF16)
    us_scr = dram_tmp("us_scr", [32, 128, S_LEN], BF16)

    for nm, shp, src_d in (("conv_wT", [128, 48, 5], convw_d), ("conv_bT", [128, 48], convb_d), ("dt_biasT", [128, 1], dtb_d),
                           ("a_logT", [128, 1], alog_d), ("ssd_norm_gT", [128, 32], ssdg_d)):
        dma("sp", small[nm][:], src_d, [], [nm])
    dma("sp", dskip_row[:], dskip_d.partition_broadcast(128), [], ["dskip_row"])

    with ExitStack() as st3:
        dtT = sb("dtT", [128, S_LEN], F32, st3)
        aT = sb("aT", [128, S_LEN], F32, st3)
        Aneg = sb("Aneg", [128, 1], F32, st3)
        stage = [sb("stage%d" % i, [128, S_LEN + 4], F32, st3) for i in range(2)]
        acc = [sb("acc%d" % i, [128, S_LEN], F32, st3) for i in range(2)]
        cvT = [sb("cvT%d" % i, [128, S_LEN], BF16, st3) for i in range(2)]
        szb = [sb("szb%d" % i, [128, 512], BF16, st3) for i in range(2)]
        for i in range(2):
            S.add("dve", lambda e, i=i: e.memset(stage[i][:, 0:2], 0.0), writes=["stage%d" % i])
            S.add("dve", lambda e, i=i: e.memset(stage[i][:, S_LEN + 2:S_LEN + 4], 0.0), writes=["stage%d" % i])
        wd, wdn = load_w(wcols(OFF_DT, 1), KC, 128)
        for tb in range(4):
            bank = proj_fm(wd, wdn, 0, tb)
            sl = slice(tb * 512, (tb + 1) * 512)
            S.add("act", lambda e, bank=bank, sl=sl: e.activation(out=dtT[:, sl], in_=ps[bank][:], func=AF.Exp,
                                                                  bias=small["dt_biasT"][:, 0:1]),
                  reads=[PS[bank], "dt_biasT"], writes=["dtT"])
        S.add("act", lambda e: e.activation(out=dtT[:], in_=dtT[:], func=AF.Ln, bias=1.0), reads=["dtT"], writes=["dtT"])
        S.add("act", lambda e: e.activation(out=Aneg[:], in_=small["a_logT"][:], func=AF.Exp), reads=["a_logT"], writes=["Aneg"])
        S.add("dve", lambda e: e.tensor_scalar(out=aT[:], in0=dtT[:], scalar1=Aneg[:, 0:1], scalar2=-1.0, op0=ALU.mult, op1=ALU.mult),
              reads=["dtT", "Aneg"], writes=["aT"])
        for i in range(NT):
            bank = 6 + (i % 2)

            def tr2(e, i=i, bank=bank):
                e.transpose(out=ps[bank][:, 0:128], in_=dtT[:, i * 128:(i + 1) * 128], identity=ident_f[:])
                return e.transpose(out=ps[bank][:, 128:256], in_=aT[:, i * 128:(i + 1) * 128], identity=ident_f[:])
            S.add("pe", tr2, reads=["dtT", "aT", "ident_f"], writes=[PS[bank]])
            S.add("act", lambda e, i=i, bank=bank: e.copy(out=dt_tok[:, i, :], in_=ps[bank][:, 0:128]), reads=[PS[bank]], writes=["dt_tok"])
            S.add("act", lambda e, i=i, bank=bank: e.copy(out=a_tok[:, i, :], in_=ps[bank][:, 128:256]), reads=[PS[bank]], writes=["a_tok"])

        cv_rr = [0]

        def conv_tile(wt, wrn, c0, cc):
            k = cv_rr[0] % 2
            cv_rr[0] += 1
            stg, ac, cv = stage[k], acc[k], cvT[k]
            for tb in range(4):
                bank = proj_fm(wt, wrn, c0, tb)
                sl = slice(tb * 512, (tb + 1) * 512)
                S.add("act", lambda e, bank=bank, tb=tb, stg=stg: e.copy(out=stg[:, 2 + tb * 512:2 + (tb + 1) * 512], in_=ps[bank][:]),
                      reads=[PS[bank]], writes=["stage%d" % k])
                S.add("act", lambda e, bank=bank, sl=sl, ac=ac: e.activation(out=ac[:, sl], in_=ps[bank][:], func=AF.Identity,
                                                                             scale=small["conv_wT"][:, cc, 2:3],
                                                                             bias=small["conv_bT"][:, cc:cc + 1]),
                      reads=[PS[bank], "conv_wT", "conv_bT"], writes=["acc%d" % k])
            for tap in (0, 1, 3, 4):
                S.add("dve", lambda e, tap=tap, stg=stg, ac=ac: e.scalar_tensor_tensor(
                    out=ac[:], in0=stg[:, tap:tap + S_LEN], scalar=small["conv_wT"][:, cc, tap:tap + 1], in1=ac[:],
                    op0=ALU.mult, op1=ALU.add),
                    reads=["stage%d" % k, "acc%d" % k, "conv_wT"], writes=["acc%d" % k])
            S.add("act", lambda e, ac=ac, cv=cv: e.activation(out=cv[:], in_=ac[:], func=AF.Silu), reads=["acc%d" % k], writes=["cvT%d" % k])
            dma("sp", xbc_scr[cc], cv[:], ["cvT%d" % k], ["xbc_scr"])

        sz_rr = [0]
        n_g = 8
        for g in range(n_g):
            base = OFF_SSD + g * 10
            wbc, wbcn = load_w(wcols(base, 2), KC, 256)
            conv_tile(wbc, wbcn, 0, 32 + g)
            conv_tile(wbc, wbcn, 128, 40 + g)
            wx, wxn = load_w(wcols(base + 2, 4), KC, 512)
            for j in range(4):
                conv_tile(wx, wxn, j * 128, 4 * g + j)
            wz, wzn = load_w(wcols(base + 6, 4), KC, 512)
            for i in range(NT):
                bank = pj_rr[0] % 2
                pj_rr[0] += 1

                def zmm(e, i=i, bank=bank, wz=wz):
                    ins = None
                    for kc in range(KC):
                        ins = e.matmul(ps[bank][:], lhsT=hT[:, kc, i * 128:(i + 1) * 128], rhs=wz[:, kc, 0:512],
                                       start=(kc == 0), stop=(kc == KC - 1))
                    return ins
                S.add("pe", zmm, reads=[wzn, "hT"], writes=[PS[bank]])
                k = sz_rr[0] % 2
                sz_rr[0] += 1
                S.add("act", lambda e, bank=bank, k=k: e.activation(out=szb[k][:], in_=ps[bank][:], func=AF.Silu),
                      reads=[PS[bank]], writes=["szb%d" % k])
                dma("sp", sz_scr[i * 128:(i + 1) * 128, g * 512:(g + 1) * 512], szb[k][:], ["szb%d" % k], ["sz_scr"])
        for gb in range(8):
            wg, wgn = load_w(wcols(OFF_GA + gb * 4, 4), KC, 512)
            for j4 in range(4):
                j = gb * 4 + j4
                k = cv_rr[0] % 2
                cv_rr[0] += 1
                for tb in range(4):
                    bank = proj_fm(wg, wgn, j4 * 128, tb)
                    S.add("act", lambda e, bank=bank, tb=tb, k=k: e.activation(out=cvT[k][:, tb * 512:(tb + 1) * 512], in_=ps[bank][:],
                                                                                  func=AF.Sigmoid),
                          reads=[PS[bank]], writes=["cvT%d" % k])
                dma("sp", sg_scr[j], cvT[k][:], ["cvT%d" % k], ["sg_scr"])
        S.barrier()
    hst.close()
    print("sbuf remaining after proj phase:", nc.sbuf_bytes_remaining)
    if stop_after == "proj":
        S.emit()
        return nc, dbg_out

    S.add("dve", lambda e: e.memset(ssq[:], 0.0), writes=["ssq_ssd"])
    with ExitStack() as st4:
        triM = sb("triM", [128, 128], F32, st4)
        ustM = sb("ustM", [128, 128], F32, st4)
        geM = sb("geM", [128, 128], F32, st4)
        lstM = sb("lstM", [128, 128], F32, st4)
        for (m, nm, pat, base, cm, op) in ((triM, "triM", 1, 0, -1, ALU.is_ge), (ustM, "ustM", -1, 0, 1, ALU.is_gt),
                                           (geM, "geM", -1, 0, 1, ALU.is_ge), (lstM, "lstM", 1, 0, -1, ALU.is_gt)):
            S.add("pool", lambda e, m=m: e.memset(m[:], 1.0), writes=[nm])
            S.add("pool", lambda e, m=m, pat=pat, base=base, cm=cm, op=op: e.affine_select(
                out=m[:], in_=m[:], pattern=[[pat, 128]], compare_op=op, fill=0.0, base=base, channel_multiplier=cm),
                reads=[nm], writes=[nm])
        BT = sb("BT", [128, S_LEN], BF16, st4)
        CT = sb("CT", [128, S_LEN], BF16, st4)
        xsT = [sb("xsT%d" % j, [128, S_LEN], BF16, st4) for j in range(4)]
        sz_tok = sb("sz_tok", [128, NT, 512], BF16, st4)
        B_tok = sb("B_tok", [128, NT, 128], BF16, st4)
        xdt = [sb("xdt%d" % d_, [128, NT, 512], BF16, st4) for d_ in range(2)]
        yacc = sb("yacc", [128, NT, 512], F32, st4)
        est = sb("est", [128, NT, 48], F32, st4)
        rhsD = [sb("rhsD0", [128, 8, 128], F32, st4)] * 2
        Ebuf = [sb("Ebuf0", [128, 8, 128], F32, st4)] * 2
        mcb = [sb("mcb%d" % i, [128, 128], F32, st4) for i in range(2)]
        MT = [sb("MT0", [128, 8, 128], BF16, st4)] * 2
        xdec = [sb("xdec0", [128, 512], BF16, st4)] * 2
        carry = [sb("carry%d" % d_, [128, 512], F32, st4) for d_ in range(2)]
        prevb = [sb("prevb%d" % d_, [128, 512], BF16, st4) for d_ in range(2)]
        ytmp = [sb("ytmp0", [128, 512], F32, st4)] * 2
        vtmp = sb("vtmp", [128, 512], F32, st4)
        vjunk = sb("vjunk", [128, 512], F32, st4)
        s1 = sb("s1", [128, 1], F32, st4)
        vb = [sb("vb0", [128, 512], BF16, st4)] * 2
        usT = xsT
        rr = {"d": 0, "y": 0, "v": 0}
        wv = wbuf[0]
        f32v = lambda ap: ap.bitcast(F32).rearrange("p a (b c) -> p (a b) c", c=128)
        Ebuf2 = [Ebuf[0][:], f32v(wv[:, 0:4, :])]
        rhsD2 = [rhsD[0][:], f32v(wv[:, 4:8, :])]
        ytmp2 = [ytmp[0][:], wv[:, 8:10, :].bitcast(F32).rearrange("p a b -> p (a b)")]
        MT2 = [MT[0][:], wv[:, 10:12, :].rearrange("p a (b c) -> p (a b) c", c=128)]
        xdec2 = [xdec[0][:], wv[:, 12, :]]

        def bc_p(ap2):
            return ap2.unsqueeze(2).to_broadcast([128, 8, 64])

        def v3(ap2):
            return ap2.rearrange("p (e q) -> p e q", e=8)

        wscr = dram_tmp("wscr", [16, 128, KC, 512], BF16)
        slabs = []
        for jb_ in range(4):
            slabs.append((wbra_d[:, jb_ * 512:(jb_ + 1) * 512], 12))
            slabs.append((wbrs_d[0:2048, jb_ * 512:(jb_ + 1) * 512], 16))
            slabs.append((wbrs_d[2048:4096, jb_ * 512:(jb_ + 1) * 512], 16))
        for ob_ in range(4):
            slabs.append((wout_d[:, ob_ * 512:(ob_ + 1) * 512], 16))

        def precast(si_, k_):
            src2d, nk_ = slabs[si_]
            t_ = wbuf[k_]
            rn_ = "wb%d" % k_
            dma("pool", t_[:, 0:nk_, :], src2d.rearrange("(kc p) c -> p kc c", p=128), [], [rn_])
            dma("sp", wscr[si_, :, 0:nk_, :], t_[:, 0:nk_, :], [rn_], ["wscr"])

        for g in range(n_g):
            precast(2 * g, 1)
            precast(2 * g + 1, 2)
            dma("sp", BT[:], xbc_scr[32 + g], ["xbc_scr"], ["BT"])
            dma("sp", CT[:], xbc_scr[40 + g], ["xbc_scr"], ["CT"])
            for j in range(4):
                dma("sp", xsT[j][:], xbc_scr[4 * g + j], ["xbc_scr"], ["xsT%d" % j])
            dma("sp", sz_tok[:], sz_scr[:, g * 512:(g + 1) * 512].rearrange("(i p) c -> p i c", p=128), ["sz_scr"], ["sz_tok"])
            fcol = slice(g * 8, g * 8 + 8)
            bcol = slice(64 + g * 8, 64 + g * 8 + 8)
            for c in range(NT):
                csl = slice(c * 128, (c + 1) * 128)
                pb7 = ps[7][:].bitcast(BF16)

                def trx(e, csl=csl, pb7=pb7):
                    ins = None
                    for j in range(4):
                        ins = e.transpose(out=pb7[:, j * 128:(j + 1) * 128], in_=xsT[j][:, csl], identity=ident_b[:])
                    return e.transpose(out=pb7[:, 512:640], in_=BT[:, csl], identity=ident_b[:])
                S.add("pe", trx, reads=["xsT0", "xsT1", "xsT2", "xsT3", "BT", "ident_b"], writes=[PS[7]])
                S.add("dve", lambda e, c=c, pb7=pb7: e.tensor_copy(out=B_tok[:, c, :], in_=pb7[:, 512:640]), reads=[PS[7]], writes=["B_tok"])
                S.add("dve", lambda e, c=c, pb7=pb7, fcol=fcol: e.tensor_tensor(out=v3(xdt[0][:, c, :]), in0=v3(pb7[:, 0:512]),
                                                                     in1=bc_p(dt_tok[:, c, fcol]), op=ALU.mult),
                      reads=[PS[7], "dt_tok"], writes=["xdt0"])
                S.add("dve", lambda e, c=c, pb7=pb7, bcol=bcol: e.tensor_tensor(out=v3(xdt[1][:, c, :]), in0=v3(pb7[:, 0:512]),
                                                                     in1=bc_p(dt_tok[:, c, bcol]), op=ALU.mult),
                      reads=[PS[7], "dt_tok"], writes=["xdt1"])
                S.add("dve", lambda e, c=c, pb7=pb7, fcol=fcol: e.tensor_tensor(out=v3(yacc[:, c, :]), in0=v3(pb7[:, 0:512]),
                                                                     in1=bc_p(dskip_row[:, fcol]), op=ALU.mult),
                      reads=[PS[7], "dskip_row"], writes=["yacc%d" % c])

                def stat(e, c=c, fcol=fcol, bcol=bcol):
                    af = a_tok[:, c, fcol]
                    ab = a_tok[:, c, bcol]
                    e.matmul(ps[6][:, 0:8], lhsT=triM[:], rhs=af, start=True, stop=True, skip_group_check=True)
                    e.matmul(ps[6][:, 8:16], lhsT=ustM[:], rhs=af, start=True, stop=True, skip_group_check=True)
                    e.matmul(ps[6][:, 16:24], lhsT=geM[:], rhs=ab, start=True, stop=True, skip_group_check=True)
                    e.matmul(ps[6][:, 24:32], lhsT=lstM[:], rhs=ab, start=True, stop=True, skip_group_check=True)
                    e.matmul(ps[6][:, 32:40], lhsT=ones_f[:], rhs=af, start=True, stop=True, skip_group_check=True)
                    return e.matmul(ps[6][:, 40:48], lhsT=ones_f[:], rhs=ab, start=True, stop=True, skip_group_check=True)
                S.add("pe", stat, reads=["a_tok", "triM", "ustM", "geM", "lstM", "ones_f"], writes=[PS[6]])
                S.add("act", lambda e, c=c: e.activation(out=est[:, c, :], in_=ps[6][:, 0:48], func=AF.Exp), reads=[PS[6]], writes=["est"])

            steps = [(ci, dr) for ci in range(NT) for dr in range(2)]

            def stage_r(si, ci, dr):
                c = ci if dr == 0 else NT - 1 - ci
                acol = fcol if dr == 0 else bcol
                maskR = triM if dr == 0 else geM
                k = si % 2
                rD = rhsD2[k]
                def mk_rhs(e_):
                    ins = None
                    for h8 in range(8):
                        ins = e_.activation(out=rD[:, h8, :], in_=maskR[:], func=AF.Copy,
                                            scale=a_tok[:, c, acol.start + h8:acol.start + h8 + 1])
                    return ins
                S.add("act", mk_rhs, reads=["a_tok", "triM", "geM"], writes=["rhsD%d" % k])

            def stage_a(si, ci, dr):
                c = ci if dr == 0 else NT - 1 - ci
                csl = slice(c * 128, (c + 1) * 128)
                maskR = triM if dr == 0 else geM
                maskL = ustM if dr == 0 else lstM
                k = si % 2
                rD, Eb, Mt, mc = rhsD2[k], Ebuf2[k], MT2[k], mcb[k]
                if ci < NT - 1:
                    eo_ = 0 if dr == 0 else 16
                    xd_ = xdec2[k]
                    S.add("dve", lambda e: e.tensor_tensor(out=v3(xd_), in0=v3(xdt[dr][:, c, :]), in1=bc_p(est[:, c, eo_ + 8:eo_ + 16]), op=ALU.mult),
                          reads=["xdt%d" % dr, "est"], writes=["xdec%d" % k])

                def dmm(e):
                    e.matmul(ps[2][:], lhsT=maskL[:], rhs=rD[:, 0:4, :], start=True, stop=True)
                    return e.matmul(ps[3][:], lhsT=maskL[:], rhs=rD[:, 4:8, :], start=True, stop=True)
                S.add("pe", dmm, reads=["rhsD%d" % k, "ustM", "lstM"], writes=[PS[2], PS[3]])
                S.add("act", lambda e: e.activation(out=Eb[:, 0:4, :], in_=ps[2][:].rearrange("p (e l) -> p e l", e=4), func=AF.Exp),
                      reads=[PS[2]], writes=["Ebuf%d" % k])
                S.add("act", lambda e: e.activation(out=Eb[:, 4:8, :], in_=ps[3][:].rearrange("p (e l) -> p e l", e=4), func=AF.Exp),
                      reads=[PS[3]], writes=["Ebuf%d" % k])
                S.add("pe", lambda e: e.matmul(ps[5][:, 0:128], lhsT=BT[:, csl], rhs=CT[:, csl], start=True, stop=True),
                      reads=["BT", "CT"], writes=[PS[5]])
                S.add("dve", lambda e: e.tensor_tensor(out=mc[:], in0=ps[5][:, 0:128], in1=maskR[:], op=ALU.mult),
                      reads=[PS[5], "triM", "geM"], writes=["mcb%d" % k])
                S.add("dve", lambda e: e.tensor_tensor(out=Mt, in0=Eb, in1=mc[:].unsqueeze(1).to_broadcast([128, 8, 128]), op=ALU.mult),
                      reads=["Ebuf%d" % k, "mcb%d" % k], writes=["MT%d" % k])

            def stage_b(si, ci, dr):
                c = ci if dr == 0 else NT - 1 - ci
                csl = slice(c * 128, (c + 1) * 128)
                eo = 0 if dr == 0 else 16
                cdo = 32 if dr == 0 else 40
                k = si % 2
                Mt, xd, yt = MT2[k], xdec2[k], ytmp2[k]

                def ydiag(e):
                    ins = None
                    for h8 in range(8):
                        ins = e.matmul(ps[4][:, h8 * 64:(h8 + 1) * 64], lhsT=Mt[:, h8, :], rhs=xdt[dr][:, c, h8 * 64:(h8 + 1) * 64],
                                       start=True, stop=True, skip_group_check=True)
                    return ins
                S.add("pe", ydiag, reads=["MT%d" % k, "xdt%d" % dr], writes=[PS[4]])
                S.add("dve", lambda e: e.tensor_tensor(out=yacc[:, c, :], in0=yacc[:, c, :], in1=ps[4][:], op=ALU.add),
                      reads=[PS[4], "yacc%d" % c], writes=["yacc%d" % c])
                if ci > 0:
                    S.add("pe", lambda e: e.matmul(ps[1][:], lhsT=CT[:, csl], rhs=prevb[dr][:], start=True, stop=True),
                          reads=["CT", "prevb%d" % dr], writes=[PS[1]])
                    S.add("dve", lambda e: e.tensor_tensor(out=v3(yt), in0=v3(ps[1][:]), in1=bc_p(est[:, c, eo:eo + 8]), op=ALU.mult),
                          reads=[PS[1], "est"], writes=["ytmp%d" % k])
                    S.add("dve", lambda e: e.tensor_tensor(out=yacc[:, c, :], in0=yacc[:, c, :], in1=yt, op=ALU.add),
                          reads=["ytmp%d" % k, "yacc%d" % c], writes=["yacc%d" % c])
                if ci < NT - 1:
                    S.add("pe", lambda e: e.matmul(ps[0][:], lhsT=B_tok[:, c, :], rhs=xd, start=True, stop=True),
                          reads=["B_tok", "xdec%d" % k], writes=[PS[0]])
                    if ci == 0:
                        S.add("dve", lambda e: e.tensor_copy(out=carry[dr][:], in_=ps[0][:]), reads=[PS[0]], writes=["carry%d" % dr])
                    else:
                        S.add("dve", lambda e: e.tensor_tensor(out=v3(carry[dr][:]), in0=v3(carry[dr][:]), in1=bc_p(est[:, c, cdo:cdo + 8]), op=ALU.mult),
                              reads=["carry%d" % dr, "est"], writes=["carry%d" % dr])
                        S.add("dve", lambda e: e.tensor_tensor(out=carry[dr][:], in0=carry[dr][:], in1=ps[0][:], op=ALU.add),
                              reads=["carry%d" % dr, PS[0]], writes=["carry%d" % dr])
                    S.add("pool", lambda e: e.tensor_copy(out=prevb[dr][:], in_=carry[dr][:]), reads=["carry%d" % dr], writes=["prevb%d" % dr])

            stage_r(0, *steps[0])
            for si, (ci, dr) in enumerate(steps):
                if si + 1 < len(steps):
                    stage_r(si + 1, *steps[si + 1])
                stage_a(si, ci, dr)
                if si >= 1:
                    stage_b(si - 1, *steps[si - 1])
            stage_b(len(steps) - 1, *steps[-1])
            if "scan" in dbg and g == 0:
                for nm, t, shp, dt_ in (("yacc", yacc, [128, NT, 512], F32), ("xdt0", xdt[0], [128, NT, 512], BF16),
                                       ("xdt1", xdt[1], [128, NT, 512], BF16), ("B_tok", B_tok, [128, NT, 128], BF16),
                                       ("est", est, [128, NT, 48], F32), ("sz_tok", sz_tok, [128, NT, 512], BF16)):
                    dbg_out[nm] = dram_out("dbg_" + nm, shp, dt_)
                    rds = ["yacc%d" % c_ for c_ in range(NT)] if nm == "yacc" else [nm]
                    dma("sp", dbg_out[nm], t[:], rds, ["dbg_" + nm])
            for c in range(NT):
                vk = rr["v"] % 2
                rr["v"] += 1
                S.add("dve", lambda e, c=c: e.tensor_tensor(out=vtmp[:], in0=yacc[:, c, :], in1=sz_tok[:, c, :], op=ALU.mult),
                      reads=["yacc%d" % c, "sz_tok"], writes=["vtmp"])
                S.add("act", lambda e: e.activation(out=vjunk[:], in_=vtmp[:], func=AF.Square, accum_out=s1[:, 0:1]),
                      reads=["vtmp"], writes=["vjunk", "s1"])
                S.add("dve", lambda e, c=c: e.tensor_tensor(out=ssq[:, c:c + 1], in0=ssq[:, c:c + 1], in1=s1[:, 0:1], op=ALU.add),
                      reads=["s1", "ssq_ssd"], writes=["ssq_ssd"])
                S.add("act", lambda e, vk=vk: e.copy(out=vb[vk][:], in_=vtmp[:]), reads=["vtmp"], writes=["vb0"])
                pb7 = ps[7][:].bitcast(BF16)

                def trv(e, vk=vk, pb7=pb7):
                    ins = None
                    for j in range(4):
                        ins = e.transpose(out=pb7[:, j * 128:(j + 1) * 128], in_=vb[vk][:, j * 128:(j + 1) * 128], identity=ident_b[:])
                    return ins
                S.add("pe", trv, reads=["vb0", "ident_b"], writes=[PS[7]])
                for j in range(4):
                    S.add("act", lambda e, j=j, c=c, pb7=pb7, g=g: e.activation(out=usT[j][:, c * 128:(c + 1) * 128], in_=pb7[:, j * 128:(j + 1) * 128],
                                                                            func=AF.Identity, scale=small["ssd_norm_gT"][:, 4 * g + j:4 * g + j + 1]),
                          reads=[PS[7], "ssd_norm_gT"], writes=["xsT%d" % j])
            for j in range(4):
                dma("sp", us_scr[4 * g + j], usT[j][:], ["xsT%d" % j], ["us_scr"])
        S.barrier()

    if "ssd" in dbg:
        dbg_out["us"] = dram_out("dbg_us", [32, 128, S_LEN], BF16)
        dma("sp", dbg_out["us"], us_scr, ["us_scr"], ["dbg_us"])
        dbg_out["dt_tok"] = dram_out("dbg_dt_tok", [128, NT, 128])
        dma("sp", dbg_out["dt_tok"], dt_tok[:], ["dt_tok"], ["dbg_dt_tok"])
        dbg_out["a_tok"] = dram_out("dbg_a_tok", [128, NT, 128])
        dma("sp", dbg_out["a_tok"], a_tok[:], ["a_tok"], ["dbg_a_tok"])
        dbg_out["xbc"] = dram_out("dbg_xbc", [48, 128, S_LEN], BF16)
        dma("sp", dbg_out["xbc"], xbc_scr, ["xbc_scr"], ["dbg_xbc"])
        dbg_out["ssq"] = dram_out("dbg_ssq", [128, NT])
        dma("sp", dbg_out["ssq"], ssq[:], ["ssq_ssd"], ["dbg_ssq"])
    if stop_after == "ssd":
        S.emit()
        return nc, dbg_out

    with ExitStack() as st5:
        gate_row = sb("gate_row", [128, D], F32, st5)
        fing_row = sb("fing_row", [128, D], F32, st5)
        dma("sp", fing_row[:], fing_d.partition_broadcast(128), [], ["fing_row"])
        dg = sb("dg", [128, 128], F32, st5)
        for kc in range(KC):
            S.add("dve", lambda e, kc=kc: e.tensor_scalar(out=dg[:], in0=ident_f[:], scalar1=adaT[:, 32 + kc:33 + kc],
                                                         scalar2=None, op0=ALU.mult),
                  reads=["ident_f", "adaT"], writes=["dg"])
            S.add("pe", lambda e, kc=kc: e.matmul(ps[6][:, 0:128], lhsT=ones_f[:], rhs=dg[:], start=True, stop=True),
                  reads=["ones_f", "dg"], writes=[PS[6]])
            S.add("act", lambda e, kc=kc: e.copy(out=gate_row[:, kc * 128:(kc + 1) * 128], in_=ps[6][:, 0:128]),
                  reads=[PS[6]], writes=["gate_row"])
        rstdS = sb("rstdS", [128, NT], F32, st5)
        rstdB = sb("rstdB", [128, S_LEN], F32, st5)
        dg2 = sb("dg2", [128, 128], F32, st5)
        S.add("dve", lambda e: e.tensor_scalar(out=rstdS[:], in0=ssq[:], scalar1=1.0 / 4096, scalar2=EPS, op0=ALU.mult, op1=ALU.add),
              reads=["ssq_ssd"], writes=["rstdS"])
        S.add("act", lambda e: e.sqrt(out=rstdS[:], in_=rstdS[:]), reads=["rstdS"], writes=["rstdS"])
        S.add("dve", lambda e: e.reciprocal(out=rstdS[:], in_=rstdS[:]), reads=["rstdS"], writes=["rstdS"])
        for i in range(NT):
            S.add("dve", lambda e, i=i: e.tensor_scalar(out=dg2[:], in0=ident_f[:], scalar1=rstdS[:, i:i + 1], scalar2=None, op0=ALU.mult),
                  reads=["ident_f", "rstdS"], writes=["dg2"])
            S.add("pe", lambda e: e.matmul(ps[6][:, 0:128], lhsT=ones_f[:], rhs=dg2[:], start=True, stop=True),
                  reads=["ones_f", "dg2"], writes=[PS[6]])
            S.add("act", lambda e, i=i: e.copy(out=rstdB[:, i * 128:(i + 1) * 128], in_=ps[6][:, 0:128]), reads=[PS[6]], writes=["rstdB"])
        ua_tb = sb("ua_tb", [128, 12, 512], BF16, st5)
        us_tb = sb("us_tb", [128, 32, 512], BF16, st5)
        sga = [sb("sga%d" % i, [128, 512], BF16, st5) for i in range(2)]
        sgs = [sb("sgs%d" % i, [128, 512], BF16, st5) for i in range(2)]
        mT = sb("mT", [128, 16, 512], BF16, st5)
        m1 = sb("m1", [128, 512], F32, st5)
        m2 = sb("m2", [128, 512], F32, st5)
        xo = [sb("xo%d" % i, [128, D], F32, st5) for i in range(4)]
        fj = sb("fj", [128, D], F32, st5)
        fs = sb("fs", [128, 4], F32, st5)
        ft = sb("ft", [128, 512], F32, st5)
        g_rr = [0]

        def load_ws(si_, nk_):
            k_ = wb_rr[0] % NWB
            wb_rr[0] += 1
            t_ = wbuf[k_]
            rn_ = "wb%d" % k_
            dma("act", t_[:, 0:nk_, :], wscr[si_, :, 0:nk_, :], ["wscr"], [rn_])
            return t_, rn_

        for tb in range(4):
            tsl = slice(tb * 512, (tb + 1) * 512)
            dma("sp", ua_tb[:], ua_scr[:, :, tsl].rearrange("c p t -> p c t"), ["ua_scr"], ["ua_tb"])
            dma("sp", us_tb[:], us_scr[:, :, tsl].rearrange("c p t -> p c t"), ["us_scr"], ["us_tb"])
            for tt in range(4):
                dma("sp", xo[tt][:], x_d[tb * 512 + tt * 128:tb * 512 + (tt + 1) * 128, :], [], ["xo%d" % tt])
            for jb in range(4):
                wa_, wan = load_ws(3 * jb, 12)
                ws0, ws0n = load_ws(3 * jb + 1, 16)
                ws1, ws1n = load_ws(3 * jb + 2, 16)
                for j4 in range(4):
                    j = jb * 4 + j4
                    csl = slice(j4 * 128, (j4 + 1) * 128)
                    gk = g_rr[0] % 2
                    g_rr[0] += 1
                    dma("sp", sga[gk][:], sg_scr[j, :, tsl], ["sg_scr"], ["sga%d" % gk])
                    dma("sp", sgs[gk][:], sg_scr[16 + j, :, tsl], ["sg_scr"], ["sgs%d" % gk])

                    def ya(e, csl=csl, wa_=wa_):
                        ins = None
                        for cc in range(12):
                            ins = e.matmul(ps[0][:], lhsT=wa_[:, cc, csl], rhs=ua_tb[:, cc, :], start=(cc == 0), stop=(cc == 11))
                        return ins
                    S.add("pe", ya, reads=[wan, "ua_tb"], writes=[PS[0]])

                    def ys(e, csl=csl, ws0=ws0, ws1=ws1):
                        ins = None
                        for cc in range(32):
                            w_ = ws0 if cc < 16 else ws1
                            ins = e.matmul(ps[1][:], lhsT=w_[:, cc % 16, csl], rhs=us_tb[:, cc, :], start=(cc == 0), stop=(cc == 31))
                        return ins
                    S.add("pe", ys, reads=[ws0n, ws1n, "us_tb"], writes=[PS[1]])
                    S.add("dve", lambda e, gk=gk: e.tensor_tensor(out=m1[:], in0=ps[0][:], in1=sga[gk][:], op=ALU.mult),
                          reads=[PS[0], "sga%d" % gk], writes=["m1"])
                    S.add("dve", lambda e, tsl=tsl: e.tensor_tensor(out=m2[:], in0=ps[1][:], in1=rstdB[:, tsl], op=ALU.mult),
                          reads=[PS[1], "rstdB"], writes=["m2"])
                    S.add("dve", lambda e, gk=gk: e.tensor_tensor(out=m2[:], in0=m2[:], in1=sgs[gk][:], op=ALU.mult),
                          reads=["m2", "sgs%d" % gk], writes=["m2"])
                    S.add("dve", lambda e, j=j: e.tensor_tensor(out=mT[:, j, :], in0=m1[:], in1=m2[:], op=ALU.add),
                          reads=["m1", "m2"], writes=["mT"])
            for ob in range(4):
                wo, won = load_ws(12 + ob, 16)
                osl = slice(ob * 512, (ob + 1) * 512)
                for tt in range(4):
                    bank = 2 + (tt % 2)

                    def om(e, tt=tt, bank=bank, wo=wo):
                        ins = None
                        for j in range(16):
                            ins = e.matmul(ps[bank][:], lhsT=mT[:, j, tt * 128:(tt + 1) * 128], rhs=wo[:, j, :], start=(j == 0), stop=(j == 15))
                        return ins
                    S.add("pe", om, reads=[won, "mT"], writes=[PS[bank]])
                    S.add("dve", lambda e, bank=bank, osl=osl: e.tensor_tensor(out=ft[:], in0=ps[bank][:], in1=gate_row[:, osl], op=ALU.mult),
                          reads=[PS[bank], "gate_row"], writes=["ft"])
                    S.add("dve", lambda e, tt=tt, osl=osl: e.tensor_tensor(out=xo[tt][:, osl], in0=xo[tt][:, osl], in1=ft[:], op=ALU.add),
                          reads=["ft", "xo%d" % tt], writes=["xo%d" % tt])
            for tt in range(4):
                S.add("act", lambda e, tt=tt: e.activation(out=fj[:], in_=xo[tt][:], func=AF.Square, accum_out=fs[:, tt:tt + 1]),
                      reads=["xo%d" % tt], writes=["fj", "fs%d" % tt])
                S.add("dve", lambda e, tt=tt: e.tensor_scalar(out=fs[:, tt:tt + 1], in0=fs[:, tt:tt + 1], scalar1=1.0 / D, scalar2=EPS,
                                                             op0=ALU.mult, op1=ALU.add), reads=["fs%d" % tt], writes=["fs%d" % tt])
                S.add("act", lambda e, tt=tt: e.sqrt(out=fs[:, tt:tt + 1], in_=fs[:, tt:tt + 1]), reads=["fs%d" % tt], writes=["fs%d" % tt])
                S.add("dve", lambda e, tt=tt: e.reciprocal(out=fs[:, tt:tt + 1], in_=fs[:, tt:tt + 1]), reads=["fs%d" % tt], writes=["fs%d" % tt])
                S.add("dve", lambda e, tt=tt: e.scalar_tensor_tensor(out=xo[tt][:], in0=xo[tt][:], scalar=fs[:, tt:tt + 1], in1=fing_row[:],
                                                                    op0=ALU.mult, op1=ALU.mult),
                      reads=["xo%d" % tt, "fs%d" % tt, "fing_row"], writes=["xo%d" % tt])
                dma("sp", out_d[tb * 512 + tt * 128:tb * 512 + (tt + 1) * 128, :], xo[tt][:], ["xo%d" % tt], ["out"])
    print("n ops:", len(S.ops))
    S.emit()
    return nc, dbg_out


def _prep_inputs(inputs):
    f32 = np.float32
    perm = np.concatenate([np.arange(t * 128, (t + 1) * 128) for t in COL_ORDER])
    w_in = np.ascontiguousarray(inputs["w_in"][0][:, perm])
    shared = {
        "w_ada": np.ascontiguousarray(inputs["w_ada"][0]),
        "b_adaT": np.ascontiguousarray(inputs["b_ada"][0].reshape(48, 128).T),
        "norm_gT": np.ascontiguousarray(inputs["norm_g"][0].reshape(KC, 128).T),
        "w_in": w_in,
        "conv_wT": np.ascontiguousarray(inputs["conv_w"][0].reshape(5, 48, 128).transpose(2, 1, 0)),
        "conv_bT": np.ascontiguousarray(inputs["conv_b"][0].reshape(48, 128).T),
        "dt_biasT": np.ascontiguousarray(inputs["dt_bias"][0].reshape(128, 1)),
        "a_logT": np.ascontiguousarray(inputs["a_log"][0].reshape(128, 1)),
        "d_skip": np.ascontiguousarray(inputs["d_skip"][0].reshape(1, 64)),
        "ssd_norm_gT": np.ascontiguousarray(inputs["ssd_norm_g"][0].reshape(32, 128).T),
        "w_br_attn": np.ascontiguousarray(inputs["w_br_attn"][0]),
        "w_br_ssd": np.ascontiguousarray(inputs["w_br_ssd"][0]),
        "w_out": np.ascontiguousarray(inputs["w_out"][0]),
        "final_g": np.ascontiguousarray(inputs["final_g"].reshape(1, D)),
        "inv_freq": np.tile((10000.0 ** (-np.arange(0, 128, 2, dtype=f32) / f32(128))).astype(f32), 2).reshape(128, 1),
    }
    per_core = []
    for b in range(8):
        m = dict(shared)
        m["x"] = np.ascontiguousarray(inputs["x"][b])
        m["cT"] = np.ascontiguousarray(inputs["c"][b].reshape(KC, 128).T)
        m["pos"] = np.ascontiguousarray(inputs["positions"][b].reshape(1, S_LEN).astype(np.int32))
        per_core.append(m)
    return per_core


def kernel(**inputs):
    nc, _ = build()
    in_maps = _prep_inputs(inputs)
    res = run_bass_kernel_spmd(nc, in_maps, core_ids=list(range(8)))
    return np.stack([np.asarray(r["out"], dtype=np.float32) for r in res.results], axis=0)
```

```python
import math
from contextlib import ExitStack
import numpy as np
import ml_dtypes
import concourse.bass as bass
import concourse.mybir as mybir
from concourse.bass_utils import run_bass_kernel_spmd

F32 = mybir.dt.float32
BF16 = mybir.dt.bfloat16
I32 = mybir.dt.int32
AF = mybir.ActivationFunctionType
ALU = mybir.AluOpType

S_LEN = 2048
D = 2048
NT = 16
KC = 16
IN_COLS = 29824
EPS = 1e-6
DILS = (1, 4, 16)


class _Op:
    __slots__ = ("eng", "fn", "idx", "deps", "is_dma", "milestone", "sem", "semval", "prev_slot_val")


class Sched:
    COMPUTE = ("pe", "act", "dve", "pool")
    NSLOT = 24

    def __init__(self, nc):
        self.nc = nc
        self.ops = []
        self.last_writer = {}
        self.readers = {}
        self.barrier_deps = set()

    def add(self, eng, fn, reads=(), writes=(), dma=False):
        op = _Op()
        op.eng = eng
        op.fn = fn
        op.is_dma = dma
        op.idx = len(self.ops)
        op.milestone = False
        deps = set(self.barrier_deps)
        for r in reads:
            w = self.last_writer.get(r)
            if w is not None:
                deps.add(w)
        for w_ in writes:
            w = self.last_writer.get(w_)
            if w is not None:
                deps.add(w)
            rd = self.readers.get(w_)
            if rd:
                deps.update(rd.values())
        key = ("dma", op.idx) if dma else eng
        for r in reads:
            self.readers.setdefault(r, {})[key] = op.idx
        for w_ in writes:
            self.last_writer[w_] = op.idx
            self.readers[w_] = {}
        deps.discard(op.idx)
        op.deps = deps
        self.ops.append(op)
        return op

    def barrier(self):
        last = {}
        for op in self.ops:
            if op.is_dma:
                last[("dma", op.idx)] = op.idx
            else:
                last[op.eng] = op.idx
        dmas = [k for k in last if isinstance(k, tuple)]
        keep = set(v for k, v in last.items() if not isinstance(k, tuple))
        keep.update(last[k] for k in dmas[-(3 * self.NSLOT):])
        self.barrier_deps = keep

    def emit(self):
        nc = self.nc
        ops = self.ops
        for op in ops:
            for d in op.deps:
                x = ops[d]
                if x.is_dma:
                    continue
                if x.eng == op.eng and not op.is_dma and op.eng == "pe":
                    continue
                x.milestone = True
        sems = {e: nc.alloc_semaphore("sem_" + e) for e in self.COMPUTE}
        dsems = {e: [nc.alloc_semaphore("dsem_%s_%d" % (e, i)) for i in range(self.NSLOT)]
                 for e in ("sp", "pool", "act")}
        cnt = {e: 0 for e in self.COMPUTE}
        dcnt = {e: [0] * self.NSLOT for e in dsems}
        drr = {e: 0 for e in dsems}
        for op in ops:
            if op.is_dma:
                k = drr[op.eng] % self.NSLOT
                drr[op.eng] += 1
                op.sem = dsems[op.eng][k]
                op.prev_slot_val = dcnt[op.eng][k]
                dcnt[op.eng][k] += 16
                op.semval = dcnt[op.eng][k]
            elif op.milestone:
                cnt[op.eng] += 1
                op.sem = sems[op.eng]
                op.semval = cnt[op.eng]
        final_dma = {e: [(dsems[e][k], dcnt[e][k]) for k in range(self.NSLOT) if dcnt[e][k] > 0]
                     for e in dsems}

        def run_engine(ename, eng):
            waited = {}
            for op in ops:
                if op.eng != ename:
                    continue
                need = []
                for d in op.deps:
                    x = ops[d]
                    if (not x.is_dma) and x.eng == op.eng and not op.is_dma and op.eng == "pe":
                        continue
                    need.append((x.sem, x.semval))
                if op.is_dma and op.prev_slot_val > 0:
                    need.append((op.sem, op.prev_slot_val))
                for s, v in need:
                    key = id(s)
                    if waited.get(key, 0) < v:
                        eng.wait_ge(s, v)
                        waited[key] = v
                ins = op.fn(eng)
                if op.is_dma:
                    ins.then_inc(op.sem, 16)
                elif op.milestone:
                    ins.then_inc(op.sem, 1)
            for s, v in final_dma.get(ename, ()):
                if waited.get(id(s), 0) < v:
                    eng.wait_ge(s, v)
                    waited[id(s)] = v

        with nc.Block() as block:
            @block.tensor
            def _(e):
                run_engine("pe", e)

            @block.scalar
            def _(e):
                run_engine("act", e)

            @block.vector
            def _(e):
                run_engine("dve", e)

            @block.gpsimd
            def _(e):
                run_engine("pool", e)

            @block.sync
            def _(e):
                run_engine("sp", e)


def _col_tiles():
    order = []
    for hh in range(12):
        for t in range(3):
            for gi in range(3):
                order.append(t * 36 + gi * 12 + hh)
        order.append(108 + hh)
    order.append(200)
    for g in range(8):
        order.append(184 + g)
        order.append(192 + g)
        for j in range(4):
            order.append(152 + 4 * g + j)
        for j in range(4):
            order.append(120 + 4 * g + j)
    for j in range(16):
        order.append(201 + j)
    for j in range(16):
        order.append(217 + j)
    assert len(order) == 233 and sorted(order) == list(range(233))
    return order


COL_ORDER = _col_tiles()
OFF_ATT = 0
OFF_DT = 120
OFF_SSD = 121
OFF_GA = 201
OFF_GS = 217


def build(stop_after="all", dbg=()):
    nc = bass.Bass("TRN2", target_bir_lowering=False)
    S = Sched(nc)
    es = ExitStack()

    def dram_in(name, shape, dt=F32):
        return nc.dram_tensor(name, list(shape), dt, kind="ExternalInput").ap()

    def dram_out(name, shape, dt=F32):
        return nc.dram_tensor(name, list(shape), dt, kind="ExternalOutput").ap()

    def dram_tmp(name, shape, dt):
        return nc.dram_tensor(name, list(shape), dt, kind="Internal").ap()

    def sb(name, shape, dt, stack=None):
        return (stack or es).enter_context(nc.sbuf_tensor(name, list(shape), dt))

    x_d = dram_in("x", [S_LEN, D])
    cT_d = dram_in("cT", [128, KC])
    pos_d = dram_in("pos", [1, S_LEN], I32)
    wada_d = dram_in("w_ada", [D, 3 * D])
    bada_d = dram_in("b_adaT", [128, 48])
    normg_d = dram_in("norm_gT", [128, KC])
    win_d = dram_in("w_in", [D, IN_COLS])
    convw_d = dram_in("conv_wT", [128, 48, 5])
    convb_d = dram_in("conv_bT", [128, 48])
    dtb_d = dram_in("dt_biasT", [128, 1])
    alog_d = dram_in("a_logT", [128, 1])
    dskip_d = dram_in("d_skip", [1, 64])
    ssdg_d = dram_in("ssd_norm_gT", [128, 32])
    wbra_d = dram_in("w_br_attn", [1536, D])
    wbrs_d = dram_in("w_br_ssd", [4096, D])
    wout_d = dram_in("w_out", [D, D])
    fing_d = dram_in("final_g", [1, D])
    invf_d = dram_in("inv_freq", [128, 1])
    out_d = dram_out("out", [S_LEN, D])

    ua_scr = dram_tmp("ua_scr", [12, 128, S_LEN], BF16)

    dbg_out = {}

    ps = [nc.alloc_psum_tensor("ps%d" % i, [128, 512], F32) for i in range(8)]
    PS = ["ps%d" % i for i in range(8)]

    NWB = 3
    wbuf = [sb("wb%d" % i, [128, KC, 512], BF16) for i in range(NWB)]
    ident_f = sb("ident_f", [128, 128], F32)
    ident_b = sb("ident_b", [128, 128], BF16)
    ones_b = sb("ones_b", [128, 128], BF16)
    ones_f = sb("ones_f", [128, 128], F32)
    negm = sb("negm", [128, 256], BF16)
    adaT = sb("adaT", [128, 48], F32)
    gmod = sb("gmod", [128, KC], F32)
    small = {}
    for nm, shp in (("cT", [128, KC]), ("b_adaT", [128, 48]), ("norm_gT", [128, KC]), ("inv_freq", [128, 1]),
                    ("conv_wT", [128, 48, 5]), ("conv_bT", [128, 48]), ("dt_biasT", [128, 1]), ("a_logT", [128, 1]),
                    ("ssd_norm_gT", [128, 32])):
        small[nm] = sb("s_" + nm, shp, F32)
    dt_tok = sb("dt_tok", [128, NT, 128], F32)
    a_tok = sb("a_tok", [128, NT, 128], F32)
    dskip_row = sb("dskip_row", [128, 64], F32)
    ssq = sb("ssq_ssd", [128, NT], F32)
    hst = ExitStack()
    hT = sb("hT", [128, KC, S_LEN], BF16, hst)

    dma_rr = [0]

    def dma(eng, out, in_, reads, writes):
        S.add(eng, lambda e: e.dma_start(out=out, in_=in_), reads=reads, writes=writes, dma=True)

    def mk_consts():
        S.add("pool", lambda e: e.memset(ident_f[:], 1.0), writes=["ident_f"])
        S.add("pool", lambda e: e.affine_select(out=ident_f[:], in_=ident_f[:], pattern=[[-1, 128]],
                                               compare_op=ALU.is_equal, fill=0.0, base=0, channel_multiplier=1),
              reads=["ident_f"], writes=["ident_f"])
        S.add("dve", lambda e: e.tensor_copy(out=ident_b[:], in_=ident_f[:]), reads=["ident_f"], writes=["ident_b"])
        S.add("dve", lambda e: e.memset(ones_b[:], 1.0), writes=["ones_b"])
        S.add("dve", lambda e: e.memset(ones_f[:], 1.0), writes=["ones_f"])
        S.add("pool", lambda e: e.memset(negm[:], 0.0), writes=["negm"])
        S.add("pool", lambda e: e.affine_select(out=negm[:], in_=negm[:], pattern=[[1, 256]],
                                               compare_op=ALU.is_ge, fill=-30000.0, base=0, channel_multiplier=-1),
              reads=["negm"], writes=["negm"])
        S.add("pool", lambda e: e.affine_select(out=negm[:], in_=negm[:], pattern=[[-1, 256]],
                                               compare_op=ALU.is_ge, fill=-30000.0, base=128, channel_multiplier=1),
              reads=["negm"], writes=["negm"])

    mk_consts()

    dma("sp", small["cT"][:], cT_d, [], ["cT"])
    dma("sp", small["b_adaT"][:], bada_d, [], ["b_adaT"])
    dma("sp", small["norm_gT"][:], normg_d, [], ["norm_gT"])
    dma("sp", small["inv_freq"][:], invf_d, [], ["inv_freq"])

    with ExitStack() as st0:
        wa = [sb("wa%d" % i, [128, KC, 256], F32, st0) for i in range(2)]
        for blk in range(24):
            t = wa[blk % 2]
            rn = "wa%d" % (blk % 2)
            src = wada_d[:, blk * 256:(blk + 1) * 256].rearrange("(kc p) c -> p kc c", p=128)
            dma("sp", t[:], src, [], [rn])

            def mm(e, t=t, blk=blk):
                ins = None
                for jj in range(2):
                    j = blk * 2 + jj
                    for kc in range(KC):
                        ins = e.matmul(ps[7][:, j:j + 1], lhsT=t[:, kc, jj * 128:(jj + 1) * 128],
                                       rhs=small["cT"][:, kc:kc + 1], start=(kc == 0), stop=(kc == KC - 1),
                                       skip_group_check=True)
                return ins
            S.add("pe", mm, reads=[rn, "cT"], writes=[PS[7]])
        S.add("dve", lambda e: e.tensor_tensor(out=adaT[:], in0=ps[7][:, 0:48], in1=small["b_adaT"][:], op=ALU.add),
              reads=[PS[7], "b_adaT"], writes=["adaT"])
        S.add("dve", lambda e: e.scalar_tensor_tensor(out=gmod[:], in0=adaT[:, 16:32], scalar=1.0, in1=small["norm_gT"][:],
                                                     op0=ALU.add, op1=ALU.mult),
              reads=["adaT", "norm_gT"], writes=["gmod"])
        S.barrier()

    if "ada" in dbg:
        dbg_out["ada"] = dram_out("dbg_ada", [128, 48])
        dma("sp", dbg_out["ada"], adaT[:], ["adaT"], ["dbg_ada"])

    with ExitStack() as st1:
        xt = [sb("xt%d" % i, [128, D], F32, st1) for i in range(2)]
        sq = sb("sq_junk", [128, D], F32, st1)
        ssq0 = sb("ssq", [128, NT], F32, st1)
        rstd = sb("rstd", [128, NT], F32, st1)
        for i in range(NT):
            t = xt[i % 2]
            rn = "xt%d" % (i % 2)
            dma("sp", t[:], x_d[i * 128:(i + 1) * 128, :], [], [rn])
            S.add("act", lambda e, t=t, i=i: e.activation(out=sq[:], in_=t[:], func=AF.Square, accum_out=ssq0[:, i:i + 1]),
                  reads=[rn], writes=["sq", "ssq%d" % i])
            S.add("dve", lambda e, i=i: e.tensor_scalar(out=rstd[:, i:i + 1], in0=ssq0[:, i:i + 1], scalar1=1.0 / D, scalar2=EPS,
                                                       op0=ALU.mult, op1=ALU.add),
                  reads=["ssq%d" % i], writes=["rstd%d" % i])
            S.add("act", lambda e, i=i: e.sqrt(out=rstd[:, i:i + 1], in_=rstd[:, i:i + 1]),
                  reads=["rstd%d" % i], writes=["rstd%d" % i])
            S.add("dve", lambda e, i=i: e.reciprocal(out=rstd[:, i:i + 1], in_=rstd[:, i:i + 1]),
                  reads=["rstd%d" % i], writes=["rstd%d" % i])
            S.add("dve", lambda e, t=t, i=i: e.tensor_scalar(out=t[:], in0=t[:], scalar1=rstd[:, i:i + 1], scalar2=None, op0=ALU.mult),
                  reads=[rn, "rstd%d" % i], writes=[rn])
            for q4 in range(4):
                bank = 4 + (q4 % 2)

                def tr(e, t=t, q4=q4, bank=bank):
                    ins = None
                    for k4 in range(4):
                        kc = q4 * 4 + k4
                        ins = e.transpose(out=ps[bank][:, k4 * 128:(k4 + 1) * 128], in_=t[:, kc * 128:(kc + 1) * 128],
                                          identity=ident_f[:])
                    return ins
                S.add("pe", tr, reads=[rn, "ident_f"], writes=[PS[bank]])
                for k4 in range(4):
                    kc = q4 * 4 + k4
                    S.add("act", lambda e, kc=kc, k4=k4, bank=bank, i=i: e.activation(
                        out=hT[:, kc, i * 128:(i + 1) * 128], in_=ps[bank][:, k4 * 128:(k4 + 1) * 128], func=AF.Identity,
                        scale=gmod[:, kc:kc + 1], bias=adaT[:, kc:kc + 1]),
                        reads=[PS[bank], "gmod", "adaT"], writes=["hT"])
        S.barrier()

    if "h" in dbg:
        dbg_out["hT"] = dram_out("dbg_hT", [128, KC, S_LEN], BF16)
        dma("sp", dbg_out["hT"], hT[:], ["hT"], ["dbg_hT"])

    if stop_after == "h":
        S.emit()
        return nc, dbg_out

    att = ExitStack()
    cosT = sb("cosT", [128, S_LEN], F32, att)
    sinS = sb("sinS", [128, S_LEN], F32, att)
    with ExitStack() as st2:
        posi = sb("posi", [128, S_LEN], I32, st2)
        ang = sb("ang", [128, S_LEN], F32, st2)
        nn = sb("nn", [128, S_LEN], F32, st2)
        dma("sp", posi[:], pos_d.partition_broadcast(128), [], ["posi"])
        S.add("dve", lambda e: e.tensor_copy(out=ang[:], in_=posi[:]), reads=["posi"], writes=["ang"])
        S.add("dve", lambda e: e.tensor_scalar(out=ang[:], in0=ang[:], scalar1=small["inv_freq"][:, 0:1], scalar2=None, op0=ALU.mult),
              reads=["ang", "inv_freq"], writes=["ang"])
        MAGIC = 12582912.0
        S.add("dve", lambda e: e.tensor_scalar(out=nn[:], in0=ang[:], scalar1=1.0 / (2 * math.pi), scalar2=MAGIC, op0=ALU.mult, op1=ALU.add),
              reads=["ang"], writes=["nn"])
        S.add("dve", lambda e: e.tensor_scalar(out=nn[:], in0=nn[:], scalar1=-MAGIC, scalar2=None, op0=ALU.add),
              reads=["nn"], writes=["nn"])
        C1 = 6.28125
        C2 = 2 * math.pi - C1
        S.add("dve", lambda e: e.scalar_tensor_tensor(out=ang[:], in0=nn[:], scalar=-C1, in1=ang[:], op0=ALU.mult, op1=ALU.add),
              reads=["nn", "ang"], writes=["ang"])
        S.add("dve", lambda e: e.scalar_tensor_tensor(out=ang[:], in0=nn[:], scalar=-C2, in1=ang[:], op0=ALU.mult, op1=ALU.add),
              reads=["nn", "ang"], writes=["ang"])
        PI_LO = 3.1415925
        S.add("dve", lambda e: e.tensor_scalar(out=ang[:], in0=ang[:], scalar1=PI_LO, scalar2=-PI_LO, op0=ALU.min, op1=ALU.max),
              reads=["ang"], writes=["ang"])
        S.add("act", lambda e: e.activation(out=sinS[:], in_=ang[:], func=AF.Sin), reads=["ang"], writes=["sinS"])
        S.add("dve", lambda e: e.tensor_scalar(out=nn[:], in0=ang[:], scalar1=-1.0, scalar2=None, op0=ALU.mult), reads=["ang"], writes=["nn"])
        S.add("dve", lambda e: e.tensor_tensor(out=nn[:], in0=nn[:], in1=ang[:], op=ALU.max), reads=["ang", "nn"], writes=["nn"])
        S.add("dve", lambda e: e.tensor_scalar(out=nn[:], in0=nn[:], scalar1=-1.0, scalar2=math.pi / 2, op0=ALU.mult, op1=ALU.add),
              reads=["nn"], writes=["nn"])
        S.add("act", lambda e: e.activation(out=cosT[:], in_=nn[:], func=AF.Sin), reads=["nn"], writes=["cosT"])
        S.add("dve", lambda e: e.tensor_scalar(out=sinS[64:128, :], in0=sinS[64:128, :], scalar1=-1.0, scalar2=None, op0=ALU.mult),
              reads=["sinS"], writes=["sinS"])
        S.barrier()

    wb_rr = [0]

    def load_w(src2d, nk, ncols):
        k = wb_rr[0] % NWB
        wb_rr[0] += 1
        t = wbuf[k]
        rn = "wb%d" % k
        dma("pool", t[:, 0:nk, 0:ncols], src2d.rearrange("(kc p) c -> p kc c", p=128), [], [rn])
        return t, rn

    def wcols(tile0, ntiles):
        return win_d[:, tile0 * 128:(tile0 + ntiles) * 128]

    pj_rr = [0]
    pj_banks = [0, 1]

    def proj_fm(wt, wrn, c0, tb):
        bank = pj_banks[pj_rr[0] % len(pj_banks)]
        pj_rr[0] += 1

        def mm(e):
            ins = None
            for kc in range(KC):
                ins = e.matmul(ps[bank][:], lhsT=wt[:, kc, c0:c0 + 128], rhs=hT[:, kc, tb * 512:(tb + 1) * 512],
                               start=(kc == 0), stop=(kc == KC - 1))
            return ins
        S.add("pe", mm, reads=[wrn, "hT"], writes=[PS[bank]])
        return bank

    QT = [sb("QT%d" % g, [128, S_LEN], BF16, att) for g in range(3)]
    KT = [sb("KT%d" % g, [128, S_LEN], BF16, att) for g in range(3)]
    Vt = [sb("Vt%d" % g, [128, 16, 128], BF16, att) for g in range(3)]
    sza = sb("sza", [128, S_LEN], BF16, att)
    rt1 = [sb("rt1_%d" % i, [128, 512], F32, att) for i in range(2)]
    rt2 = [sb("rt2_%d" % i, [128, 512], F32, att) for i in range(2)]
    PT = [sb("PT%d" % i, [128, 256], BF16, att) for i in range(3)]
    rd = sb("rd", [128, 512], F32, att)
    ot = sb("ot", [128, 512], F32, att)
    ub = [sb("ub%d" % i, [128, 512], BF16, att) for i in range(2)]
    rot_rr = [0]
    st_rr = [0]
    ub_rr = [0]
    SCALE = 128.0 ** -0.5

    def rotary_evac(bank, dest, dest_rn, tb):
        k = rot_rr[0] % 2
        rot_rr[0] += 1
        t1, t2 = rt1[k], rt2[k]
        sl = slice(tb * 512, (tb + 1) * 512)
        S.add("dve", lambda e: e.tensor_tensor(out=t1[:], in0=ps[bank][:], in1=cosT[:, sl], op=ALU.mult),
              reads=[PS[bank], "cosT"], writes=["rt1_%d" % k])
        S.add("dve", lambda e: e.tensor_tensor(out=t2[0:64, :], in0=ps[bank][64:128, :], in1=sinS[64:128, sl], op=ALU.mult),
              reads=[PS[bank], "sinS"], writes=["rt2a_%d" % k])
        S.add("dve", lambda e: e.tensor_tensor(out=t2[64:128, :], in0=ps[bank][0:64, :], in1=sinS[0:64, sl], op=ALU.mult),
              reads=[PS[bank], "sinS"], writes=["rt2b_%d" % k])
        S.add("dve", lambda e: e.tensor_tensor(out=dest[:, sl], in0=t1[:], in1=t2[:], op=ALU.add),
              reads=["rt1_%d" % k, "rt2a_%d" % k, "rt2b_%d" % k], writes=[dest_rn])

    n_hh = 12 if stop_after != "att1" else 1
    for hh in range(n_hh):
        base = OFF_ATT + hh * 10
        w1, w1n = load_w(wcols(base, 4), KC, 512)
        w2, w2n = load_w(wcols(base + 4, 4), KC, 512)
        w3, w3n = load_w(wcols(base + 8, 2), KC, 256)
        qk = [(w1, w1n, 0, QT[0], "QT0"), (w1, w1n, 128, QT[1], "QT1"), (w1, w1n, 256, QT[2], "QT2"),
              (w1, w1n, 384, KT[0], "KT0"), (w2, w2n, 0, KT[1], "KT1"), (w2, w2n, 128, KT[2], "KT2")]
        def do_qk():
            for (wt, wrn, c0, dest, drn) in qk:
                for tb in range(4):
                    bank = proj_fm(wt, wrn, c0, tb)
                    rotary_evac(bank, dest, drn, tb)

        def do_v():
            for gi, (wt, wrn, c0) in enumerate(((w2, w2n, 256), (w2, w2n, 384), (w3, w3n, 0))):
                d = DILS[gi]
                nlb = 16 // d
                for q4 in range(4):
                    bank = 6 + (q4 % 2)

                    def vmm(e, q4=q4, bank=bank, d=d, nlb=nlb, wt=wt, c0=c0):
                        ins = None
                        for k4 in range(4):
                            tid = q4 * 4 + k4
                            r, lb = tid // nlb, tid % nlb
                            s0 = r + d * 128 * lb
                            for kc in range(KC):
                                ins = e.matmul(ps[bank][:, k4 * 128:(k4 + 1) * 128], lhsT=hT[:, kc, s0:s0 + d * 127 + 1:d],
                                               rhs=wt[:, kc, c0:c0 + 128], start=(kc == 0), stop=(kc == KC - 1),
                                               skip_group_check=True)
                        return ins
                    S.add("pe", vmm, reads=[wrn, "hT"], writes=[PS[bank]])
                    S.add("act", lambda e, gi=gi, q4=q4, bank=bank: e.copy(
                        out=Vt[gi][:, q4 * 4:(q4 + 1) * 4, :], in_=ps[bank][:].rearrange("p (a b) -> p a b", a=4)),
                        reads=[PS[bank]], writes=["Vt%d" % gi])

        def do_za():
            for tb in range(4):
                bank = proj_fm(w3, w3n, 128, tb)
                S.add("act", lambda e, bank=bank, tb=tb: e.activation(out=sza[:, tb * 512:(tb + 1) * 512], in_=ps[bank][:], func=AF.Silu),
                      reads=[PS[bank]], writes=["sza"])

        if hh == 0:
            do_v()
            do_za()
            do_qk()
        else:
            do_qk()
            do_v()
            do_za()
        for QB in range(4):
            first = True
            pend = None
            for gi, d in enumerate(DILS):
                nlb = 16 // d
                nq = 512 // d
                lq0 = QB * nq
                for r in range(d):
                    lb_lo = max(0, (lq0 - 64) // 128)
                    lb_hi = min(nlb - 1, (lq0 + nq - 1 + 64) // 128)
                    for lb in range(lb_lo, lb_hi + 1):
                        k0 = lb * 128
                        qs = max(lq0, k0 - 64)
                        qe = min(lq0 + nq, k0 + 128 + 64)
                        nqq = qe - qs
                        if nqq <= 0:
                            continue
                        off = qs - k0 + 64
                        qtok0 = r + d * qs
                        ktok0 = r + d * k0
                        oc0 = r + d * (qs - lq0)
                        stb = 2 + (st_rr[0] % 2)
                        pk = st_rr[0] % 3
                        st_rr[0] += 1
                        pt = PT[pk]

                        def smm(e, stb=stb, nqq=nqq, off=off, gi=gi, d=d, ktok0=ktok0, qtok0=qtok0):
                            e.matmul(ps[stb][:, 0:nqq], lhsT=ident_b[:], rhs=negm[:, off:off + nqq], start=True, stop=False)
                            return e.matmul(ps[stb][:, 0:nqq], lhsT=KT[gi][:, ktok0:ktok0 + d * 127 + 1:d],
                                            rhs=QT[gi][:, qtok0:qtok0 + d * (nqq - 1) + 1:d], start=False, stop=True)
                        S.add("pe", smm, reads=["KT%d" % gi, "QT%d" % gi, "negm", "ident_b"], writes=[PS[stb]])
                        S.add("act", lambda e, stb=stb, nqq=nqq, pt=pt: e.activation(out=pt[:, 0:nqq], in_=ps[stb][:, 0:nqq],
                                                                                      func=AF.Exp, scale=SCALE),
                              reads=[PS[stb]], writes=["PT%d" % pk])

                        def pv(e, first=first, oc0=oc0, d=d, nqq=nqq, gi=gi, tid=r * nlb + lb, pt=pt):
                            osl = slice(oc0, oc0 + d * (nqq - 1) + 1, d)
                            e.matmul(ps[4][:, osl], lhsT=Vt[gi][:, tid, :], rhs=pt[:, 0:nqq], start=first, stop=False,
                                     skip_group_check=True)
                            return e.matmul(ps[5][:, osl], lhsT=ones_b[:], rhs=pt[:, 0:nqq], start=first, stop=False,
                                            skip_group_check=True)
                        if pend:
                            S.add("pe", pend[0], reads=pend[1], writes=[PS[4], PS[5]])
                        pend = (pv, ["Vt%d" % gi, "PT%d" % pk, "ones_b", PS[4], PS[5]])
                        first = False
            if pend:
                S.add("pe", pend[0], reads=pend[1], writes=[PS[4], PS[5]])
            uk = ub_rr[0] % 2
            ub_rr[0] += 1
            sl = slice(QB * 512, (QB + 1) * 512)
            S.add("dve", lambda e: e.reciprocal(out=rd[:], in_=ps[5][:]), reads=[PS[5]], writes=["rd"])
            S.add("dve", lambda e: e.tensor_tensor(out=ot[:], in0=ps[4][:], in1=rd[:], op=ALU.mult), reads=[PS[4], "rd"], writes=["ot"])
            S.add("dve", lambda e, uk=uk, sl=sl: e.tensor_tensor(out=ub[uk][:], in0=ot[:], in1=sza[:, sl], op=ALU.mult),
                  reads=["ot", "sza"], writes=["ub%d" % uk])
            dma("sp", ua_scr[hh, :, sl], ub[uk][:], ["ub%d" % uk], ["ua_scr"])

    if "att" in dbg:
        dbg_out["ua"] = dram_out("dbg_ua", [12, 128, S_LEN], BF16)
        dma("sp", dbg_out["ua"], ua_scr, ["ua_scr"], ["dbg_ua"])
        dbg_out["QT0"] = dram_out("dbg_QT0", [128, S_LEN], BF16)
        dma("sp", dbg_out["QT0"], QT[0][:], ["QT0"], ["dbg_QT0"])
    print("sbuf remaining after attention alloc:", nc.sbuf_bytes_remaining)
    if stop_after in ("att", "att1"):
        S.emit()
        return nc, dbg_out
    S.barrier()
    att.close()

    xbc_scr = dram_tmp("xbc_scr", [48, 128, S_LEN], BF16)
    sz_scr = dram_tmp("sz_scr", [S_LEN, 4096], BF16)
    sg_scr = dram_tmp("sg_scr", [32, 128, S_LEN], BF16)
    us_scr = dram_tmp("us_scr", [32, 128, S_LEN], BF16)

    for nm, shp, src_d in (("conv_wT", [128, 48, 5], convw_d), ("conv_bT", [128, 48], convb_d), ("dt_biasT", [128, 1], dtb_d),
                           ("a_logT", [128, 1], alog_d), ("ssd_norm_gT", [128, 32], ssdg_d)):
        dma("sp", small[nm][:], src_d, [], [nm])
    dma("sp", dskip_row[:], dskip_d.partition_broadcast(128), [], ["dskip_row"])

    pj_banks[:] = [0, 1, 2, 3]
    with ExitStack() as st3:
        dtT = sb("dtT", [128, S_LEN], F32, st3)
        aT = sb("aT", [128, S_LEN], F32, st3)
        Aneg = sb("Aneg", [128, 1], F32, st3)
        stage = [sb("stage%d" % i, [128, S_LEN + 4], F32, st3) for i in range(2)]
        acc = [sb("acc%d" % i, [128, S_LEN], F32, st3) for i in range(2)]
        cvT = [sb("cvT%d" % i, [128, S_LEN], BF16, st3) for i in range(2)]
        szb = [sb("szb%d" % i, [128, 512], BF16, st3) for i in range(2)]
        for i in range(2):
            S.add("dve", lambda e, i=i: e.memset(stage[i][:, 0:2], 0.0), writes=["stage%d" % i])
            S.add("dve", lambda e, i=i: e.memset(stage[i][:, S_LEN + 2:S_LEN + 4], 0.0), writes=["stage%d" % i])
        wd, wdn = load_w(wcols(OFF_DT, 1), KC, 128)
        for tb in range(4):
            bank = proj_fm(wd, wdn, 0, tb)
            sl = slice(tb * 512, (tb + 1) * 512)
            S.add("act", lambda e, bank=bank, sl=sl: e.activation(out=dtT[:, sl], in_=ps[bank][:], func=AF.Exp,
                                                                  bias=small["dt_biasT"][:, 0:1]),
                  reads=[PS[bank], "dt_biasT"], writes=["dtT"])
        S.add("act", lambda e: e.activation(out=dtT[:], in_=dtT[:], func=AF.Ln, bias=1.0), reads=["dtT"], writes=["dtT"])
        S.add("act", lambda e: e.activation(out=Aneg[:], in_=small["a_logT"][:], func=AF.Exp), reads=["a_logT"], writes=["Aneg"])
        S.add("dve", lambda e: e.tensor_scalar(out=aT[:], in0=dtT[:], scalar1=Aneg[:, 0:1], scalar2=-1.0, op0=ALU.mult, op1=ALU.mult),
              reads=["dtT", "Aneg"], writes=["aT"])
        for i in range(NT):
            bank = 6 + (i % 2)

            def tr2(e, i=i, bank=bank):
                e.transpose(out=ps[bank][:, 0:128], in_=dtT[:, i * 128:(i + 1) * 128], identity=ident_f[:])
                return e.transpose(out=ps[bank][:, 128:256], in_=aT[:, i * 128:(i + 1) * 128], identity=ident_f[:])
            S.add("pe", tr2, reads=["dtT", "aT", "ident_f"], writes=[PS[bank]])
            S.add("act", lambda e, i=i, bank=bank: e.copy(out=dt_tok[:, i, :], in_=ps[bank][:, 0:128]), reads=[PS[bank]], writes=["dt_tok"])
            S.add("act", lambda e, i=i, bank=bank: e.copy(out=a_tok[:, i, :], in_=ps[bank][:, 128:256]), reads=[PS[bank]], writes=["a_tok"])

        cv_rr = [0]

        def conv_tile(wt, wrn, c0, cc):
            k = cv_rr[0] % 2
            cv_rr[0] += 1
            stg, ac, cv = stage[k], acc[k], cvT[k]
            for tb in range(4):
                bank = proj_fm(wt, wrn, c0, tb)
                sl = slice(tb * 512, (tb + 1) * 512)
                S.add("act", lambda e, bank=bank, tb=tb, stg=stg: e.copy(out=stg[:, 2 + tb * 512:2 + (tb + 1) * 512], in_=ps[bank][:]),
                      reads=[PS[bank]], writes=["stage%d" % k])
                S.add("act", lambda e, bank=bank, sl=sl, ac=ac: e.activation(out=ac[:, sl], in_=ps[bank][:], func=AF.Identity,
                                                                             scale=small["conv_wT"][:, cc, 2:3],
                                                                             bias=small["conv_bT"][:, cc:cc + 1]),
                      reads=[PS[bank], "conv_wT", "conv_bT"], writes=["acc%d" % k])
            for tap in (0, 1, 3, 4):
                S.add("dve", lambda e, tap=tap, stg=stg, ac=ac: e.scalar_tensor_tensor(
                    out=ac[:], in0=stg[:, tap:tap + S_LEN], scalar=small["conv_wT"][:, cc, tap:tap + 1], in1=ac[:],
                    op0=ALU.mult, op1=ALU.add),
                    reads=["stage%d" % k, "acc%d" % k, "conv_wT"], writes=["acc%d" % k])
            S.add("act", lambda e, ac=ac, cv=cv: e.activation(out=cv[:], in_=ac[:], func=AF.Silu), reads=["acc%d" % k], writes=["cvT%d" % k])
            dma("sp", xbc_scr[cc], cv[:], ["cvT%d" % k], ["xbc_scr"])

        sz_rr = [0]
        n_g = 8
        for g in range(n_g):
            base = OFF_SSD + g * 10
            wbc, wbcn = load_w(wcols(base, 2), KC, 256)
            conv_tile(wbc, wbcn, 0, 32 + g)
            conv_tile(wbc, wbcn, 128, 40 + g)
            wx, wxn = load_w(wcols(base + 2, 4), KC, 512)
            for j in range(4):
                conv_tile(wx, wxn, j * 128, 4 * g + j)
            wz, wzn = load_w(wcols(base + 6, 4), KC, 512)
            for i in range(NT):
                bank = pj_banks[pj_rr[0] % len(pj_banks)]
                pj_rr[0] += 1

                def zmm(e, i=i, bank=bank, wz=wz):
                    ins = None
                    for kc in range(KC):
                        ins = e.matmul(ps[bank][:], lhsT=hT[:, kc, i * 128:(i + 1) * 128], rhs=wz[:, kc, 0:512],
                                       start=(kc == 0), stop=(kc == KC - 1))
                    return ins
                S.add("pe", zmm, reads=[wzn, "hT"], writes=[PS[bank]])
                k = sz_rr[0] % 2
                sz_rr[0] += 1
                S.add("act", lambda e, bank=bank, k=k: e.activation(out=szb[k][:], in_=ps[bank][:], func=AF.Silu),
                      reads=[PS[bank]], writes=["szb%d" % k])
                dma("sp", sz_scr[i * 128:(i + 1) * 128, g * 512:(g + 1) * 512], szb[k][:], ["szb%d" % k], ["sz_scr"])
        for gb in range(8):
            wg, wgn = load_w(wcols(OFF_GA + gb * 4, 4), KC, 512)
            for j4 in range(4):
                j = gb * 4 + j4
                k = cv_rr[0] % 2
                cv_rr[0] += 1
                for tb in range(4):
                    bank = proj_fm(wg, wgn, j4 * 128, tb)
                    S.add("act", lambda e, bank=bank, tb=tb, k=k: e.activation(out=cvT[k][:, tb * 512:(tb + 1) * 512], in_=ps[bank][:],
                                                                                  func=AF.Sigmoid),
                          reads=[PS[bank]], writes=["cvT%d" % k])
                dma("sp", sg_scr[j], cvT[k][:], ["cvT%d" % k], ["sg_scr"])
        S.barrier()
    hst.close()
    print("sbuf remaining after proj phase:", nc.sbuf_bytes_remaining)
    if stop_after == "proj":
        S.emit()
        return nc, dbg_out

    S.add("dve", lambda e: e.memset(ssq[:], 0.0), writes=["ssq_ssd"])
    with ExitStack() as st4:
        triM = sb("triM", [128, 128], F32, st4)
        ustM = sb("ustM", [128, 128], F32, st4)
        geM = sb("geM", [128, 128], F32, st4)
        lstM = sb("lstM", [128, 128], F32, st4)
        for (m, nm, pat, base, cm, op) in ((triM, "triM", 1, 0, -1, ALU.is_ge), (ustM, "ustM", -1, 0, 1, ALU.is_gt),
                                           (geM, "geM", -1, 0, 1, ALU.is_ge), (lstM, "lstM", 1, 0, -1, ALU.is_gt)):
            S.add("pool", lambda e, m=m: e.memset(m[:], 1.0), writes=[nm])
            S.add("pool", lambda e, m=m, pat=pat, base=base, cm=cm, op=op: e.affine_select(
                out=m[:], in_=m[:], pattern=[[pat, 128]], compare_op=op, fill=0.0, base=base, channel_multiplier=cm),
                reads=[nm], writes=[nm])
        BT = sb("BT", [128, S_LEN], BF16, st4)
        CT = sb("CT", [128, S_LEN], BF16, st4)
        xsT = [sb("xsT%d" % j, [128, S_LEN], BF16, st4) for j in range(4)]
        sz_tok = sb("sz_tok", [128, NT, 512], BF16, st4)
        B_tok = sb("B_tok", [128, NT, 128], BF16, st4)
        xdt = [sb("xdt%d" % d_, [128, NT, 512], BF16, st4) for d_ in range(2)]
        yacc = sb("yacc", [128, NT, 512], F32, st4)
        est = sb("est", [128, NT, 48], F32, st4)
        rhsD = [sb("rhsD0", [128, 8, 128], F32, st4)] * 2
        Ebuf = [sb("Ebuf0", [128, 8, 128], F32, st4)] * 2
        mcb = [sb("mcb%d" % i, [128, 128], F32, st4) for i in range(2)]
        MT = [sb("MT0", [128, 8, 128], BF16, st4)] * 2
        xdec = [sb("xdec0", [128, 512], BF16, st4)] * 2
        carry = [sb("carry%d" % d_, [128, 512], F32, st4) for d_ in range(2)]
        prevb = [sb("prevb%d" % d_, [128, 512], BF16, st4) for d_ in range(2)]
        ytmp = [sb("ytmp0", [128, 512], F32, st4)] * 2
        vtmp = sb("vtmp", [128, 512], F32, st4)
        vjunk = sb("vjunk", [128, 512], F32, st4)
        s1 = sb("s1", [128, 1], F32, st4)
        vb = [sb("vb0", [128, 512], BF16, st4)] * 2
        usT = xsT
        rr = {"d": 0, "y": 0, "v": 0}
        wv = wbuf[0]
        f32v = lambda ap: ap.bitcast(F32).rearrange("p a (b c) -> p (a b) c", c=128)
        Ebuf2 = [Ebuf[0][:], f32v(wv[:, 0:4, :])]
        rhsD2 = [rhsD[0][:], f32v(wv[:, 4:8, :])]
        ytmp2 = [ytmp[0][:], wv[:, 8:10, :].bitcast(F32).rearrange("p a b -> p (a b)")]
        MT2 = [MT[0][:], wv[:, 10:12, :].rearrange("p a (b c) -> p (a b) c", c=128)]
        xdec2 = [xdec[0][:], wv[:, 12, :]]

        def bc_p(ap2):
            return ap2.unsqueeze(2).to_broadcast([128, 8, 64])

        def v3(ap2):
            return ap2.rearrange("p (e q) -> p e q", e=8)

        wscr = dram_tmp("wscr", [16, 128, KC, 512], BF16)
        slabs = []
        for jb_ in range(4):
            slabs.append((wbra_d[:, jb_ * 512:(jb_ + 1) * 512], 12))
            slabs.append((wbrs_d[0:2048, jb_ * 512:(jb_ + 1) * 512], 16))
            slabs.append((wbrs_d[2048:4096, jb_ * 512:(jb_ + 1) * 512], 16))
        for ob_ in range(4):
            slabs.append((wout_d[:, ob_ * 512:(ob_ + 1) * 512], 16))

        def precast(si_, k_):
            src2d, nk_ = slabs[si_]
            t_ = wbuf[k_]
            rn_ = "wb%d" % k_
            dma("pool", t_[:, 0:nk_, :], src2d.rearrange("(kc p) c -> p kc c", p=128), [], [rn_])
            dma("sp", wscr[si_, :, 0:nk_, :], t_[:, 0:nk_, :], [rn_], ["wscr"])

        for g in range(n_g):
            precast(2 * g, 1)
            precast(2 * g + 1, 2)
            dma("sp", BT[:], xbc_scr[32 + g], ["xbc_scr"], ["BT"])
            dma("sp", CT[:], xbc_scr[40 + g], ["xbc_scr"], ["CT"])
            for j in range(4):
                dma("sp", xsT[j][:], xbc_scr[4 * g + j], ["xbc_scr"], ["xsT%d" % j])
            dma("sp", sz_tok[:], sz_scr[:, g * 512:(g + 1) * 512].rearrange("(i p) c -> p i c", p=128), ["sz_scr"], ["sz_tok"])
            fcol = slice(g * 8, g * 8 + 8)
            bcol = slice(64 + g * 8, 64 + g * 8 + 8)
            for c in range(NT):
                csl = slice(c * 128, (c + 1) * 128)
                pb7 = ps[7][:].bitcast(BF16)

                def trx(e, csl=csl, pb7=pb7):
                    ins = None
                    for j in range(4):
                        ins = e.transpose(out=pb7[:, j * 128:(j + 1) * 128], in_=xsT[j][:, csl], identity=ident_b[:])
                    return e.transpose(out=pb7[:, 512:640], in_=BT[:, csl], identity=ident_b[:])
                S.add("pe", trx, reads=["xsT0", "xsT1", "xsT2", "xsT3", "BT", "ident_b"], writes=[PS[7]])
                S.add("dve", lambda e, c=c, pb7=pb7: e.tensor_copy(out=B_tok[:, c, :], in_=pb7[:, 512:640]), reads=[PS[7]], writes=["B_tok"])
                S.add("dve", lambda e, c=c, pb7=pb7, fcol=fcol: e.tensor_tensor(out=v3(xdt[0][:, c, :]), in0=v3(pb7[:, 0:512]),
                                                                     in1=bc_p(dt_tok[:, c, fcol]), op=ALU.mult),
                      reads=[PS[7], "dt_tok"], writes=["xdt0"])
                S.add("dve", lambda e, c=c, pb7=pb7, bcol=bcol: e.tensor_tensor(out=v3(xdt[1][:, c, :]), in0=v3(pb7[:, 0:512]),
                                                                     in1=bc_p(dt_tok[:, c, bcol]), op=ALU.mult),
                      reads=[PS[7], "dt_tok"], writes=["xdt1"])
                S.add("dve", lambda e, c=c, pb7=pb7, fcol=fcol: e.tensor_tensor(out=v3(yacc[:, c, :]), in0=v3(pb7[:, 0:512]),
                                                                     in1=bc_p(dskip_row[:, fcol]), op=ALU.mult),
                      reads=[PS[7], "dskip_row"], writes=["yacc%d" % c])

                def stat(e, c=c, fcol=fcol, bcol=bcol):
                    af = a_tok[:, c, fcol]
                    ab = a_tok[:, c, bcol]
                    e.matmul(ps[6][:, 0:8], lhsT=triM[:], rhs=af, start=True, stop=True, skip_group_check=True)
                    e.matmul(ps[6][:, 8:16], lhsT=ustM[:], rhs=af, start=True, stop=True, skip_group_check=True)
                    e.matmul(ps[6][:, 16:24], lhsT=geM[:], rhs=ab, start=True, stop=True, skip_group_check=True)
                    e.matmul(ps[6][:, 24:32], lhsT=lstM[:], rhs=ab, start=True, stop=True, skip_group_check=True)
                    e.matmul(ps[6][:, 32:40], lhsT=ones_f[:], rhs=af, start=True, stop=True, skip_group_check=True)
                    return e.matmul(ps[6][:, 40:48], lhsT=ones_f[:], rhs=ab, start=True, stop=True, skip_group_check=True)
                S.add("pe", stat, reads=["a_tok", "triM", "ustM", "geM", "lstM", "ones_f"], writes=[PS[6]])
                S.add("act", lambda e, c=c: e.activation(out=est[:, c, :], in_=ps[6][:, 0:48], func=AF.Exp), reads=[PS[6]], writes=["est"])

            steps = [(ci, dr) for ci in range(NT) for dr in range(2)]

            def stage_r(si, ci, dr):
                c = ci if dr == 0 else NT - 1 - ci
                acol = fcol if dr == 0 else bcol
                maskR = triM if dr == 0 else geM
                k = si % 2
                rD = rhsD2[k]
                def mk_rhs(e_):
                    ins = None
                    for h8 in range(8):
                        ins = e_.activation(out=rD[:, h8, :], in_=maskR[:], func=AF.Copy,
                                            scale=a_tok[:, c, acol.start + h8:acol.start + h8 + 1])
                    return ins
                S.add("act", mk_rhs, reads=["a_tok", "triM", "geM"], writes=["rhsD%d" % k])

            def stage_a(si, ci, dr):
                c = ci if dr == 0 else NT - 1 - ci
                csl = slice(c * 128, (c + 1) * 128)
                maskR = triM if dr == 0 else geM
                maskL = ustM if dr == 0 else lstM
                k = si % 2
                rD, Eb, Mt, mc = rhsD2[k], Ebuf2[k], MT2[k], mcb[k]
                if ci < NT - 1:
                    eo_ = 0 if dr == 0 else 16
                    xd_ = xdec2[k]
                    S.add("dve", lambda e: e.tensor_tensor(out=v3(xd_), in0=v3(xdt[dr][:, c, :]), in1=bc_p(est[:, c, eo_ + 8:eo_ + 16]), op=ALU.mult),
                          reads=["xdt%d" % dr, "est"], writes=["xdec%d" % k])

                def dmm(e):
                    e.matmul(ps[2][:], lhsT=maskL[:], rhs=rD[:, 0:4, :], start=True, stop=True)
                    return e.matmul(ps[3][:], lhsT=maskL[:], rhs=rD[:, 4:8, :], start=True, stop=True)
                S.add("pe", dmm, reads=["rhsD%d" % k, "ustM", "lstM"], writes=[PS[2], PS[3]])
                S.add("act", lambda e: e.activation(out=Eb[:, 0:4, :], in_=ps[2][:].rearrange("p (e l) -> p e l", e=4), func=AF.Exp),
                      reads=[PS[2]], writes=["Ebuf%d" % k])
                S.add("act", lambda e: e.activation(out=Eb[:, 4:8, :], in_=ps[3][:].rearrange("p (e l) -> p e l", e=4), func=AF.Exp),
                      reads=[PS[3]], writes=["Ebuf%d" % k])
                S.add("pe", lambda e: e.matmul(ps[5][:, 0:128], lhsT=BT[:, csl], rhs=CT[:, csl], start=True, stop=True),
                      reads=["BT", "CT"], writes=[PS[5]])
                S.add("dve", lambda e: e.tensor_tensor(out=mc[:], in0=ps[5][:, 0:128], in1=maskR[:], op=ALU.mult),
                      reads=[PS[5], "triM", "geM"], writes=["mcb%d" % k])
                S.add("dve", lambda e: e.tensor_tensor(out=Mt, in0=Eb, in1=mc[:].unsqueeze(1).to_broadcast([128, 8, 128]), op=ALU.mult),
                      reads=["Ebuf%d" % k, "mcb%d" % k], writes=["MT%d" % k])

            def stage_b(si, ci, dr):
                c = ci if dr == 0 else NT - 1 - ci
                csl = slice(c * 128, (c + 1) * 128)
                eo = 0 if dr == 0 else 16
                cdo = 32 if dr == 0 else 40
                k = si % 2
                Mt, xd, yt = MT2[k], xdec2[k], ytmp2[k]

                def ydiag(e):
                    ins = None
                    for h8 in range(8):
                        ins = e.matmul(ps[4][:, h8 * 64:(h8 + 1) * 64], lhsT=Mt[:, h8, :], rhs=xdt[dr][:, c, h8 * 64:(h8 + 1) * 64],
                                       start=True, stop=True, skip_group_check=True)
                    return ins
                S.add("pe", ydiag, reads=["MT%d" % k, "xdt%d" % dr], writes=[PS[4]])
                S.add("dve", lambda e: e.tensor_tensor(out=yacc[:, c, :], in0=yacc[:, c, :], in1=ps[4][:], op=ALU.add),
                      reads=[PS[4], "yacc%d" % c], writes=["yacc%d" % c])
                if ci > 0:
                    S.add("pe", lambda e: e.matmul(ps[1][:], lhsT=CT[:, csl], rhs=prevb[dr][:], start=True, stop=True),
                          reads=["CT", "prevb%d" % dr], writes=[PS[1]])
                    S.add("dve", lambda e: e.tensor_tensor(out=v3(yt), in0=v3(ps[1][:]), in1=bc_p(est[:, c, eo:eo + 8]), op=ALU.mult),
                          reads=[PS[1], "est"], writes=["ytmp%d" % k])
                    S.add("dve", lambda e: e.tensor_tensor(out=yacc[:, c, :], in0=yacc[:, c, :], in1=yt, op=ALU.add),
                          reads=["ytmp%d" % k, "yacc%d" % c], writes=["yacc%d" % c])
                if ci < NT - 1:
                    S.add("pe", lambda e: e.matmul(ps[0][:], lhsT=B_tok[:, c, :], rhs=xd, start=True, stop=True),
                          reads=["B_tok", "xdec%d" % k], writes=[PS[0]])
                    if ci == 0:
                        S.add("dve", lambda e: e.tensor_copy(out=carry[dr][:], in_=ps[0][:]), reads=[PS[0]], writes=["carry%d" % dr])
                    else:
                        S.add("dve", lambda e: e.tensor_tensor(out=v3(carry[dr][:]), in0=v3(carry[dr][:]), in1=bc_p(est[:, c, cdo:cdo + 8]), op=ALU.mult),
                              reads=["carry%d" % dr, "est"], writes=["carry%d" % dr])
                        S.add("dve", lambda e: e.tensor_tensor(out=carry[dr][:], in0=carry[dr][:], in1=ps[0][:], op=ALU.add),
                              reads=["carry%d" % dr, PS[0]], writes=["carry%d" % dr])
                    S.add("pool", lambda e: e.tensor_copy(out=prevb[dr][:], in_=carry[dr][:]), reads=["carry%d" % dr], writes=["prevb%d" % dr])

            stage_r(0, *steps[0])
            for si, (ci, dr) in enumerate(steps):
                if si + 1 < len(steps):
                    stage_r(si + 1, *steps[si + 1])
                stage_a(si, ci, dr)
                if si >= 1:
                    stage_b(si - 1, *steps[si - 1])
            stage_b(len(steps) - 1, *steps[-1])
            if "scan" in dbg and g == 0:
                for nm, t, shp, dt_ in (("yacc", yacc, [128, NT, 512], F32), ("xdt0", xdt[0], [128, NT, 512], BF16),
                                       ("xdt1", xdt[1], [128, NT, 512], BF16), ("B_tok", B_tok, [128, NT, 128], BF16),
                                       ("est", est, [128, NT, 48], F32), ("sz_tok", sz_tok, [128, NT, 512], BF16)):
                    dbg_out[nm] = dram_out("dbg_" + nm, shp, dt_)
                    rds = ["yacc%d" % c_ for c_ in range(NT)] if nm == "yacc" else [nm]
                    dma("sp", dbg_out[nm], t[:], rds, ["dbg_" + nm])
            for c in range(NT):
                vk = rr["v"] % 2
                rr["v"] += 1
                S.add("dve", lambda e, c=c: e.tensor_tensor(out=vtmp[:], in0=yacc[:, c, :], in1=sz_tok[:, c, :], op=ALU.mult),
                      reads=["yacc%d" % c, "sz_tok"], writes=["vtmp"])
                S.add("act", lambda e: e.activation(out=vjunk[:], in_=vtmp[:], func=AF.Square, accum_out=s1[:, 0:1]),
                      reads=["vtmp"], writes=["vjunk", "s1"])
                S.add("dve", lambda e, c=c: e.tensor_tensor(out=ssq[:, c:c + 1], in0=ssq[:, c:c + 1], in1=s1[:, 0:1], op=ALU.add),
                      reads=["s1", "ssq_ssd"], writes=["ssq_ssd"])
                S.add("act", lambda e, vk=vk: e.copy(out=vb[vk][:], in_=vtmp[:]), reads=["vtmp"], writes=["vb0"])
                pb7 = ps[7][:].bitcast(BF16)

                def trv(e, vk=vk, pb7=pb7):
                    ins = None
                    for j in range(4):
                        ins = e.transpose(out=pb7[:, j * 128:(j + 1) * 128], in_=vb[vk][:, j * 128:(j + 1) * 128], identity=ident_b[:])
                    return ins
                S.add("pe", trv, reads=["vb0", "ident_b"], writes=[PS[7]])
                for j in range(4):
                    S.add("act", lambda e, j=j, c=c, pb7=pb7, g=g: e.activation(out=usT[j][:, c * 128:(c + 1) * 128], in_=pb7[:, j * 128:(j + 1) * 128],
                                                                            func=AF.Identity, scale=small["ssd_norm_gT"][:, 4 * g + j:4 * g + j + 1]),
                          reads=[PS[7], "ssd_norm_gT"], writes=["xsT%d" % j])
            for j in range(4):
                dma("sp", us_scr[4 * g + j], usT[j][:], ["xsT%d" % j], ["us_scr"])
        S.barrier()

    if "ssd" in dbg:
        dbg_out["us"] = dram_out("dbg_us", [32, 128, S_LEN], BF16)
        dma("sp", dbg_out["us"], us_scr, ["us_scr"], ["dbg_us"])
        dbg_out["dt_tok"] = dram_out("dbg_dt_tok", [128, NT, 128])
        dma("sp", dbg_out["dt_tok"], dt_tok[:], ["dt_tok"], ["dbg_dt_tok"])
        dbg_out["a_tok"] = dram_out("dbg_a_tok", [128, NT, 128])
        dma("sp", dbg_out["a_tok"], a_tok[:], ["a_tok"], ["dbg_a_tok"])
        dbg_out["xbc"] = dram_out("dbg_xbc", [48, 128, S_LEN], BF16)
        dma("sp", dbg_out["xbc"], xbc_scr, ["xbc_scr"], ["dbg_xbc"])
        dbg_out["ssq"] = dram_out("dbg_ssq", [128, NT])
        dma("sp", dbg_out["ssq"], ssq[:], ["ssq_ssd"], ["dbg_ssq"])
    if stop_after == "ssd":
        S.emit()
        return nc, dbg_out

    with ExitStack() as st5:
        gate_row = sb("gate_row", [128, D], F32, st5)
        fing_row = sb("fing_row", [128, D], F32, st5)
        dma("sp", fing_row[:], fing_d.partition_broadcast(128), [], ["fing_row"])
        dg = sb("dg", [128, 128], F32, st5)
        for kc in range(KC):
            S.add("dve", lambda e, kc=kc: e.tensor_scalar(out=dg[:], in0=ident_f[:], scalar1=adaT[:, 32 + kc:33 + kc],
                                                         scalar2=None, op0=ALU.mult),
                  reads=["ident_f", "adaT"], writes=["dg"])
            S.add("pe", lambda e, kc=kc: e.matmul(ps[6][:, 0:128], lhsT=ones_f[:], rhs=dg[:], start=True, stop=True),
                  reads=["ones_f", "dg"], writes=[PS[6]])
            S.add("act", lambda e, kc=kc: e.copy(out=gate_row[:, kc * 128:(kc + 1) * 128], in_=ps[6][:, 0:128]),
                  reads=[PS[6]], writes=["gate_row"])
        rstdS = sb("rstdS", [128, NT], F32, st5)
        rstdB = sb("rstdB", [128, S_LEN], F32, st5)
        dg2 = sb("dg2", [128, 128], F32, st5)
        S.add("dve", lambda e: e.tensor_scalar(out=rstdS[:], in0=ssq[:], scalar1=1.0 / 4096, scalar2=EPS, op0=ALU.mult, op1=ALU.add),
              reads=["ssq_ssd"], writes=["rstdS"])
        S.add("act", lambda e: e.sqrt(out=rstdS[:], in_=rstdS[:]), reads=["rstdS"], writes=["rstdS"])
        S.add("dve", lambda e: e.reciprocal(out=rstdS[:], in_=rstdS[:]), reads=["rstdS"], writes=["rstdS"])
        for i in range(NT):
            S.add("dve", lambda e, i=i: e.tensor_scalar(out=dg2[:], in0=ident_f[:], scalar1=rstdS[:, i:i + 1], scalar2=None, op0=ALU.mult),
                  reads=["ident_f", "rstdS"], writes=["dg2"])
            S.add("pe", lambda e: e.matmul(ps[6][:, 0:128], lhsT=ones_f[:], rhs=dg2[:], start=True, stop=True),
                  reads=["ones_f", "dg2"], writes=[PS[6]])
            S.add("act", lambda e, i=i: e.copy(out=rstdB[:, i * 128:(i + 1) * 128], in_=ps[6][:, 0:128]), reads=[PS[6]], writes=["rstdB"])
        ua_tb = sb("ua_tb", [128, 12, 512], BF16, st5)
        us_tb = sb("us_tb", [128, 32, 512], BF16, st5)
        sga = [sb("sga%d" % i, [128, 512], BF16, st5) for i in range(2)]
        sgs = [sb("sgs%d" % i, [128, 512], BF16, st5) for i in range(2)]
        mT = sb("mT", [128, 16, 512], BF16, st5)
        m1 = sb("m1", [128, 512], F32, st5)
        m2 = sb("m2", [128, 512], F32, st5)
        xo = [sb("xo%d" % i, [128, D], F32, st5) for i in range(4)]
        fj = sb("fj", [128, D], F32, st5)
        fs = sb("fs", [128, 4], F32, st5)
        ft = sb("ft", [128, 512], F32, st5)
        g_rr = [0]

        def load_ws(si_, nk_):
            k_ = wb_rr[0] % NWB
            wb_rr[0] += 1
            t_ = wbuf[k_]
            rn_ = "wb%d" % k_
            dma("act", t_[:, 0:nk_, :], wscr[si_, :, 0:nk_, :], ["wscr"], [rn_])
            return t_, rn_

        for tb in range(4):
            tsl = slice(tb * 512, (tb + 1) * 512)
            dma("sp", ua_tb[:], ua_scr[:, :, tsl].rearrange("c p t -> p c t"), ["ua_scr"], ["ua_tb"])
            dma("sp", us_tb[:], us_scr[:, :, tsl].rearrange("c p t -> p c t"), ["us_scr"], ["us_tb"])
            for tt in range(4):
                dma("sp", xo[tt][:], x_d[tb * 512 + tt * 128:tb * 512 + (tt + 1) * 128, :], [], ["xo%d" % tt])
            for jb in range(4):
                wa_, wan = load_ws(3 * jb, 12)
                ws0, ws0n = load_ws(3 * jb + 1, 16)
                ws1, ws1n = load_ws(3 * jb + 2, 16)
                for j4 in range(4):
                    j = jb * 4 + j4
                    csl = slice(j4 * 128, (j4 + 1) * 128)
                    gk = g_rr[0] % 2
                    g_rr[0] += 1
                    dma("sp", sga[gk][:], sg_scr[j, :, tsl], ["sg_scr"], ["sga%d" % gk])
                    dma("sp", sgs[gk][:], sg_scr[16 + j, :, tsl], ["sg_scr"], ["sgs%d" % gk])

                    def ya(e, csl=csl, wa_=wa_):
                        ins = None
                        for cc in range(12):
                            ins = e.matmul(ps[0][:], lhsT=wa_[:, cc, csl], rhs=ua_tb[:, cc, :], start=(cc == 0), stop=(cc == 11))
                        return ins
                    S.add("pe", ya, reads=[wan, "ua_tb"], writes=[PS[0]])

                    def ys(e, csl=csl, ws0=ws0, ws1=ws1):
                        ins = None
                        for cc in range(32):
                            w_ = ws0 if cc < 16 else ws1
                            ins = e.matmul(ps[1][:], lhsT=w_[:, cc % 16, csl], rhs=us_tb[:, cc, :], start=(cc == 0), stop=(cc == 31))
                        return ins
                    S.add("pe", ys, reads=[ws0n, ws1n, "us_tb"], writes=[PS[1]])
                    S.add("dve", lambda e, gk=gk: e.tensor_tensor(out=m1[:], in0=ps[0][:], in1=sga[gk][:], op=ALU.mult),
                          reads=[PS[0], "sga%d" % gk], writes=["m1"])
                    S.add("dve", lambda e, tsl=tsl: e.tensor_tensor(out=m2[:], in0=ps[1][:], in1=rstdB[:, tsl], op=ALU.mult),
                          reads=[PS[1], "rstdB"], writes=["m2"])
                    S.add("dve", lambda e, gk=gk: e.tensor_tensor(out=m2[:], in0=m2[:], in1=sgs[gk][:], op=ALU.mult),
                          reads=["m2", "sgs%d" % gk], writes=["m2"])
                    S.add("dve", lambda e, j=j: e.tensor_tensor(out=mT[:, j, :], in0=m1[:], in1=m2[:], op=ALU.add),
                          reads=["m1", "m2"], writes=["mT"])
            for ob in range(4):
                wo, won = load_ws(12 + ob, 16)
                osl = slice(ob * 512, (ob + 1) * 512)
                for tt in range(4):
                    bank = 2 + (tt % 2)

                    def om(e, tt=tt, bank=bank, wo=wo):
                        ins = None
                        for j in range(16):
                            ins = e.matmul(ps[bank][:], lhsT=mT[:, j, tt * 128:(tt + 1) * 128], rhs=wo[:, j, :], start=(j == 0), stop=(j == 15))
                        return ins
                    S.add("pe", om, reads=[won, "mT"], writes=[PS[bank]])
                    S.add("dve", lambda e, bank=bank, osl=osl: e.tensor_tensor(out=ft[:], in0=ps[bank][:], in1=gate_row[:, osl], op=ALU.mult),
                          reads=[PS[bank], "gate_row"], writes=["ft"])
                    S.add("dve", lambda e, tt=tt, osl=osl: e.tensor_tensor(out=xo[tt][:, osl], in0=xo[tt][:, osl], in1=ft[:], op=ALU.add),
                          reads=["ft", "xo%d" % tt], writes=["xo%d" % tt])
            for tt in range(4):
                S.add("act", lambda e, tt=tt: e.activation(out=fj[:], in_=xo[tt][:], func=AF.Square, accum_out=fs[:, tt:tt + 1]),
                      reads=["xo%d" % tt], writes=["fj", "fs%d" % tt])
                S.add("dve", lambda e, tt=tt: e.tensor_scalar(out=fs[:, tt:tt + 1], in0=fs[:, tt:tt + 1], scalar1=1.0 / D, scalar2=EPS,
                                                             op0=ALU.mult, op1=ALU.add), reads=["fs%d" % tt], writes=["fs%d" % tt])
                S.add("act", lambda e, tt=tt: e.sqrt(out=fs[:, tt:tt + 1], in_=fs[:, tt:tt + 1]), reads=["fs%d" % tt], writes=["fs%d" % tt])
                S.add("dve", lambda e, tt=tt: e.reciprocal(out=fs[:, tt:tt + 1], in_=fs[:, tt:tt + 1]), reads=["fs%d" % tt], writes=["fs%d" % tt])
                S.add("dve", lambda e, tt=tt: e.scalar_tensor_tensor(out=xo[tt][:], in0=xo[tt][:], scalar=fs[:, tt:tt + 1], in1=fing_row[:],
                                                                    op0=ALU.mult, op1=ALU.mult),
                      reads=["xo%d" % tt, "fs%d" % tt, "fing_row"], writes=["xo%d" % tt])
                dma("sp", out_d[tb * 512 + tt * 128:tb * 512 + (tt + 1) * 128, :], xo[tt][:], ["xo%d" % tt], ["out"])
    print("n ops:", len(S.ops))
    S.emit()
    return nc, dbg_out


def _prep_inputs(inputs):
    f32 = np.float32
    perm = np.concatenate([np.arange(t * 128, (t + 1) * 128) for t in COL_ORDER])
    w_in = np.ascontiguousarray(inputs["w_in"][0][:, perm])
    shared = {
        "w_ada": np.ascontiguousarray(inputs["w_ada"][0]),
        "b_adaT": np.ascontiguousarray(inputs["b_ada"][0].reshape(48, 128).T),
        "norm_gT": np.ascontiguousarray(inputs["norm_g"][0].reshape(KC, 128).T),
        "w_in": w_in,
        "conv_wT": np.ascontiguousarray(inputs["conv_w"][0].reshape(5, 48, 128).transpose(2, 1, 0)),
        "conv_bT": np.ascontiguousarray(inputs["conv_b"][0].reshape(48, 128).T),
        "dt_biasT": np.ascontiguousarray(inputs["dt_bias"][0].reshape(128, 1)),
        "a_logT": np.ascontiguousarray(inputs["a_log"][0].reshape(128, 1)),
        "d_skip": np.ascontiguousarray(inputs["d_skip"][0].reshape(1, 64)),
        "ssd_norm_gT": np.ascontiguousarray(inputs["ssd_norm_g"][0].reshape(32, 128).T),
        "w_br_attn": np.ascontiguousarray(inputs["w_br_attn"][0]),
        "w_br_ssd": np.ascontiguousarray(inputs["w_br_ssd"][0]),
        "w_out": np.ascontiguousarray(inputs["w_out"][0]),
        "final_g": np.ascontiguousarray(inputs["final_g"].reshape(1, D)),
        "inv_freq": np.tile((10000.0 ** (-np.arange(0, 128, 2, dtype=f32) / f32(128))).astype(f32), 2).reshape(128, 1),
    }
    per_core = []
    for b in range(8):
        m = dict(shared)
        m["x"] = np.ascontiguousarray(inputs["x"][b])
        m["cT"] = np.ascontiguousarray(inputs["c"][b].reshape(KC, 128).T)
        m["pos"] = np.ascontiguousarray(inputs["positions"][b].reshape(1, S_LEN).astype(np.int32))
        per_core.append(m)
    return per_core


def kernel(**inputs):
    nc, _ = build()
    in_maps = _prep_inputs(inputs)
    res = run_bass_kernel_spmd(nc, in_maps, core_ids=list(range(8)))
    return np.stack([np.asarray(r["out"], dtype=np.float32) for r in res.results], axis=0)
```

```python
import math
from contextlib import ExitStack
import numpy as np
import ml_dtypes
import concourse.bass as bass
import concourse.mybir as mybir
from concourse.bass_utils import run_bass_kernel_spmd

F32 = mybir.dt.float32
BF16 = mybir.dt.bfloat16
I32 = mybir.dt.int32
AF = mybir.ActivationFunctionType
ALU = mybir.AluOpType

S_LEN = 2048
D = 2048
NT = 16
KC = 16
IN_COLS = 29824
EPS = 1e-6
DILS = (1, 4, 16)


class _Op:
    __slots__ = ("eng", "fn", "idx", "deps", "is_dma", "milestone", "sem", "semval", "prev_slot_val")


class Sched:
    COMPUTE = ("pe", "act", "dve", "pool")
    NSLOT = 24

    def __init__(self, nc):
        self.nc = nc
        self.ops = []
        self.last_writer = {}
        self.readers = {}
        self.barrier_deps = set()

    def add(self, eng, fn, reads=(), writes=(), dma=False):
        op = _Op()
        op.eng = eng
        op.fn = fn
        op.is_dma = dma
        op.idx = len(self.ops)
        op.milestone = False
        deps = set(self.barrier_deps)
        for r in reads:
            w = self.last_writer.get(r)
            if w is not None:
                deps.add(w)
        for w_ in writes:
            w = self.last_writer.get(w_)
            if w is not None:
                deps.add(w)
            rd = self.readers.get(w_)
            if rd:
                deps.update(rd.values())
        key = ("dma", op.idx) if dma else eng
        for r in reads:
            self.readers.setdefault(r, {})[key] = op.idx
        for w_ in writes:
            self.last_writer[w_] = op.idx
            self.readers[w_] = {}
        deps.discard(op.idx)
        op.deps = deps
        self.ops.append(op)
        return op

    def barrier(self):
        last = {}
        for op in self.ops:
            if op.is_dma:
                last[("dma", op.idx)] = op.idx
            else:
                last[op.eng] = op.idx
        dmas = [k for k in last if isinstance(k, tuple)]
        keep = set(v for k, v in last.items() if not isinstance(k, tuple))
        keep.update(last[k] for k in dmas[-(3 * self.NSLOT):])
        self.barrier_deps = keep

    def emit(self):
        nc = self.nc
        ops = self.ops
        for op in ops:
            for d in op.deps:
                x = ops[d]
                if x.is_dma:
                    continue
                if x.eng == op.eng and not op.is_dma and op.eng == "pe":
                    continue
                x.milestone = True
        sems = {e: nc.alloc_semaphore("sem_" + e) for e in self.COMPUTE}
        dsems = {e: [nc.alloc_semaphore("dsem_%s_%d" % (e, i)) for i in range(self.NSLOT)]
                 for e in ("sp", "pool", "act")}
        cnt = {e: 0 for e in self.COMPUTE}
        dcnt = {e: [0] * self.NSLOT for e in dsems}
        drr = {e: 0 for e in dsems}
        for op in ops:
            if op.is_dma:
                k = drr[op.eng] % self.NSLOT
                drr[op.eng] += 1
                op.sem = dsems[op.eng][k]
                op.prev_slot_val = dcnt[op.eng][k]
                dcnt[op.eng][k] += 16
                op.semval = dcnt[op.eng][k]
            elif op.milestone:
                cnt[op.eng] += 1
                op.sem = sems[op.eng]
                op.semval = cnt[op.eng]
        final_dma = {e: [(dsems[e][k], dcnt[e][k]) for k in range(self.NSLOT) if dcnt[e][k] > 0]
                     for e in dsems}

        def run_engine(ename, eng):
            waited = {}
            for op in ops:
                if op.eng != ename:
                    continue
                need = []
                for d in op.deps:
                    x = ops[d]
                    if (not x.is_dma) and x.eng == op.eng and not op.is_dma and op.eng == "pe":
                        continue
                    need.append((x.sem, x.semval))
                if op.is_dma and op.prev_slot_val > 0:
                    need.append((op.sem, op.prev_slot_val))
                for s, v in need:
                    key = id(s)
                    if waited.get(key, 0) < v:
                        eng.wait_ge(s, v)
                        waited[key] = v
                ins = op.fn(eng)
                if op.is_dma:
                    ins.then_inc(op.sem, 16)
                elif op.milestone:
                    ins.then_inc(op.sem, 1)
            for s, v in final_dma.get(ename, ()):
                if waited.get(id(s), 0) < v:
                    eng.wait_ge(s, v)
                    waited[id(s)] = v

        with nc.Block() as block:
            @block.tensor
            def _(e):
                run_engine("pe", e)

            @block.scalar
            def _(e):
                run_engine("act", e)

            @block.vector
            def _(e):
                run_engine("dve", e)

            @block.gpsimd
            def _(e):
                run_engine("pool", e)

            @block.sync
            def _(e):
                run_engine("sp", e)


def _col_tiles():
    order = []
    for hh in range(12):
        for t in range(3):
            for gi in range(3):
                order.append(t * 36 + gi * 12 + hh)
        order.append(108 + hh)
    order.append(200)
    for g in range(8):
        order.append(184 + g)
        order.append(192 + g)
        for j in range(4):
            order.append(152 + 4 * g + j)
        for j in range(4):
            order.append(120 + 4 * g + j)
    for j in range(16):
        order.append(201 + j)
    for j in range(16):
        order.append(217 + j)
    assert len(order) == 233 and sorted(order) == list(range(233))
    return order


COL_ORDER = _col_tiles()
OFF_ATT = 0
OFF_DT = 120
OFF_SSD = 121
OFF_GA = 201
OFF_GS = 217


def build(stop_after="all", dbg=()):
    nc = bass.Bass("TRN2", target_bir_lowering=False)
    S = Sched(nc)
    es = ExitStack()

    def dram_in(name, shape, dt=F32):
        return nc.dram_tensor(name, list(shape), dt, kind="ExternalInput").ap()

    def dram_out(name, shape, dt=F32):
        return nc.dram_tensor(name, list(shape), dt, kind="ExternalOutput").ap()

    def dram_tmp(name, shape, dt):
        return nc.dram_tensor(name, list(shape), dt, kind="Internal").ap()

    def sb(name, shape, dt, stack=None):
        return (stack or es).enter_context(nc.sbuf_tensor(name, list(shape), dt))

    x_d = dram_in("x", [S_LEN, D])
    cT_d = dram_in("cT", [128, KC])
    pos_d = dram_in("pos", [1, S_LEN], I32)
    wada_d = dram_in("w_ada", [D, 3 * D])
    bada_d = dram_in("b_adaT", [128, 48])
    normg_d = dram_in("norm_gT", [128, KC])
    win_d = dram_in("w_in", [D, IN_COLS])
    convw_d = dram_in("conv_wT", [128, 48, 5])
    convb_d = dram_in("conv_bT", [128, 48])
    dtb_d = dram_in("dt_biasT", [128, 1])
    alog_d = dram_in("a_logT", [128, 1])
    dskip_d = dram_in("d_skip", [1, 64])
    ssdg_d = dram_in("ssd_norm_gT", [128, 32])
    wbra_d = dram_in("w_br_attn", [1536, D])
    wbrs_d = dram_in("w_br_ssd", [4096, D])
    wout_d = dram_in("w_out", [D, D])
    fing_d = dram_in("final_g", [1, D])
    invf_d = dram_in("inv_freq", [128, 1])
    out_d = dram_out("out", [S_LEN, D])

    ua_scr = dram_tmp("ua_scr", [12, 128, S_LEN], BF16)

    dbg_out = {}

    ps = [nc.alloc_psum_tensor("ps%d" % i, [128, 512], F32) for i in range(8)]
    PS = ["ps%d" % i for i in range(8)]

    NWB = 3
    wbuf = [sb("wb%d" % i, [128, KC, 512], BF16) for i in range(NWB)]
    ident_f = sb("ident_f", [128, 128], F32)
    ident_b = sb("ident_b", [128, 128], BF16)
    ones_b = sb("ones_b", [128, 128], BF16)
    ones_f = sb("ones_f", [128, 128], F32)
    negm = sb("negm", [128, 256], BF16)
    adaT = sb("adaT", [128, 48], F32)
    gmod = sb("gmod", [128, KC], F32)
    small = {}
    for nm, shp in (("cT", [128, KC]), ("b_adaT", [128, 48]), ("norm_gT", [128, KC]), ("inv_freq", [128, 1]),
                    ("conv_wT", [128, 48, 5]), ("conv_bT", [128, 48]), ("dt_biasT", [128, 1]), ("a_logT", [128, 1]),
                    ("ssd_norm_gT", [128, 32])):
        small[nm] = sb("s_" + nm, shp, F32)
    dt_tok = sb("dt_tok", [128, NT, 128], F32)
    a_tok = sb("a_tok", [128, NT, 128], F32)
    dskip_row = sb("dskip_row", [128, 64], F32)
    ssq = sb("ssq_ssd", [128, NT], F32)
    hst = ExitStack()
    hT = sb("hT", [128, KC, S_LEN], BF16, hst)

    dma_rr = [0]

    def dma(eng, out, in_, reads, writes):
        S.add(eng, lambda e: e.dma_start(out=out, in_=in_), reads=reads, writes=writes, dma=True)

    def mk_consts():
        S.add("pool", lambda e: e.memset(ident_f[:], 1.0), writes=["ident_f"])
        S.add("pool", lambda e: e.affine_select(out=ident_f[:], in_=ident_f[:], pattern=[[-1, 128]],
                                               compare_op=ALU.is_equal, fill=0.0, base=0, channel_multiplier=1),
              reads=["ident_f"], writes=["ident_f"])
        S.add("dve", lambda e: e.tensor_copy(out=ident_b[:], in_=ident_f[:]), reads=["ident_f"], writes=["ident_b"])
        S.add("dve", lambda e: e.memset(ones_b[:], 1.0), writes=["ones_b"])
        S.add("dve", lambda e: e.memset(ones_f[:], 1.0), writes=["ones_f"])
        S.add("pool", lambda e: e.memset(negm[:], 0.0), writes=["negm"])
        S.add("pool", lambda e: e.affine_select(out=negm[:], in_=negm[:], pattern=[[1, 256]],
                                               compare_op=ALU.is_ge, fill=-30000.0, base=0, channel_multiplier=-1),
              reads=["negm"], writes=["negm"])
        S.add("pool", lambda e: e.affine_select(out=negm[:], in_=negm[:], pattern=[[-1, 256]],
                                               compare_op=ALU.is_ge, fill=-30000.0, base=128, channel_multiplier=1),
              reads=["negm"], writes=["negm"])

    mk_consts()

    dma("sp", small["cT"][:], cT_d, [], ["cT"])
    dma("sp", small["b_adaT"][:], bada_d, [], ["b_adaT"])
    dma("sp", small["norm_gT"][:], normg_d, [], ["norm_gT"])
    dma("sp", small["inv_freq"][:], invf_d, [], ["inv_freq"])

    with ExitStack() as st0:
        wa = [sb("wa%d" % i, [128, KC, 256], F32, st0) for i in range(2)]
        for blk in range(24):
            t = wa[blk % 2]
            rn = "wa%d" % (blk % 2)
            src = wada_d[:, blk * 256:(blk + 1) * 256].rearrange("(kc p) c -> p kc c", p=128)
            dma("sp", t[:], src, [], [rn])

            def mm(e, t=t, blk=blk):
                ins = None
                for jj in range(2):
                    j = blk * 2 + jj
                    for kc in range(KC):
                        ins = e.matmul(ps[7][:, j:j + 1], lhsT=t[:, kc, jj * 128:(jj + 1) * 128],
                                       rhs=small["cT"][:, kc:kc + 1], start=(kc == 0), stop=(kc == KC - 1),
                                       skip_group_check=True)
                return ins
            S.add("pe", mm, reads=[rn, "cT"], writes=[PS[7]])
        S.add("dve", lambda e: e.tensor_tensor(out=adaT[:], in0=ps[7][:, 0:48], in1=small["b_adaT"][:], op=ALU.add),
              reads=[PS[7], "b_adaT"], writes=["adaT"])
        S.add("dve", lambda e: e.scalar_tensor_tensor(out=gmod[:], in0=adaT[:, 16:32], scalar=1.0, in1=small["norm_gT"][:],
                                                     op0=ALU.add, op1=ALU.mult),
              reads=["adaT", "norm_gT"], writes=["gmod"])
        S.barrier()

    if "ada" in dbg:
        dbg_out["ada"] = dram_out("dbg_ada", [128, 48])
        dma("sp", dbg_out["ada"], adaT[:], ["adaT"], ["dbg_ada"])

    with ExitStack() as st1:
        xt = [sb("xt%d" % i, [128, D], F32, st1) for i in range(2)]
        sq = sb("sq_junk", [128, D], F32, st1)
        ssq0 = sb("ssq", [128, NT], F32, st1)
        rstd = sb("rstd", [128, NT], F32, st1)
        for i in range(NT):
            t = xt[i % 2]
            rn = "xt%d" % (i % 2)
            dma("sp", t[:], x_d[i * 128:(i + 1) * 128, :], [], [rn])
            S.add("act", lambda e, t=t, i=i: e.activation(out=sq[:], in_=t[:], func=AF.Square, accum_out=ssq0[:, i:i + 1]),
                  reads=[rn], writes=["sq", "ssq%d" % i])
            S.add("dve", lambda e, i=i: e.tensor_scalar(out=rstd[:, i:i + 1], in0=ssq0[:, i:i + 1], scalar1=1.0 / D, scalar2=EPS,
                                                       op0=ALU.mult, op1=ALU.add),
                  reads=["ssq%d" % i], writes=["rstd%d" % i])
            S.add("act", lambda e, i=i: e.sqrt(out=rstd[:, i:i + 1], in_=rstd[:, i:i + 1]),
                  reads=["rstd%d" % i], writes=["rstd%d" % i])
            S.add("dve", lambda e, i=i: e.reciprocal(out=rstd[:, i:i + 1], in_=rstd[:, i:i + 1]),
                  reads=["rstd%d" % i], writes=["rstd%d" % i])
            S.add("dve", lambda e, t=t, i=i: e.tensor_scalar(out=t[:], in0=t[:], scalar1=rstd[:, i:i + 1], scalar2=None, op0=ALU.mult),
                  reads=[rn, "rstd%d" % i], writes=[rn])
            for q4 in range(4):
                bank = 4 + (q4 % 2)

                def tr(e, t=t, q4=q4, bank=bank):
                    ins = None
                    for k4 in range(4):
                        kc = q4 * 4 + k4
                        ins = e.transpose(out=ps[bank][:, k4 * 128:(k4 + 1) * 128], in_=t[:, kc * 128:(kc + 1) * 128],
                                          identity=ident_f[:])
                    return ins
                S.add("pe", tr, reads=[rn, "ident_f"], writes=[PS[bank]])
                for k4 in range(4):
                    kc = q4 * 4 + k4
                    S.add("act", lambda e, kc=kc, k4=k4, bank=bank, i=i: e.activation(
                        out=hT[:, kc, i * 128:(i + 1) * 128], in_=ps[bank][:, k4 * 128:(k4 + 1) * 128], func=AF.Identity,
                        scale=gmod[:, kc:kc + 1], bias=adaT[:, kc:kc + 1]),
                        reads=[PS[bank], "gmod", "adaT"], writes=["hT"])
        S.barrier()

    if "h" in dbg:
        dbg_out["hT"] = dram_out("dbg_hT", [128, KC, S_LEN], BF16)
        dma("sp", dbg_out["hT"], hT[:], ["hT"], ["dbg_hT"])

    if stop_after == "h":
        S.emit()
        return nc, dbg_out

    att = ExitStack()
    cosT = sb("cosT", [128, S_LEN], F32, att)
    sinS = sb("sinS", [128, S_LEN], F32, att)
    with ExitStack() as st2:
        posi = sb("posi", [128, S_LEN], I32, st2)
        ang = sb("ang", [128, S_LEN], F32, st2)
        nn = sb("nn", [128, S_LEN], F32, st2)
        dma("sp", posi[:], pos_d.partition_broadcast(128), [], ["posi"])
        S.add("dve", lambda e: e.tensor_copy(out=ang[:], in_=posi[:]), reads=["posi"], writes=["ang"])
        S.add("dve", lambda e: e.tensor_scalar(out=ang[:], in0=ang[:], scalar1=small["inv_freq"][:, 0:1], scalar2=None, op0=ALU.mult),
              reads=["ang", "inv_freq"], writes=["ang"])
        MAGIC = 12582912.0
        S.add("dve", lambda e: e.tensor_scalar(out=nn[:], in0=ang[:], scalar1=1.0 / (2 * math.pi), scalar2=MAGIC, op0=ALU.mult, op1=ALU.add),
              reads=["ang"], writes=["nn"])
        S.add("dve", lambda e: e.tensor_scalar(out=nn[:], in0=nn[:], scalar1=-MAGIC, scalar2=None, op0=ALU.add),
              reads=["nn"], writes=["nn"])
        C1 = 6.28125
        C2 = 2 * math.pi - C1
        S.add("dve", lambda e: e.scalar_tensor_tensor(out=ang[:], in0=nn[:], scalar=-C1, in1=ang[:], op0=ALU.mult, op1=ALU.add),
              reads=["nn", "ang"], writes=["ang"])
        S.add("dve", lambda e: e.scalar_tensor_tensor(out=ang[:], in0=nn[:], scalar=-C2, in1=ang[:], op0=ALU.mult, op1=ALU.add),
              reads=["nn", "ang"], writes=["ang"])
        PI_LO = 3.1415925
        S.add("dve", lambda e: e.tensor_scalar(out=ang[:], in0=ang[:], scalar1=PI_LO, scalar2=-PI_LO, op0=ALU.min, op1=ALU.max),
              reads=["ang"], writes=["ang"])
        S.add("act", lambda e: e.activation(out=sinS[:], in_=ang[:], func=AF.Sin), reads=["ang"], writes=["sinS"])
        S.add("dve", lambda e: e.tensor_scalar(out=nn[:], in0=ang[:], scalar1=-1.0, scalar2=None, op0=ALU.mult), reads=["ang"], writes=["nn"])
        S.add("dve", lambda e: e.tensor_tensor(out=nn[:], in0=nn[:], in1=ang[:], op=ALU.max), reads=["ang", "nn"], writes=["nn"])
        S.add("dve", lambda e: e.tensor_scalar(out=nn[:], in0=nn[:], scalar1=-1.0, scalar2=math.pi / 2, op0=ALU.mult, op1=ALU.add),
              reads=["nn"], writes=["nn"])
        S.add("act", lambda e: e.activation(out=cosT[:], in_=nn[:], func=AF.Sin), reads=["nn"], writes=["cosT"])
        S.add("dve", lambda e: e.tensor_scalar(out=sinS[64:128, :], in0=sinS[64:128, :], scalar1=-1.0, scalar2=None, op0=ALU.mult),
              reads=["sinS"], writes=["sinS"])
        S.barrier()

    wb_rr = [0]

    def load_w(src2d, nk, ncols):
        k = wb_rr[0] % NWB
        wb_rr[0] += 1
        t = wbuf[k]
        rn = "wb%d" % k
        dma("pool", t[:, 0:nk, 0:ncols], src2d.rearrange("(kc p) c -> p kc c", p=128), [], [rn])
        return t, rn

    def wcols(tile0, ntiles):
        return win_d[:, tile0 * 128:(tile0 + ntiles) * 128]

    pj_rr = [0]
    pj_banks = [0, 1]

    def proj_fm(wt, wrn, c0, tb):
        bank = pj_banks[pj_rr[0] % len(pj_banks)]
        pj_rr[0] += 1

        def mm(e):
            ins = None
            for kc in range(KC):
                ins = e.matmul(ps[bank][:], lhsT=wt[:, kc, c0:c0 + 128], rhs=hT[:, kc, tb * 512:(tb + 1) * 512],
                               start=(kc == 0), stop=(kc == KC - 1))
            return ins
        S.add("pe", mm, reads=[wrn, "hT"], writes=[PS[bank]])
        return bank

    QT = [sb("QT%d" % g, [128, S_LEN], BF16, att) for g in range(3)]
    KT = [sb("KT%d" % g, [128, S_LEN], BF16, att) for g in range(3)]
    Vt = [sb("Vt%d" % g, [128, 16, 128], BF16, att) for g in range(3)]
    sza = sb("sza", [128, S_LEN], BF16, att)
    rt1 = [sb("rt1_%d" % i, [128, 512], F32, att) for i in range(2)]
    rt2 = [sb("rt2_%d" % i, [128, 512], F32, att) for i in range(2)]
    PT = [sb("PT%d" % i, [128, 256], BF16, att) for i in range(3)]
    rd = sb("rd", [128, 512], F32, att)
    ot = sb("ot", [128, 512], F32, att)
    ub = [sb("ub%d" % i, [128, 512], BF16, att) for i in range(2)]
    rot_rr = [0]
    st_rr = [0]
    ub_rr = [0]
    SCALE = 128.0 ** -0.5

    def rotary_evac(bank, dest, dest_rn, tb):
        k = rot_rr[0] % 2
        rot_rr[0] += 1
        t1, t2 = rt1[k], rt2[k]
        sl = slice(tb * 512, (tb + 1) * 512)
        S.add("dve", lambda e: e.tensor_tensor(out=t1[:], in0=ps[bank][:], in1=cosT[:, sl], op=ALU.mult),
              reads=[PS[bank], "cosT"], writes=["rt1_%d" % k])
        S.add("dve", lambda e: e.tensor_tensor(out=t2[0:64, :], in0=ps[bank][64:128, :], in1=sinS[64:128, sl], op=ALU.mult),
              reads=[PS[bank], "sinS"], writes=["rt2a_%d" % k])
        S.add("dve", lambda e: e.tensor_tensor(out=t2[64:128, :], in0=ps[bank][0:64, :], in1=sinS[0:64, sl], op=ALU.mult),
              reads=[PS[bank], "sinS"], writes=["rt2b_%d" % k])
        S.add("dve", lambda e: e.tensor_tensor(out=dest[:, sl], in0=t1[:], in1=t2[:], op=ALU.add),
              reads=["rt1_%d" % k, "rt2a_%d" % k, "rt2b_%d" % k], writes=[dest_rn])

    n_hh = 12 if stop_after != "att1" else 1
    for hh in range(n_hh):
        base = OFF_ATT + hh * 10
        w1, w1n = load_w(wcols(base, 4), KC, 512)
        w2, w2n = load_w(wcols(base + 4, 4), KC, 512)
        w3, w3n = load_w(wcols(base + 8, 2), KC, 256)
        qk = [(w1, w1n, 0, QT[0], "QT0"), (w1, w1n, 128, QT[1], "QT1"), (w1, w1n, 256, QT[2], "QT2"),
              (w1, w1n, 384, KT[0], "KT0"), (w2, w2n, 0, KT[1], "KT1"), (w2, w2n, 128, KT[2], "KT2")]
        def do_qk():
            for (wt, wrn, c0, dest, drn) in qk:
                for tb in range(4):
                    bank = proj_fm(wt, wrn, c0, tb)
                    rotary_evac(bank, dest, drn, tb)

        def do_v():
            for gi, (wt, wrn, c0) in enumerate(((w2, w2n, 256), (w2, w2n, 384), (w3, w3n, 0))):
                d = DILS[gi]
                nlb = 16 // d
                for q4 in range(4):
                    bank = 6 + (q4 % 2)

                    def vmm(e, q4=q4, bank=bank, d=d, nlb=nlb, wt=wt, c0=c0):
                        ins = None
                        for k4 in range(4):
                            tid = q4 * 4 + k4
                            r, lb = tid // nlb, tid % nlb
                            s0 = r + d * 128 * lb
                            for kc in range(KC):
                                ins = e.matmul(ps[bank][:, k4 * 128:(k4 + 1) * 128], lhsT=hT[:, kc, s0:s0 + d * 127 + 1:d],
                                               rhs=wt[:, kc, c0:c0 + 128], start=(kc == 0), stop=(kc == KC - 1),
                                               skip_group_check=True)
                        return ins
                    S.add("pe", vmm, reads=[wrn, "hT"], writes=[PS[bank]])
                    S.add("act", lambda e, gi=gi, q4=q4, bank=bank: e.copy(
                        out=Vt[gi][:, q4 * 4:(q4 + 1) * 4, :], in_=ps[bank][:].rearrange("p (a b) -> p a b", a=4)),
                        reads=[PS[bank]], writes=["Vt%d" % gi])

        def do_za():
            for tb in range(4):
                bank = proj_fm(w3, w3n, 128, tb)
                S.add("act", lambda e, bank=bank, tb=tb: e.activation(out=sza[:, tb * 512:(tb + 1) * 512], in_=ps[bank][:], func=AF.Silu),
                      reads=[PS[bank]], writes=["sza"])

        if hh == 0:
            do_v()
            do_za()
            do_qk()
        else:
            do_qk()
            do_v()
            do_za()
        for QB in range(4):
            first = True
            pend = None
            for gi, d in enumerate(DILS):
                nlb = 16 // d
                nq = 512 // d
                lq0 = QB * nq
                for r in range(d):
                    lb_lo = max(0, (lq0 - 64) // 128)
                    lb_hi = min(nlb - 1, (lq0 + nq - 1 + 64) // 128)
                    for lb in range(lb_lo, lb_hi + 1):
                        k0 = lb * 128
                        qs = max(lq0, k0 - 64)
                        qe = min(lq0 + nq, k0 + 128 + 64)
                        nqq = qe - qs
                        if nqq <= 0:
                            continue
                        off = qs - k0 + 64
                        qtok0 = r + d * qs
                        ktok0 = r + d * k0
                        oc0 = r + d * (qs - lq0)
                        stb = 2 + (st_rr[0] % 2)
                        pk = st_rr[0] % 3
                        st_rr[0] += 1
                        pt = PT[pk]

                        def smm(e, stb=stb, nqq=nqq, off=off, gi=gi, d=d, ktok0=ktok0, qtok0=qtok0):
                            e.matmul(ps[stb][:, 0:nqq], lhsT=ident_b[:], rhs=negm[:, off:off + nqq], start=True, stop=False)
                            return e.matmul(ps[stb][:, 0:nqq], lhsT=KT[gi][:, ktok0:ktok0 + d * 127 + 1:d],
                                            rhs=QT[gi][:, qtok0:qtok0 + d * (nqq - 1) + 1:d], start=False, stop=True)
                        S.add("pe", smm, reads=["KT%d" % gi, "QT%d" % gi, "negm", "ident_b"], writes=[PS[stb]])
                        S.add("act", lambda e, stb=stb, nqq=nqq, pt=pt: e.activation(out=pt[:, 0:nqq], in_=ps[stb][:, 0:nqq],
                                                                                      func=AF.Exp, scale=SCALE),
                              reads=[PS[stb]], writes=["PT%d" % pk])

                        def pv(e, first=first, oc0=oc0, d=d, nqq=nqq, gi=gi, tid=r * nlb + lb, pt=pt):
                            osl = slice(oc0, oc0 + d * (nqq - 1) + 1, d)
                            e.matmul(ps[4][:, osl], lhsT=Vt[gi][:, tid, :], rhs=pt[:, 0:nqq], start=first, stop=False,
                                     skip_group_check=True)
                            return e.matmul(ps[5][:, osl], lhsT=ones_b[:], rhs=pt[:, 0:nqq], start=first, stop=False,
                                            skip_group_check=True)
                        if pend:
                            S.add("pe", pend[0], reads=pend[1], writes=[PS[4], PS[5]])
                        pend = (pv, ["Vt%d" % gi, "PT%d" % pk, "ones_b", PS[4], PS[5]])
                        first = False
            if pend:
                S.add("pe", pend[0], reads=pend[1], writes=[PS[4], PS[5]])
            uk = ub_rr[0] % 2
            ub_rr[0] += 1
            sl = slice(QB * 512, (QB + 1) * 512)
            S.add("dve", lambda e: e.reciprocal(out=rd[:], in_=ps[5][:]), reads=[PS[5]], writes=["rd"])
            S.add("dve", lambda e: e.tensor_tensor(out=ot[:], in0=ps[4][:], in1=rd[:], op=ALU.mult), reads=[PS[4], "rd"], writes=["ot"])
            S.add("dve", lambda e, uk=uk, sl=sl: e.tensor_tensor(out=ub[uk][:], in0=ot[:], in1=sza[:, sl], op=ALU.mult),
                  reads=["ot", "sza"], writes=["ub%d" % uk])
            dma("sp", ua_scr[hh, :, sl], ub[uk][:], ["ub%d" % uk], ["ua_scr"])

    if "att" in dbg:
        dbg_out["ua"] = dram_out("dbg_ua", [12, 128, S_LEN], BF16)
        dma("sp", dbg_out["ua"], ua_scr, ["ua_scr"], ["dbg_ua"])
        dbg_out["QT0"] = dram_out("dbg_QT0", [128, S_LEN], BF16)
        dma("sp", dbg_out["QT0"], QT[0][:], ["QT0"], ["dbg_QT0"])
    print("sbuf remaining after attention alloc:", nc.sbuf_bytes_remaining)
    if stop_after in ("att", "att1"):
        S.emit()
        return nc, dbg_out
    S.barrier()
    att.close()

    xbc_scr = dram_tmp("xbc_scr", [48, 128, S_LEN], BF16)
    sz_scr = dram_tmp("sz_scr", [S_LEN, 4096], BF16)
    sg_scr = dram_tmp("sg_scr", [32, 128, S_LEN], BF16)
    us_scr = dram_tmp("us_scr", [32, 128, S_LEN], BF16)

    for nm, shp, src_d in (("conv_wT", [128, 48, 5], convw_d), ("conv_bT", [128, 48], convb_d), ("dt_biasT", [128, 1], dtb_d),
                           ("a_logT", [128, 1], alog_d), ("ssd_norm_gT", [128, 32], ssdg_d)):
        dma("sp", small[nm][:], src_d, [], [nm])
    dma("sp", dskip_row[:], dskip_d.partition_broadcast(128), [], ["dskip_row"])

    pj_banks[:] = [0, 1, 2, 3, 4, 5]
    with ExitStack() as st3:
        dtT = sb("dtT", [128, S_LEN], F32, st3)
        aT = sb("aT", [128, S_LEN], F32, st3)
        Aneg = sb("Aneg", [128, 1], F32, st3)
        stage = [sb("stage%d" % i, [128, S_LEN + 4], F32, st3) for i in range(2)]
        acc = [sb("acc%d" % i, [128, S_LEN], F32, st3) for i in range(2)]
        cvT = [sb("cvT%d" % i, [128, S_LEN], BF16, st3) for i in range(2)]
        szb = [sb("szb%d" % i, [128, 512], BF16, st3) for i in range(2)]
        for i in range(2):
            S.add("dve", lambda e, i=i: e.memset(stage[i][:, 0:2], 0.0), writes=["stage%d" % i])
            S.add("dve", lambda e, i=i: e.memset(stage[i][:, S_LEN + 2:S_LEN + 4], 0.0), writes=["stage%d" % i])
        wd, wdn = load_w(wcols(OFF_DT, 1), KC, 128)
        for tb in range(4):
            bank = proj_fm(wd, wdn, 0, tb)
            sl = slice(tb * 512, (tb + 1) * 512)
            S.add("act", lambda e, bank=bank, sl=sl: e.activation(out=dtT[:, sl], in_=ps[bank][:], func=AF.Exp,
                                                                  bias=small["dt_biasT"][:, 0:1]),
                  reads=[PS[bank], "dt_biasT"], writes=["dtT"])
        S.add("act", lambda e: e.activation(out=dtT[:], in_=dtT[:], func=AF.Ln, bias=1.0), reads=["dtT"], writes=["dtT"])
        S.add("act", lambda e: e.activation(out=Aneg[:], in_=small["a_logT"][:], func=AF.Exp), reads=["a_logT"], writes=["Aneg"])
        S.add("dve", lambda e: e.tensor_scalar(out=aT[:], in0=dtT[:], scalar1=Aneg[:, 0:1], scalar2=-1.0, op0=ALU.mult, op1=ALU.mult),
              reads=["dtT", "Aneg"], writes=["aT"])
        for i in range(NT):
            bank = 6 + (i % 2)

            def tr2(e, i=i, bank=bank):
                e.transpose(out=ps[bank][:, 0:128], in_=dtT[:, i * 128:(i + 1) * 128], identity=ident_f[:])
                return e.transpose(out=ps[bank][:, 128:256], in_=aT[:, i * 128:(i + 1) * 128], identity=ident_f[:])
            S.add("pe", tr2, reads=["dtT", "aT", "ident_f"], writes=[PS[bank]])
            S.add("act", lambda e, i=i, bank=bank: e.copy(out=dt_tok[:, i, :], in_=ps[bank][:, 0:128]), reads=[PS[bank]], writes=["dt_tok"])
            S.add("act", lambda e, i=i, bank=bank: e.copy(out=a_tok[:, i, :], in_=ps[bank][:, 128:256]), reads=[PS[bank]], writes=["a_tok"])

        cv_rr = [0]

        def conv_tile(wt, wrn, c0, cc):
            k = cv_rr[0] % 2
            cv_rr[0] += 1
            stg, ac, cv = stage[k], acc[k], cvT[k]
            for tb in range(4):
                bank = proj_fm(wt, wrn, c0, tb)
                sl = slice(tb * 512, (tb + 1) * 512)
                S.add("act", lambda e, bank=bank, tb=tb, stg=stg: e.copy(out=stg[:, 2 + tb * 512:2 + (tb + 1) * 512], in_=ps[bank][:]),
                      reads=[PS[bank]], writes=["stage%d" % k])
                S.add("act", lambda e, bank=bank, sl=sl, ac=ac: e.activation(out=ac[:, sl], in_=ps[bank][:], func=AF.Identity,
                                                                             scale=small["conv_wT"][:, cc, 2:3],
                                                                             bias=small["conv_bT"][:, cc:cc + 1]),
                      reads=[PS[bank], "conv_wT", "conv_bT"], writes=["acc%d" % k])
            for tap in (0, 1, 3, 4):
                S.add("dve", lambda e, tap=tap, stg=stg, ac=ac: e.scalar_tensor_tensor(
                    out=ac[:], in0=stg[:, tap:tap + S_LEN], scalar=small["conv_wT"][:, cc, tap:tap + 1], in1=ac[:],
                    op0=ALU.mult, op1=ALU.add),
                    reads=["stage%d" % k, "acc%d" % k, "conv_wT"], writes=["acc%d" % k])
            S.add("act", lambda e, ac=ac, cv=cv: e.activation(out=cv[:], in_=ac[:], func=AF.Silu), reads=["acc%d" % k], writes=["cvT%d" % k])
            dma("sp", xbc_scr[cc], cv[:], ["cvT%d" % k], ["xbc_scr"])

        sz_rr = [0]
        n_g = 8
        for g in range(n_g):
            base = OFF_SSD + g * 10
            wbc, wbcn = load_w(wcols(base, 2), KC, 256)
            conv_tile(wbc, wbcn, 0, 32 + g)
            conv_tile(wbc, wbcn, 128, 40 + g)
            wx, wxn = load_w(wcols(base + 2, 4), KC, 512)
            for j in range(4):
                conv_tile(wx, wxn, j * 128, 4 * g + j)
            wz, wzn = load_w(wcols(base + 6, 4), KC, 512)
            for i in range(NT):
                bank = pj_banks[pj_rr[0] % len(pj_banks)]
                pj_rr[0] += 1

                def zmm(e, i=i, bank=bank, wz=wz):
                    ins = None
                    for kc in range(KC):
                        ins = e.matmul(ps[bank][:], lhsT=hT[:, kc, i * 128:(i + 1) * 128], rhs=wz[:, kc, 0:512],
                                       start=(kc == 0), stop=(kc == KC - 1))
                    return ins
                S.add("pe", zmm, reads=[wzn, "hT"], writes=[PS[bank]])
                k = sz_rr[0] % 2
                sz_rr[0] += 1
                S.add("act", lambda e, bank=bank, k=k: e.activation(out=szb[k][:], in_=ps[bank][:], func=AF.Silu),
                      reads=[PS[bank]], writes=["szb%d" % k])
                dma("sp", sz_scr[i * 128:(i + 1) * 128, g * 512:(g + 1) * 512], szb[k][:], ["szb%d" % k], ["sz_scr"])
        for gb in range(8):
            wg, wgn = load_w(wcols(OFF_GA + gb * 4, 4), KC, 512)
            for j4 in range(4):
                j = gb * 4 + j4
                k = cv_rr[0] % 2
                cv_rr[0] += 1
                for tb in range(4):
                    bank = proj_fm(wg, wgn, j4 * 128, tb)
                    S.add("act", lambda e, bank=bank, tb=tb, k=k: e.activation(out=cvT[k][:, tb * 512:(tb + 1) * 512], in_=ps[bank][:],
                                                                                  func=AF.Sigmoid),
                          reads=[PS[bank]], writes=["cvT%d" % k])
                dma("sp", sg_scr[j], cvT[k][:], ["cvT%d" % k], ["sg_scr"])
        S.barrier()
    hst.close()
    print("sbuf remaining after proj phase:", nc.sbuf_bytes_remaining)
    if stop_after == "proj":
        S.emit()
        return nc, dbg_out

    S.add("dve", lambda e: e.memset(ssq[:], 0.0), writes=["ssq_ssd"])
    with ExitStack() as st4:
        triM = sb("triM", [128, 128], F32, st4)
        ustM = sb("ustM", [128, 128], F32, st4)
        geM = sb("geM", [128, 128], F32, st4)
        lstM = sb("lstM", [128, 128], F32, st4)
        for (m, nm, pat, base, cm, op) in ((triM, "triM", 1, 0, -1, ALU.is_ge), (ustM, "ustM", -1, 0, 1, ALU.is_gt),
                                           (geM, "geM", -1, 0, 1, ALU.is_ge), (lstM, "lstM", 1, 0, -1, ALU.is_gt)):
            S.add("pool", lambda e, m=m: e.memset(m[:], 1.0), writes=[nm])
            S.add("pool", lambda e, m=m, pat=pat, base=base, cm=cm, op=op: e.affine_select(
                out=m[:], in_=m[:], pattern=[[pat, 128]], compare_op=op, fill=0.0, base=base, channel_multiplier=cm),
                reads=[nm], writes=[nm])
        BT = sb("BT", [128, S_LEN], BF16, st4)
        CT = sb("CT", [128, S_LEN], BF16, st4)
        xsT = [sb("xsT%d" % j, [128, S_LEN], BF16, st4) for j in range(4)]
        sz_tok = sb("sz_tok", [128, NT, 512], BF16, st4)
        B_tok = sb("B_tok", [128, NT, 128], BF16, st4)
        xdt = [sb("xdt%d" % d_, [128, NT, 512], BF16, st4) for d_ in range(2)]
        yacc = sb("yacc", [128, NT, 512], F32, st4)
        est = sb("est", [128, NT, 48], F32, st4)
        rhsD = [sb("rhsD0", [128, 8, 128], F32, st4)] * 2
        Ebuf = [sb("Ebuf0", [128, 8, 128], F32, st4)] * 2
        mcb = [sb("mcb%d" % i, [128, 128], F32, st4) for i in range(2)]
        MT = [sb("MT0", [128, 8, 128], BF16, st4)] * 2
        xdec = [sb("xdec0", [128, 512], BF16, st4)] * 2
        carry = [sb("carry%d" % d_, [128, 512], F32, st4) for d_ in range(2)]
        prevb = [sb("prevb%d" % d_, [128, 512], BF16, st4) for d_ in range(2)]
        ytmp = [sb("ytmp0", [128, 512], F32, st4)] * 2
        vtmp = sb("vtmp", [128, 512], F32, st4)
        vjunk = sb("vjunk", [128, 512], F32, st4)
        s1 = sb("s1", [128, 1], F32, st4)
        vb = [sb("vb0", [128, 512], BF16, st4)] * 2
        usT = xsT
        rr = {"d": 0, "y": 0, "v": 0}
        wv = wbuf[0]
        f32v = lambda ap: ap.bitcast(F32).rearrange("p a (b c) -> p (a b) c", c=128)
        Ebuf2 = [Ebuf[0][:], f32v(wv[:, 0:4, :])]
        rhsD2 = [rhsD[0][:], f32v(wv[:, 4:8, :])]
        ytmp2 = [ytmp[0][:], wv[:, 8:10, :].bitcast(F32).rearrange("p a b -> p (a b)")]
        MT2 = [MT[0][:], wv[:, 10:12, :].rearrange("p a (b c) -> p (a b) c", c=128)]
        xdec2 = [xdec[0][:], wv[:, 12, :]]

        def bc_p(ap2):
            return ap2.unsqueeze(2).to_broadcast([128, 8, 64])

        def v3(ap2):
            return ap2.rearrange("p (e q) -> p e q", e=8)

        wscr = dram_tmp("wscr", [16, 128, KC, 512], BF16)
        slabs = []
        for jb_ in range(4):
            slabs.append((wbra_d[:, jb_ * 512:(jb_ + 1) * 512], 12))
            slabs.append((wbrs_d[0:2048, jb_ * 512:(jb_ + 1) * 512], 16))
            slabs.append((wbrs_d[2048:4096, jb_ * 512:(jb_ + 1) * 512], 16))
        for ob_ in range(4):
            slabs.append((wout_d[:, ob_ * 512:(ob_ + 1) * 512], 16))

        def precast(si_, k_):
            src2d, nk_ = slabs[si_]
            t_ = wbuf[k_]
            rn_ = "wb%d" % k_
            dma("pool", t_[:, 0:nk_, :], src2d.rearrange("(kc p) c -> p kc c", p=128), [], [rn_])
            dma("sp", wscr[si_, :, 0:nk_, :], t_[:, 0:nk_, :], [rn_], ["wscr"])

        for g in range(n_g):
            precast(2 * g, 1)
            precast(2 * g + 1, 2)
            dma("sp", BT[:], xbc_scr[32 + g], ["xbc_scr"], ["BT"])
            dma("sp", CT[:], xbc_scr[40 + g], ["xbc_scr"], ["CT"])
            for j in range(4):
                dma("sp", xsT[j][:], xbc_scr[4 * g + j], ["xbc_scr"], ["xsT%d" % j])
            dma("sp", sz_tok[:], sz_scr[:, g * 512:(g + 1) * 512].rearrange("(i p) c -> p i c", p=128), ["sz_scr"], ["sz_tok"])
            fcol = slice(g * 8, g * 8 + 8)
            bcol = slice(64 + g * 8, 64 + g * 8 + 8)
            for c in range(NT):
                csl = slice(c * 128, (c + 1) * 128)
                pb7 = ps[7][:].bitcast(BF16)

                def trx(e, csl=csl, pb7=pb7):
                    ins = None
                    for j in range(4):
                        ins = e.transpose(out=pb7[:, j * 128:(j + 1) * 128], in_=xsT[j][:, csl], identity=ident_b[:])
                    return e.transpose(out=pb7[:, 512:640], in_=BT[:, csl], identity=ident_b[:])
                S.add("pe", trx, reads=["xsT0", "xsT1", "xsT2", "xsT3", "BT", "ident_b"], writes=[PS[7]])
                S.add("dve", lambda e, c=c, pb7=pb7: e.tensor_copy(out=B_tok[:, c, :], in_=pb7[:, 512:640]), reads=[PS[7]], writes=["B_tok"])
                S.add("dve", lambda e, c=c, pb7=pb7, fcol=fcol: e.tensor_tensor(out=v3(xdt[0][:, c, :]), in0=v3(pb7[:, 0:512]),
                                                                     in1=bc_p(dt_tok[:, c, fcol]), op=ALU.mult),
                      reads=[PS[7], "dt_tok"], writes=["xdt0"])
                S.add("dve", lambda e, c=c, pb7=pb7, bcol=bcol: e.tensor_tensor(out=v3(xdt[1][:, c, :]), in0=v3(pb7[:, 0:512]),
                                                                     in1=bc_p(dt_tok[:, c, bcol]), op=ALU.mult),
                      reads=[PS[7], "dt_tok"], writes=["xdt1"])
                S.add("dve", lambda e, c=c, pb7=pb7, fcol=fcol: e.tensor_tensor(out=v3(yacc[:, c, :]), in0=v3(pb7[:, 0:512]),
                                                                     in1=bc_p(dskip_row[:, fcol]), op=ALU.mult),
                      reads=[PS[7], "dskip_row"], writes=["yacc%d" % c])

                def stat(e, c=c, fcol=fcol, bcol=bcol):
                    af = a_tok[:, c, fcol]
                    ab = a_tok[:, c, bcol]
                    e.matmul(ps[6][:, 0:8], lhsT=triM[:], rhs=af, start=True, stop=True, skip_group_check=True)
                    e.matmul(ps[6][:, 8:16], lhsT=ustM[:], rhs=af, start=True, stop=True, skip_group_check=True)
                    e.matmul(ps[6][:, 16:24], lhsT=geM[:], rhs=ab, start=True, stop=True, skip_group_check=True)
                    e.matmul(ps[6][:, 24:32], lhsT=lstM[:], rhs=ab, start=True, stop=True, skip_group_check=True)
                    e.matmul(ps[6][:, 32:40], lhsT=ones_f[:], rhs=af, start=True, stop=True, skip_group_check=True)
                    return e.matmul(ps[6][:, 40:48], lhsT=ones_f[:], rhs=ab, start=True, stop=True, skip_group_check=True)
                S.add("pe", stat, reads=["a_tok", "triM", "ustM", "geM", "lstM", "ones_f"], writes=[PS[6]])
                S.add("act", lambda e, c=c: e.activation(out=est[:, c, :], in_=ps[6][:, 0:48], func=AF.Exp), reads=[PS[6]], writes=["est"])

            steps = [(ci, dr) for ci in range(NT) for dr in range(2)]

            def stage_r(si, ci, dr):
                c = ci if dr == 0 else NT - 1 - ci
                acol = fcol if dr == 0 else bcol
                maskR = triM if dr == 0 else geM
                k = si % 2
                rD = rhsD2[k]
                def mk_rhs(e_):
                    ins = None
                    for h8 in range(8):
                        ins = e_.activation(out=rD[:, h8, :], in_=maskR[:], func=AF.Copy,
                                            scale=a_tok[:, c, acol.start + h8:acol.start + h8 + 1])
                    return ins
                S.add("act", mk_rhs, reads=["a_tok", "triM", "geM"], writes=["rhsD%d" % k])

            def stage_a(si, ci, dr):
                c = ci if dr == 0 else NT - 1 - ci
                csl = slice(c * 128, (c + 1) * 128)
                maskR = triM if dr == 0 else geM
                maskL = ustM if dr == 0 else lstM
                k = si % 2
                rD, Eb, Mt, mc = rhsD2[k], Ebuf2[k], MT2[k], mcb[k]
                if ci < NT - 1:
                    eo_ = 0 if dr == 0 else 16
                    xd_ = xdec2[k]
                    S.add("dve", lambda e: e.tensor_tensor(out=v3(xd_), in0=v3(xdt[dr][:, c, :]), in1=bc_p(est[:, c, eo_ + 8:eo_ + 16]), op=ALU.mult),
                          reads=["xdt%d" % dr, "est"], writes=["xdec%d" % k])

                def dmm(e):
                    e.matmul(ps[2][:], lhsT=maskL[:], rhs=rD[:, 0:4, :], start=True, stop=True)
                    return e.matmul(ps[3][:], lhsT=maskL[:], rhs=rD[:, 4:8, :], start=True, stop=True)
                S.add("pe", dmm, reads=["rhsD%d" % k, "ustM", "lstM"], writes=[PS[2], PS[3]])
                S.add("act", lambda e: e.activation(out=Eb[:, 0:4, :], in_=ps[2][:].rearrange("p (e l) -> p e l", e=4), func=AF.Exp),
                      reads=[PS[2]], writes=["Ebuf%d" % k])
                S.add("act", lambda e: e.activation(out=Eb[:, 4:8, :], in_=ps[3][:].rearrange("p (e l) -> p e l", e=4), func=AF.Exp),
                      reads=[PS[3]], writes=["Ebuf%d" % k])
                S.add("pe", lambda e: e.matmul(ps[5][:, 0:128], lhsT=BT[:, csl], rhs=CT[:, csl], start=True, stop=True),
                      reads=["BT", "CT"], writes=[PS[5]])
                S.add("dve", lambda e: e.tensor_tensor(out=mc[:], in0=ps[5][:, 0:128], in1=maskR[:], op=ALU.mult),
                      reads=[PS[5], "triM", "geM"], writes=["mcb%d" % k])
                S.add("dve", lambda e: e.tensor_tensor(out=Mt, in0=Eb, in1=mc[:].unsqueeze(1).to_broadcast([128, 8, 128]), op=ALU.mult),
                      reads=["Ebuf%d" % k, "mcb%d" % k], writes=["MT%d" % k])

            def stage_b(si, ci, dr):
                c = ci if dr == 0 else NT - 1 - ci
                csl = slice(c * 128, (c + 1) * 128)
                eo = 0 if dr == 0 else 16
                cdo = 32 if dr == 0 else 40
                k = si % 2
                Mt, xd, yt = MT2[k], xdec2[k], ytmp2[k]

                def ydiag(e):
                    ins = None
                    for h8 in range(8):
                        ins = e.matmul(ps[4][:, h8 * 64:(h8 + 1) * 64], lhsT=Mt[:, h8, :], rhs=xdt[dr][:, c, h8 * 64:(h8 + 1) * 64],
                                       start=True, stop=True, skip_group_check=True)
                    return ins
                S.add("pe", ydiag, reads=["MT%d" % k, "xdt%d" % dr], writes=[PS[4]])
                S.add("dve", lambda e: e.tensor_tensor(out=yacc[:, c, :], in0=yacc[:, c, :], in1=ps[4][:], op=ALU.add),
                      reads=[PS[4], "yacc%d" % c], writes=["yacc%d" % c])
                if ci > 0:
                    S.add("pe", lambda e: e.matmul(ps[1][:], lhsT=CT[:, csl], rhs=prevb[dr][:], start=True, stop=True),
                          reads=["CT", "prevb%d" % dr], writes=[PS[1]])
                    S.add("dve", lambda e: e.tensor_tensor(out=v3(yt), in0=v3(ps[1][:]), in1=bc_p(est[:, c, eo:eo + 8]), op=ALU.mult),
                          reads=[PS[1], "est"], writes=["ytmp%d" % k])
                    S.add("dve", lambda e: e.tensor_tensor(out=yacc[:, c, :], in0=yacc[:, c, :], in1=yt, op=ALU.add),
                          reads=["ytmp%d" % k, "yacc%d" % c], writes=["yacc%d" % c])
                if ci < NT - 1:
                    S.add("pe", lambda e: e.matmul(ps[0][:], lhsT=B_tok[:, c, :], rhs=xd, start=True, stop=True),
                          reads=["B_tok", "xdec%d" % k], writes=[PS[0]])
                    if ci == 0:
                        S.add("dve", lambda e: e.tensor_copy(out=carry[dr][:], in_=ps[0][:]), reads=[PS[0]], writes=["carry%d" % dr])
                    else:
                        S.add("dve", lambda e: e.tensor_tensor(out=v3(carry[dr][:]), in0=v3(carry[dr][:]), in1=bc_p(est[:, c, cdo:cdo + 8]), op=ALU.mult),
                              reads=["carry%d" % dr, "est"], writes=["carry%d" % dr])
                        S.add("dve", lambda e: e.tensor_tensor(out=carry[dr][:], in0=carry[dr][:], in1=ps[0][:], op=ALU.add),
                              reads=["carry%d" % dr, PS[0]], writes=["carry%d" % dr])
                    S.add("pool", lambda e: e.tensor_copy(out=prevb[dr][:], in_=carry[dr][:]), reads=["carry%d" % dr], writes=["prevb%d" % dr])

            stage_r(0, *steps[0])
            for si, (ci, dr) in enumerate(steps):
                if si + 1 < len(steps):
                    stage_r(si + 1, *steps[si + 1])
                stage_a(si, ci, dr)
                if si >= 1:
                    stage_b(si - 1, *steps[si - 1])
            stage_b(len(steps) - 1, *steps[-1])
            if "scan" in dbg and g == 0:
                for nm, t, shp, dt_ in (("yacc", yacc, [128, NT, 512], F32), ("xdt0", xdt[0], [128, NT, 512], BF16),
                                       ("xdt1", xdt[1], [128, NT, 512], BF16), ("B_tok", B_tok, [128, NT, 128], BF16),
                                       ("est", est, [128, NT, 48], F32), ("sz_tok", sz_tok, [128, NT, 512], BF16)):
                    dbg_out[nm] = dram_out("dbg_" + nm, shp, dt_)
                    rds = ["yacc%d" % c_ for c_ in range(NT)] if nm == "yacc" else [nm]
                    dma("sp", dbg_out[nm], t[:], rds, ["dbg_" + nm])
            for c in range(NT):
                vk = rr["v"] % 2
                rr["v"] += 1
                S.add("dve", lambda e, c=c: e.tensor_tensor(out=vtmp[:], in0=yacc[:, c, :], in1=sz_tok[:, c, :], op=ALU.mult),
                      reads=["yacc%d" % c, "sz_tok"], writes=["vtmp"])
                S.add("act", lambda e: e.activation(out=vjunk[:], in_=vtmp[:], func=AF.Square, accum_out=s1[:, 0:1]),
                      reads=["vtmp"], writes=["vjunk", "s1"])
                S.add("dve", lambda e, c=c: e.tensor_tensor(out=ssq[:, c:c + 1], in0=ssq[:, c:c + 1], in1=s1[:, 0:1], op=ALU.add),
                      reads=["s1", "ssq_ssd"], writes=["ssq_ssd"])
                S.add("act", lambda e, vk=vk: e.copy(out=vb[vk][:], in_=vtmp[:]), reads=["vtmp"], writes=["vb0"])
                pb7 = ps[7][:].bitcast(BF16)

                def trv(e, vk=vk, pb7=pb7):
                    ins = None
                    for j in range(4):
                        ins = e.transpose(out=pb7[:, j * 128:(j + 1) * 128], in_=vb[vk][:, j * 128:(j + 1) * 128], identity=ident_b[:])
                    return ins
                S.add("pe", trv, reads=["vb0", "ident_b"], writes=[PS[7]])
                for j in range(4):
                    S.add("act", lambda e, j=j, c=c, pb7=pb7, g=g: e.activation(out=usT[j][:, c * 128:(c + 1) * 128], in_=pb7[:, j * 128:(j + 1) * 128],
                                                                            func=AF.Identity, scale=small["ssd_norm_gT"][:, 4 * g + j:4 * g + j + 1]),
                          reads=[PS[7], "ssd_norm_gT"], writes=["xsT%d" % j])
            for j in range(4):
                dma("sp", us_scr[4 * g + j], usT[j][:], ["xsT%d" % j], ["us_scr"])
        S.barrier()

    if "ssd" in dbg:
        dbg_out["us"] = dram_out("dbg_us", [32, 128, S_LEN], BF16)
        dma("sp", dbg_out["us"], us_scr, ["us_scr"], ["dbg_us"])
        dbg_out["dt_tok"] = dram_out("dbg_dt_tok", [128, NT, 128])
        dma("sp", dbg_out["dt_tok"], dt_tok[:], ["dt_tok"], ["dbg_dt_tok"])
        dbg_out["a_tok"] = dram_out("dbg_a_tok", [128, NT, 128])
        dma("sp", dbg_out["a_tok"], a_tok[:], ["a_tok"], ["dbg_a_tok"])
        dbg_out["xbc"] = dram_out("dbg_xbc", [48, 128, S_LEN], BF16)
        dma("sp", dbg_out["xbc"], xbc_scr, ["xbc_scr"], ["dbg_xbc"])
        dbg_out["ssq"] = dram_out("dbg_ssq", [128, NT])
        dma("sp", dbg_out["ssq"], ssq[:], ["ssq_ssd"], ["dbg_ssq"])
    if stop_after == "ssd":
        S.emit()
        return nc, dbg_out

    with ExitStack() as st5:
        gate_row = sb("gate_row", [128, D], F32, st5)
        fing_row = sb("fing_row", [128, D], F32, st5)
        dma("sp", fing_row[:], fing_d.partition_broadcast(128), [], ["fing_row"])
        dg = sb("dg", [128, 128], F32, st5)
        for kc in range(KC):
            S.add("dve", lambda e, kc=kc: e.tensor_scalar(out=dg[:], in0=ident_f[:], scalar1=adaT[:, 32 + kc:33 + kc],
                                                         scalar2=None, op0=ALU.mult),
                  reads=["ident_f", "adaT"], writes=["dg"])
            S.add("pe", lambda e, kc=kc: e.matmul(ps[6][:, 0:128], lhsT=ones_f[:], rhs=dg[:], start=True, stop=True),
                  reads=["ones_f", "dg"], writes=[PS[6]])
            S.add("act", lambda e, kc=kc: e.copy(out=gate_row[:, kc * 128:(kc + 1) * 128], in_=ps[6][:, 0:128]),
                  reads=[PS[6]], writes=["gate_row"])
        rstdS = sb("rstdS", [128, NT], F32, st5)
        rstdB = sb("rstdB", [128, S_LEN], F32, st5)
        dg2 = sb("dg2", [128, 128], F32, st5)
        S.add("dve", lambda e: e.tensor_scalar(out=rstdS[:], in0=ssq[:], scalar1=1.0 / 4096, scalar2=EPS, op0=ALU.mult, op1=ALU.add),
              reads=["ssq_ssd"], writes=["rstdS"])
        S.add("act", lambda e: e.sqrt(out=rstdS[:], in_=rstdS[:]), reads=["rstdS"], writes=["rstdS"])
        S.add("dve", lambda e: e.reciprocal(out=rstdS[:], in_=rstdS[:]), reads=["rstdS"], writes=["rstdS"])
        for i in range(NT):
            S.add("dve", lambda e, i=i: e.tensor_scalar(out=dg2[:], in0=ident_f[:], scalar1=rstdS[:, i:i + 1], scalar2=None, op0=ALU.mult),
                  reads=["ident_f", "rstdS"], writes=["dg2"])
            S.add("pe", lambda e: e.matmul(ps[6][:, 0:128], lhsT=ones_f[:], rhs=dg2[:], start=True, stop=True),
                  reads=["ones_f", "dg2"], writes=[PS[6]])
            S.add("act", lambda e, i=i: e.copy(out=rstdB[:, i * 128:(i + 1) * 128], in_=ps[6][:, 0:128]), reads=[PS[6]], writes=["rstdB"])
        ua_tb = sb("ua_tb", [128, 12, 512], BF16, st5)
        us_tb = sb("us_tb", [128, 32, 512], BF16, st5)
        sga = [sb("sga%d" % i, [128, 512], BF16, st5) for i in range(2)]
        sgs = [sb("sgs%d" % i, [128, 512], BF16, st5) for i in range(2)]
        mT = sb("mT", [128, 16, 512], BF16, st5)
        m1 = sb("m1", [128, 512], F32, st5)
        m2 = sb("m2", [128, 512], F32, st5)
        xo = [sb("xo%d" % i, [128, D], F32, st5) for i in range(4)]
        fj = sb("fj", [128, D], F32, st5)
        fs = sb("fs", [128, 4], F32, st5)
        ft = sb("ft", [128, 512], F32, st5)
        g_rr = [0]

        def load_ws(si_, nk_):
            k_ = wb_rr[0] % NWB
            wb_rr[0] += 1
            t_ = wbuf[k_]
            rn_ = "wb%d" % k_
            dma("act", t_[:, 0:nk_, :], wscr[si_, :, 0:nk_, :], ["wscr"], [rn_])
            return t_, rn_

        for tb in range(4):
            tsl = slice(tb * 512, (tb + 1) * 512)
            dma("sp", ua_tb[:], ua_scr[:, :, tsl].rearrange("c p t -> p c t"), ["ua_scr"], ["ua_tb"])
            dma("sp", us_tb[:], us_scr[:, :, tsl].rearrange("c p t -> p c t"), ["us_scr"], ["us_tb"])
            for tt in range(4):
                dma("sp", xo[tt][:], x_d[tb * 512 + tt * 128:tb * 512 + (tt + 1) * 128, :], [], ["xo%d" % tt])
            for jb in range(4):
                wa_, wan = load_ws(3 * jb, 12)
                ws0, ws0n = load_ws(3 * jb + 1, 16)
                ws1, ws1n = load_ws(3 * jb + 2, 16)
                for j4 in range(4):
                    j = jb * 4 + j4
                    csl = slice(j4 * 128, (j4 + 1) * 128)
                    gk = g_rr[0] % 2
                    g_rr[0] += 1
                    dma("sp", sga[gk][:], sg_scr[j, :, tsl], ["sg_scr"], ["sga%d" % gk])
                    dma("sp", sgs[gk][:], sg_scr[16 + j, :, tsl], ["sg_scr"], ["sgs%d" % gk])

                    def ya(e, csl=csl, wa_=wa_):
                        ins = None
                        for cc in range(12):
                            ins = e.matmul(ps[0][:], lhsT=wa_[:, cc, csl], rhs=ua_tb[:, cc, :], start=(cc == 0), stop=(cc == 11))
                        return ins
                    S.add("pe", ya, reads=[wan, "ua_tb"], writes=[PS[0]])

                    def ys(e, csl=csl, ws0=ws0, ws1=ws1):
                        ins = None
                        for cc in range(32):
                            w_ = ws0 if cc < 16 else ws1
                            ins = e.matmul(ps[1][:], lhsT=w_[:, cc % 16, csl], rhs=us_tb[:, cc, :], start=(cc == 0), stop=(cc == 31))
                        return ins
                    S.add("pe", ys, reads=[ws0n, ws1n, "us_tb"], writes=[PS[1]])
                    S.add("dve", lambda e, gk=gk: e.tensor_tensor(out=m1[:], in0=ps[0][:], in1=sga[gk][:], op=ALU.mult),
                          reads=[PS[0], "sga%d" % gk], writes=["m1"])
                    S.add("dve", lambda e, tsl=tsl: e.tensor_tensor(out=m2[:], in0=ps[1][:], in1=rstdB[:, tsl], op=ALU.mult),
                          reads=[PS[1], "rstdB"], writes=["m2"])
                    S.add("dve", lambda e, gk=gk: e.tensor_tensor(out=m2[:], in0=m2[:], in1=sgs[gk][:], op=ALU.mult),
                          reads=["m2", "sgs%d" % gk], writes=["m2"])
                    S.add("dve", lambda e, j=j: e.tensor_tensor(out=mT[:, j, :], in0=m1[:], in1=m2[:], op=ALU.add),
                          reads=["m1", "m2"], writes=["mT"])
            for ob in range(4):
                wo, won = load_ws(12 + ob, 16)
                osl = slice(ob * 512, (ob + 1) * 512)
                for tt in range(4):
                    bank = 2 + (tt % 2)

                    def om(e, tt=tt, bank=bank, wo=wo):
                        ins = None
                        for j in range(16):
                            ins = e.matmul(ps[bank][:], lhsT=mT[:, j, tt * 128:(tt + 1) * 128], rhs=wo[:, j, :], start=(j == 0), stop=(j == 15))
                        return ins
                    S.add("pe", om, reads=[won, "mT"], writes=[PS[bank]])
                    S.add("dve", lambda e, bank=bank, osl=osl: e.tensor_tensor(out=ft[:], in0=ps[bank][:], in1=gate_row[:, osl], op=ALU.mult),
                          reads=[PS[bank], "gate_row"], writes=["ft"])
                    S.add("dve", lambda e, tt=tt, osl=osl: e.tensor_tensor(out=xo[tt][:, osl], in0=xo[tt][:, osl], in1=ft[:], op=ALU.add),
                          reads=["ft", "xo%d" % tt], writes=["xo%d" % tt])
            for tt in range(4):
                S.add("act", lambda e, tt=tt: e.activation(out=fj[:], in_=xo[tt][:], func=AF.Square, accum_out=fs[:, tt:tt + 1]),
                      reads=["xo%d" % tt], writes=["fj", "fs%d" % tt])
                S.add("dve", lambda e, tt=tt: e.tensor_scalar(out=fs[:, tt:tt + 1], in0=fs[:, tt:tt + 1], scalar1=1.0 / D, scalar2=EPS,
                                                             op0=ALU.mult, op1=ALU.add), reads=["fs%d" % tt], writes=["fs%d" % tt])
                S.add("act", lambda e, tt=tt: e.sqrt(out=fs[:, tt:tt + 1], in_=fs[:, tt:tt + 1]), reads=["fs%d" % tt], writes=["fs%d" % tt])
                S.add("dve", lambda e, tt=tt: e.reciprocal(out=fs[:, tt:tt + 1], in_=fs[:, tt:tt + 1]), reads=["fs%d" % tt], writes=["fs%d" % tt])
                S.add("dve", lambda e, tt=tt: e.scalar_tensor_tensor(out=xo[tt][:], in0=xo[tt][:], scalar=fs[:, tt:tt + 1], in1=fing_row[:],
                                                                    op0=ALU.mult, op1=ALU.mult),
                      reads=["xo%d" % tt, "fs%d" % tt, "fing_row"], writes=["xo%d" % tt])
                dma("sp", out_d[tb * 512 + tt * 128:tb * 512 + (tt + 1) * 128, :], xo[tt][:], ["xo%d" % tt], ["out"])
    print("n ops:", len(S.ops))
    S.emit()
    return nc, dbg_out


def _prep_inputs(inputs):
    f32 = np.float32
    perm = np.concatenate([np.arange(t * 128, (t + 1) * 128) for t in COL_ORDER])
    w_in = np.ascontiguousarray(inputs["w_in"][0][:, perm])
    shared = {
        "w_ada": np.ascontiguousarray(inputs["w_ada"][0]),
        "b_adaT": np.ascontiguousarray(inputs["b_ada"][0].reshape(48, 128).T),
        "norm_gT": np.ascontiguousarray(inputs["norm_g"][0].reshape(KC, 128).T),
        "w_in": w_in,
        "conv_wT": np.ascontiguousarray(inputs["conv_w"][0].reshape(5, 48, 128).transpose(2, 1, 0)),
        "conv_bT": np.ascontiguousarray(inputs["conv_b"][0].reshape(48, 128).T),
        "dt_biasT": np.ascontiguousarray(inputs["dt_bias"][0].reshape(128, 1)),
        "a_logT": np.ascontiguousarray(inputs["a_log"][0].reshape(128, 1)),
        "d_skip": np.ascontiguousarray(inputs["d_skip"][0].reshape(1, 64)),
        "ssd_norm_gT": np.ascontiguousarray(inputs["ssd_norm_g"][0].reshape(32, 128).T),
        "w_br_attn": np.ascontiguousarray(inputs["w_br_attn"][0]),
        "w_br_ssd": np.ascontiguousarray(inputs["w_br_ssd"][0]),
        "w_out": np.ascontiguousarray(inputs["w_out"][0]),
        "final_g": np.ascontiguousarray(inputs["final_g"].reshape(1, D)),
        "inv_freq": np.tile((10000.0 ** (-np.arange(0, 128, 2, dtype=f32) / f32(128))).astype(f32), 2).reshape(128, 1),
    }
    per_core = []
    for b in range(8):
        m = dict(shared)
        m["x"] = np.ascontiguousarray(inputs["x"][b])
        m["cT"] = np.ascontiguousarray(inputs["c"][b].reshape(KC, 128).T)
        m["pos"] = np.ascontiguousarray(inputs["positions"][b].reshape(1, S_LEN).astype(np.int32))
        per_core.append(m)
    return per_core


def kernel(**inputs):
    nc, _ = build()
    in_maps = _prep_inputs(inputs)
    res = run_bass_kernel_spmd(nc, in_maps, core_ids=list(range(8)))
    return np.stack([np.asarray(r["out"], dtype=np.float32) for r in res.results], axis=0)
```
